# Optimizing a Trainium2 kernel written in Bass

```python
import math
import jax
import jax.numpy as jnp
from jax import lax
import numpy as np

D_MODEL = 1024
BATCH = 8
SEQ = 2048
DEPTH = 2
DEC_BATCH = 32
DEC_SEQ = 16
PAST_LEN = 2048

CHUNK = 64
N_AB_LAYERS = (DEPTH + 1) // 2
N_POOL_LAYERS = DEPTH // 2
EPS = 1e-6
NEG_INF = -1e30

A_HEADS = 8
A_KV_HEADS = 2
A_HEAD_DIM = 64
A_WIDTH = A_HEADS * A_HEAD_DIM
A_KV_WIDTH = A_KV_HEADS * A_HEAD_DIM
WINDOW = 128
WINDOW_CHUNKS = WINDOW // CHUNK

B_HEADS = 4
B_HEAD_DIM = 128
B_WIDTH = B_HEADS * B_HEAD_DIM
CONV_W = 4

POOL_SIZES = (2, 4, 8, 16)
POOL_GROUPS = len(POOL_SIZES)
C_WIDTH = D_MODEL
C_GROUP = C_WIDTH // POOL_GROUPS
POOL_HIST = max(POOL_SIZES) - 1

AB_WIDTHS = (A_WIDTH, A_KV_WIDTH, A_KV_WIDTH, A_WIDTH, 3 * B_WIDTH, B_WIDTH, B_HEADS, B_HEADS)
AB_IN = sum(AB_WIDTHS)

kernel_name = 'hybrid_swa_deltanet_pool_stream_step'


def rms_norm(x, g):
    xf = x.astype(jnp.float32)
    y = xf * lax.rsqrt(jnp.mean(xf * xf, axis=-1, keepdims=True) + EPS)
    return (y * g.astype(jnp.float32)).astype(x.dtype)


def l2_norm(x):
    return x * lax.rsqrt(jnp.sum(x * x, axis=-1, keepdims=True) + EPS)


def split_cols(z, widths):
    out, start = [], 0
    for w in widths:
        out.append(z[..., start:start + w])
        start += w
    return out


def sink_attention(q, k, v, mask, dist, sinks):
    Bn, N, Lq, H, D = q.shape
    G = k.shape[3]
    R = H // G
    qg = q.reshape(Bn, N, Lq, G, R, D)
    s = jnp.einsum('bnqgrd,bnkgd->bngrqk', qg, k, preferred_element_type=jnp.float32) * (D ** -0.5)
    slopes = 2.0 ** (-8.0 * jnp.arange(1, H + 1, dtype=jnp.float32) / H)
    s = s - slopes.reshape(G, R)[:, :, None, None] * dist[:, None, None]
    s = jnp.where(mask[:, None, None], s, NEG_INF)
    sk = sinks.astype(jnp.float32).reshape(G, R)[:, :, None, None]
    m = jnp.maximum(jnp.max(s, axis=-1, keepdims=True), sk)
    p = jnp.exp(s - m)
    w = p / (jnp.sum(p, axis=-1, keepdims=True) + jnp.exp(sk - m))
    o = jnp.einsum('bngrqk,bnkgd->bnqgrd', w.astype(v.dtype), v)
    return o.reshape(Bn, N, Lq, H * D)


def band_blocks(a):
    Bn, T = a.shape[:2]
    nc = T // CHUNK
    ap = jnp.pad(a, ((0, 0), (WINDOW, 0), (0, 0), (0, 0)))
    ap = ap.reshape((Bn, nc + WINDOW_CHUNKS, CHUNK) + a.shape[2:])
    return jnp.concatenate([ap[:, j:j + nc] for j in range(WINDOW_CHUNKS + 1)], axis=2)


def swa_prompt(q, k, v, sinks):
    Bn, T = q.shape[:2]
    nc = T // CHUNK
    lk = (WINDOW_CHUNKS + 1) * CHUNK
    qb = q.reshape(Bn, nc, CHUNK, A_HEADS, A_HEAD_DIM)
    qpos = jnp.arange(T).reshape(nc, CHUNK)
    kpos = (jnp.arange(nc)[:, None] - WINDOW_CHUNKS) * CHUNK + jnp.arange(lk)[None, :]
    mask = jnp.broadcast_to((kpos >= 0)[:, None, :], (nc, CHUNK, lk))
    dist = jnp.abs(qpos[:, :, None] - kpos[:, None, :]).astype(jnp.float32)
    return sink_attention(qb, band_blocks(k), band_blocks(v), mask, dist, sinks)


def swa_sample(q, k, v, cache_k, cache_v, sinks, pos0):
    T = q.shape[1]
    nbuf = cache_k.shape[1]
    kk = jnp.concatenate([cache_k.astype(k.dtype), k], axis=1)[:, None]
    vv = jnp.concatenate([cache_v.astype(v.dtype), v], axis=1)[:, None]
    qpos = pos0 + jnp.arange(T)
    kpos = pos0 - nbuf + jnp.arange(nbuf + T)
    mask = jnp.ones((1, T, nbuf + T), dtype=bool)
    dist = jnp.abs(qpos[:, None] - kpos[None, :]).astype(jnp.float32)[None]
    return sink_attention(q[:, None], kk, vv, mask, dist, sinks)


def causal_conv(u, hist, w):
    T = u.shape[1]
    up = jnp.concatenate([hist.astype(u.dtype), u], axis=1)
    y = up[:, 0:T] * w[0]
    for j in range(1, CONV_W):
        y = y + up[:, j:j + T] * w[j]
    return jax.nn.silu(y), up[:, -(CONV_W - 1):]


def gated_delta_rule(q, k, v, beta, g, s0, chunk):
    Bn, T, H, Dk = q.shape
    Dv = v.shape[-1]
    n = T // chunk

    def blk(a):
        a = a.reshape((Bn, n, chunk, H) + a.shape[3:])
        return jnp.moveaxis(a, 3, 1)

    q, k, v, beta, g = blk(q) * (Dk ** -0.5), blk(k), blk(v), blk(beta), blk(g)
    G = jnp.cumsum(g, axis=-1)
    lower = jnp.tril(jnp.ones((chunk, chunk), dtype=bool))
    strict = jnp.tril(jnp.ones((chunk, chunk), dtype=bool), k=-1)
    diff = G[..., :, None] - G[..., None, :]
    decay = jnp.where(lower, jnp.exp(jnp.where(lower, diff, 0.0)), 0.0)
    kb = k * beta[..., None]
    a_mat = jnp.where(strict, jnp.einsum('bhnid,bhnjd->bhnij', kb, k) * decay, 0.0) + jnp.eye(chunk, dtype=q.dtype)
    rhs = jnp.concatenate([v * beta[..., None], kb * jnp.exp(G)[..., None]], axis=-1)
    sol = lax.linalg.triangular_solve(a_mat, rhs, left_side=True, lower=True, unit_diagonal=True)
    u, w = sol[..., :Dv], sol[..., Dv:]
    qk = jnp.einsum('bhnid,bhnjd->bhnij', q, k) * decay
    q_dec = q * jnp.exp(G)[..., None]
    k_dec = k * jnp.exp(G[..., -1:] - G)[..., None]
    g_tot = jnp.exp(G[..., -1])

    def step(s, xs):
        qk_n, qd_n, kd_n, u_n, w_n, gt_n = xs
        v_new = u_n - jnp.einsum('bhck,bhkv->bhcv', w_n, s)
        o = jnp.einsum('bhck,bhkv->bhcv', qd_n, s) + jnp.einsum('bhij,bhjv->bhiv', qk_n, v_new)
        s = s * gt_n[..., None, None] + jnp.einsum('bhck,bhcv->bhkv', kd_n, v_new)
        return s, o

    xs = tuple(jnp.moveaxis(a, 2, 0) for a in (qk, q_dec, k_dec, u, w, g_tot))
    s_fin, o = lax.scan(step, s0, xs)
    o = jnp.moveaxis(jnp.moveaxis(o, 0, 2), 1, 3).reshape(Bn, T, H, Dv)
    return o, s_fin


def ab_layer(h, norm_g, w_in, q_norm, k_norm, sinks, conv_w, a_log, dt_bias, o_norm, w_out,
             cache_k, cache_v, s0, conv_hist, pos0):
    Bn, T, _ = h.shape
    f32 = jnp.float32
    z = rms_norm(h, norm_g) @ w_in
    a_q, a_k, a_v, a_g, b_qkv, b_g, b_beta, b_alpha = split_cols(z, AB_WIDTHS)
    q = rms_norm(a_q.reshape(Bn, T, A_HEADS, A_HEAD_DIM), q_norm)
    k = rms_norm(a_k.reshape(Bn, T, A_KV_HEADS, A_HEAD_DIM), k_norm)
    v = a_v.reshape(Bn, T, A_KV_HEADS, A_HEAD_DIM)
    if cache_k is None:
        o_a = swa_prompt(q, k, v, sinks)
        new_k, new_v = k[:, -WINDOW:], v[:, -WINDOW:]
        chunk = CHUNK
    else:
        o_a = swa_sample(q, k, v, cache_k, cache_v, sinks, pos0)
        new_k, new_v = k, v
        chunk = T
    o_a = o_a.reshape(Bn, T, A_WIDTH) * jax.nn.silu(a_g)
    c, new_hist = causal_conv(b_qkv, conv_hist, conv_w)
    bq, bk, bv = jnp.split(c, 3, axis=-1)
    bq = l2_norm(bq.reshape(Bn, T, B_HEADS, B_HEAD_DIM).astype(f32))
    bk = l2_norm(bk.reshape(Bn, T, B_HEADS, B_HEAD_DIM).astype(f32))
    bv = bv.reshape(Bn, T, B_HEADS, B_HEAD_DIM).astype(f32)
    beta = jax.nn.sigmoid(b_beta.astype(f32))
    g = -jnp.exp(a_log.astype(f32)) * jax.nn.softplus(b_alpha.astype(f32) + dt_bias.astype(f32))
    o_b, s_new = gated_delta_rule(bq, bk, bv, beta, g, s0.astype(f32), chunk)
    o_b = rms_norm(o_b, o_norm).astype(h.dtype).reshape(Bn, T, B_WIDTH) * jax.nn.silu(b_g)
    y = h + jnp.concatenate([o_a, o_b], axis=-1) @ w_out
    return y, (new_k, new_v, s_new.astype(h.dtype), new_hist)


def pool_layer(h, norm_g, w_in, w_grp, scale, w_out, hist, pos0):
    Bn, T, _ = h.shape
    f32 = jnp.float32
    z = rms_norm(h, norm_g) @ w_in
    u, gate = z[..., :C_WIDTH], z[..., C_WIDTH:]
    ue = jnp.concatenate([hist.astype(u.dtype), u], axis=1)
    pos = pos0 - POOL_HIST + jnp.arange(POOL_HIST + T)
    uf = ue.astype(f32) * (pos >= 0)[None, :, None]
    cs = jnp.concatenate([jnp.zeros((Bn, 1, C_WIDTH), f32), jnp.cumsum(uf, axis=1)], axis=1)
    hi = cs[:, POOL_HIST + 1:]
    means = []
    for gi, w in enumerate(POOL_SIZES):
        cols = slice(gi * C_GROUP, (gi + 1) * C_GROUP)
        lo = cs[:, POOL_HIST + 1 - w:POOL_HIST + 1 - w + T, cols]
        cnt = jnp.minimum(pos[POOL_HIST:] + 1, w).astype(f32)
        means.append((hi[..., cols] - lo) / cnt[None, :, None])
    pooled = jnp.concatenate(means, axis=-1) - u.astype(f32)
    mixed = jnp.einsum('btgc,gcd->btgd', pooled.reshape(Bn, T, POOL_GROUPS, C_GROUP), w_grp.astype(f32))
    mixed = (mixed.reshape(Bn, T, C_WIDTH) * scale.astype(f32)).astype(h.dtype)
    y = h + (mixed * jax.nn.silu(gate)) @ w_out
    return y, ue[:, -POOL_HIST:]


def setup_inputs(seed: int = 0) -> dict:
    key = jax.random.key(seed)
    ks = jax.random.split(key, 24)
    f32 = jnp.float32
    na, npl = N_AB_LAYERS, N_POOL_LAYERS

    def nrm(k, shape, s):
        return jax.random.normal(k, shape, f32) * s

    dt = jnp.exp(jax.random.uniform(ks[12], (na, B_HEADS), f32, math.log(1e-3), math.log(1e-1)))
    dt_bias = dt + jnp.log(-jnp.expm1(-dt))
    return {
        'x_prompt': nrm(ks[0], (BATCH, SEQ, D_MODEL), 1.0),
        'x_sample': nrm(ks[1], (DEC_BATCH, DEC_SEQ, D_MODEL), 1.0),
        'cache_a_k': nrm(ks[2], (na, DEC_BATCH, WINDOW, A_KV_HEADS, A_HEAD_DIM), 1.0),
        'cache_a_v': nrm(ks[3], (na, DEC_BATCH, WINDOW, A_KV_HEADS, A_HEAD_DIM), 1.0),
        'state_b_s': nrm(ks[4], (na, DEC_BATCH, B_HEADS, B_HEAD_DIM, B_HEAD_DIM), 0.1),
        'state_b_conv': nrm(ks[5], (na, DEC_BATCH, CONV_W - 1, 3 * B_WIDTH), 1.0),
        'state_c_pool': nrm(ks[6], (npl, DEC_BATCH, POOL_HIST, C_WIDTH), 1.0),
        'norm_ab': 1.0 + nrm(ks[7], (na, D_MODEL), 0.02),
        'w_in_ab': nrm(ks[8], (na, D_MODEL, AB_IN), D_MODEL ** -0.5),
        'q_norm_a': 1.0 + nrm(ks[9], (na, A_HEAD_DIM), 0.02),
        'k_norm_a': 1.0 + nrm(ks[10], (na, A_HEAD_DIM), 0.02),
        'sinks_a': nrm(ks[11], (na, A_HEADS), 0.5),
        'conv_b': nrm(ks[13], (na, CONV_W, 3 * B_WIDTH), CONV_W ** -0.5),
        'a_log_b': jnp.log(jax.random.uniform(ks[14], (na, B_HEADS), f32, 1.0, 16.0)),
        'dt_bias_b': dt_bias,
        'o_norm_b': 1.0 + nrm(ks[15], (na, B_HEAD_DIM), 0.02),
        'w_out_ab': nrm(ks[16], (na, A_WIDTH + B_WIDTH, D_MODEL), (A_WIDTH + B_WIDTH) ** -0.5),
        'norm_c': 1.0 + nrm(ks[17], (npl, D_MODEL), 0.02),
        'w_in_c': nrm(ks[18], (npl, D_MODEL, 2 * C_WIDTH), D_MODEL ** -0.5),
        'w_grp_c': nrm(ks[19], (npl, POOL_GROUPS, C_GROUP, C_GROUP), C_GROUP ** -0.5),
        'scale_c': 1.0 + nrm(ks[20], (npl, C_WIDTH), 0.1),
        'w_out_c': nrm(ks[21], (npl, C_WIDTH, D_MODEL), C_WIDTH ** -0.5),
    }


def reference(x_prompt, x_sample, cache_a_k, cache_a_v, state_b_s, state_b_conv, state_c_pool,
              norm_ab, w_in_ab, q_norm_a, k_norm_a, sinks_a, conv_b, a_log_b, dt_bias_b, o_norm_b, w_out_ab,
              norm_c, w_in_c, w_grp_c, scale_c, w_out_c):
    hp, hs = x_prompt, x_sample
    bp = x_prompt.shape[0]
    pa_k, pa_v, pb_s, pb_c, pc = [], [], [], [], []
    sa_k, sa_v, sb_s, sb_c, sc = [], [], [], [], []
    for layer in range(DEPTH):
        i = layer // 2
        if layer % 2 == 0:
            wts = (norm_ab[i], w_in_ab[i], q_norm_a[i], k_norm_a[i], sinks_a[i], conv_b[i],
                   a_log_b[i], dt_bias_b[i], o_norm_b[i], w_out_ab[i])
            s0 = jnp.zeros((bp, B_HEADS, B_HEAD_DIM, B_HEAD_DIM), jnp.float32)
            c0 = jnp.zeros((bp, CONV_W - 1, 3 * B_WIDTH), hp.dtype)
            hp, (k_, v_, s_, c_) = ab_layer(hp, *wts, None, None, s0, c0, 0)
            pa_k.append(k_)
            pa_v.append(v_)
            pb_s.append(s_)
            pb_c.append(c_)
            hs, (k_, v_, s_, c_) = ab_layer(hs, *wts, cache_a_k[i], cache_a_v[i], state_b_s[i],
                                            state_b_conv[i], PAST_LEN)
            sa_k.append(k_)
            sa_v.append(v_)
            sb_s.append(s_)
            sb_c.append(c_)
        else:
            wts = (norm_c[i], w_in_c[i], w_grp_c[i], scale_c[i], w_out_c[i])
            h0 = jnp.zeros((bp, POOL_HIST, C_WIDTH), hp.dtype)
            hp, st = pool_layer(hp, *wts, h0, 0)
            pc.append(st)
            hs, st = pool_layer(hs, *wts, state_c_pool[i], PAST_LEN)
            sc.append(st)
    return (hp, hs,
            jnp.stack(pa_k), jnp.stack(pa_v), jnp.stack(pb_s), jnp.stack(pb_c), jnp.stack(pc),
            jnp.stack(sa_k), jnp.stack(sa_v), jnp.stack(sb_s), jnp.stack(sb_c), jnp.stack(sc))
```

```python
import numpy as np
from contextlib import ExitStack
import concourse.bass as bass
import concourse.mybir as mybir
from concourse.bass_utils import run_bass_kernel_spmd

F32 = mybir.dt.float32
BF16 = mybir.dt.bfloat16
AF = mybir.ActivationFunctionType
ALU = mybir.AluOpType
AX = mybir.AxisListType

D = 1024
NEG = -30000.0
EPS = 1e-6
CQ, CKV, CAG, CBG, CBQ = 0, 512, 776, 1288, 1800
NAB = 3336


class Sched:
    ENGS = ("pe", "act", "dve", "pool", "sp")

    def __init__(self, nc, es, n_dma_sems=24):
        self.nc = nc
        self.ops = {e: [] for e in self.ENGS}
        self.sem = {e: es.enter_context(nc.semaphore("s_" + e)) for e in ("pe", "act", "dve", "pool")}
        self.cnt = {e: 0 for e in ("pe", "act", "dve", "pool")}
        self.dsem = [es.enter_context(nc.semaphore("s_dma%d" % i)) for i in range(n_dma_sems)]
        self.dval = [0] * n_dma_sems
        self.dpool = {"sp": list(range(0, 12)), "act": list(range(12, 16)), "pool": list(range(16, n_dma_sems))}
        self.dnext = {"sp": 0, "act": 0, "pool": 0}
        self.seen = {e: {} for e in self.ENGS}
        self.lastw = {}
        self.readers = {}
        self.out_tokens = []
        self.expand = {}

    def _x(self, keys):
        out = []
        for k in keys:
            out.extend(self.expand.get(k, (k,)))
        return out

    def _deps(self, eng, reads, writes):
        toks = []
        for k in reads:
            w = self.lastw.get(k)
            if w is not None:
                toks.append(w)
        for k in writes:
            w = self.lastw.get(k)
            if w is not None:
                toks.append(w)
            toks.extend(self.readers.get(k, ()))
        best = {}
        for (s, v) in toks:
            if best.get(id(s), (None, -1))[1] < v:
                best[id(s)] = (s, v)
        waits = []
        seen = self.seen[eng]
        for sid, (s, v) in best.items():
            if seen.get(sid, -1) >= v:
                continue
            seen[sid] = v
            waits.append((s, v))
        return waits

    def _commit(self, tok, reads, writes):
        for k in reads:
            self.readers.setdefault(k, []).append(tok)
        for k in writes:
            self.lastw[k] = tok
            self.readers[k] = []

    @staticmethod
    def _split(reads, writes):
        r, w = [], list(writes)
        for k in reads:
            if k.startswith("ps") and k not in w:
                w.append(k)
            elif not k.startswith("ps"):
                r.append(k)
        return r, w

    def record(self):
        self.rec = []
        return self.rec

    def stop(self):
        self.rec = None

    COST = {"pe": 0.16, "act": 0.7, "dve": 0.6, "pool": 1.0, "sp": 3.0}
    HOP = 0.3
    SLACK = 0.5

    def merge(self, *streams):
        self.rec = None
        pos = [0] * len(streams)
        eng_free = {}
        wtime, rtime = {}, {}

        def keys_of(item):
            kind, args, kw = item
            if kind == "op":
                eng, fn, reads, writes = args
                is_dma = False
            else:
                eng = args[0]
                reads, writes = kw["reads"], kw["writes"]
                is_dma = True
            r, w = self._split(self._x(reads), self._x(writes))
            return eng, r, w, is_dma

        def start_time(item):
            eng, r, w, is_dma = keys_of(item)
            t = 0.0
            for k in r:
                t = max(t, wtime.get(k, 0.0))
            for k in w:
                t = max(t, wtime.get(k, 0.0), rtime.get(k, 0.0))
            return max(eng_free.get(eng, 0.0), t + self.HOP)

        exposed = []
        for st in streams:
            cnt, written = {}, set()
            for it in st:
                _, r_, w_, _ = keys_of(it)
                for k in r_:
                    if k not in written:
                        cnt[k] = cnt.get(k, 0) + 1
                written.update(w_)
            exposed.append(cnt)
        wrote = [set() for _ in streams]

        while True:
            cand = []
            for i, st in enumerate(streams):
                if pos[i] < len(st):
                    cand.append((start_time(st[pos[i]]), i))
            if not cand:
                break
            tmin = min(c[0] for c in cand)
            best = None
            for t0_, i in cand:
                if t0_ <= tmin + self.SLACK:
                    best, bi = (t0_, i), i
                    break
            item = streams[bi][pos[bi]]
            eng, r, w, is_dma = keys_of(item)
            blocked = [k for k in w if not k.startswith("ps") and any(exposed[j].get(k, 0) > 0 for j in range(len(streams)) if j != bi)]
            if blocked:
                alt = [j for j in range(len(streams)) if j != bi and pos[j] < len(streams[j]) and any(exposed[j].get(k, 0) > 0 for k in blocked)]
                assert alt, ("merge hazard", blocked)
                bi = alt[0]
                item = streams[bi][pos[bi]]
                eng, r, w, is_dma = keys_of(item)
                best = (start_time(item), bi)
            pos[bi] += 1
            for k in r:
                if k not in wrote[bi] and exposed[bi].get(k, 0) > 0:
                    exposed[bi][k] -= 1
            wrote[bi].update(w)
            t0 = best[0]
            if is_dma:
                fin = t0 + 3.0
                eng_free[eng] = t0 + 0.1
            else:
                fin = t0 + self.COST.get(eng, 0.5)
                eng_free[eng] = fin
            for k in r:
                rtime[k] = max(rtime.get(k, 0.0), fin)
            for k in w:
                wtime[k] = fin
                rtime[k] = 0.0
            kind, args, kw = item
            if kind == "op":
                self.op(*args, **kw)
            else:
                self.dma(*args, **kw)

    def op(self, eng, fn, reads=(), writes=(), acc=False):
        if eng == "pool" and getattr(self, "nopool", False):
            eng = "dve"
        if getattr(self, "rec", None) is not None:
            self.rec.append(("op", (eng, fn, list(reads), list(writes)), {"acc": acc}))
            return None
        reads, writes = self._split(self._x(reads), self._x(writes))
        if acc:
            waits = self._deps(eng, reads, [k for k in writes if not k.startswith("ps")])
        else:
            waits = self._deps(eng, reads, writes)
        self.cnt[eng] += 1
        tok = (self.sem[eng], self.cnt[eng])
        self.ops[eng].append((fn, waits, tok, 1))
        self._commit(tok, reads, writes)
        return tok

    def dma(self, q, out, in_, reads=(), writes=(), is_output=False, **kw):
        if getattr(self, "rec", None) is not None:
            kw2 = dict(kw)
            kw2.update(reads=list(reads), writes=list(writes), is_output=is_output)
            self.rec.append(("dma", (q, out, in_), kw2))
            return None
        reads = self._x(reads)
        writes = self._x(writes)
        pl = self.dpool[q]
        i = pl[self.dnext[q] % len(pl)]
        self.dnext[q] += 1
        s = self.dsem[i]
        waits = self._deps(q, reads, writes)
        prev = self.dval[i]
        if prev > 0 and self.seen[q].get(id(s), -1) < prev:
            self.seen[q][id(s)] = prev
            waits.append((s, prev))
        self.dval[i] += 16
        tok = (s, self.dval[i])

        def fn(e, out=out, in_=in_, kw=kw):
            return e.dma_start(out=out, in_=in_, **kw)
        self.ops[q].append((fn, waits, tok, 16))
        self._commit(tok, reads, writes)
        if is_output:
            self.out_tokens.append(tok)
        return tok

    def emit(self, block):
        sched = self

        def run(engname):
            def body(e):
                for (fn, waits, tok, inc) in sched.ops[engname]:
                    for (s, v) in waits:
                        e.wait_ge(s, v)
                    ins = fn(e)
                    ins.then_inc(tok[0], inc)
                if engname == "sp":
                    best = {}
                    for (s, v) in sched.out_tokens:
                        if best.get(id(s), (None, -1))[1] < v:
                            best[id(s)] = (s, v)
                    for (s, v) in best.values():
                        e.wait_ge(s, v)
                    for en in ("pe", "act", "dve", "pool"):
                        if sched.cnt[en] > 0:
                            e.wait_ge(sched.sem[en], sched.cnt[en])
            return body
        block.tensor(run("pe"))
        block.scalar(run("act"))
        block.vector(run("dve"))
        block.gpsimd(run("pool"))
        block.sync(run("sp"))


def _layout(items):
    off = {}
    o = 0
    for name, w in items:
        off[name] = (o, w)
        o += w
    return off, o


CST_ITEMS = [
    ("ident", 128),
    ("mS_P", 128), ("tri_P", 128), ("blk_P", 128), ("cm_P", 2 * 128), ("padP", 64),
    ("seqm", 4), ("icnt0", 4), ("ccor0", 4),
]
CSTS_ITEMS = [("mS_S", 64), ("tri_S", 64), ("blk_S", 64), ("cm_S", 4 * 128)]
CST2_ITEMS = [
    ("colm", 4 * 64), ("dist_P", 2 * 128), ("am_P", 2 * 128), ("bc_S", 4 * 64),
    ("bc_P", 4 * 128), ("bp_P", 4 * 128),
    ("bh_S", 4 * 64), ("dist_S1", 64), ("am_S1", 64),
]
CST3_ITEMS = [("dist_SC", 4 * 64), ("am_SC", 4 * 64)] + CSTS_ITEMS
CST_OFF, NCST = _layout(CST_ITEMS)
CST2_OFF, NCST2 = _layout(CST2_ITEMS)
CST3_OFF, NCST3 = _layout(CST3_ITEMS)
CSTS_OFF, NCSTS = _layout(CSTS_ITEMS)
assert NCSTS == 704
POOLW = (2, 4, 8, 16)


def build_consts():
    c = np.zeros((128, NCST), np.float32)
    c2 = np.zeros((128, NCST2), np.float32)
    c3 = np.zeros((128, NCST3), np.float32)

    def put(name, arr):
        if name in CST_OFF:
            o, w = CST_OFF[name]
            dst = c
        elif name in CST3_OFF:
            o, w = CST3_OFF[name]
            dst = c3
        else:
            o, w = CST2_OFF[name]
            dst = c2
        a = np.asarray(arr, np.float32).reshape(arr.shape[0], -1)
        assert a.shape[1] == w, (name, a.shape, w)
        dst[:a.shape[0], o:o + w] = a
    p = np.arange(128)
    put("ident", np.eye(128))
    for tag, n, bs in (("P", 128, 64), ("S", 64, 16)):
        i = np.arange(n)
        same = (i[:, None] // bs) == (i[None, :] // bs)
        put("mS_" + tag, np.where(same & (i[None, :] < i[:, None]), 0.0, NEG))
        put("tri_" + tag, (same & (i[:, None] <= i[None, :])).astype(np.float32))
        put("blk_" + tag, same.astype(np.float32))
        nb = n // bs
        cm = np.zeros((n, nb, 128), np.float32)
        for b in range(nb):
            cm[b * bs:(b + 1) * bs, b, :] = 1.0
        put("cm_" + tag, cm)
    sm = np.zeros((64, 4), np.float32)
    for s in range(4):
        sm[s * 16:(s + 1) * 16, s] = 1.0
    put("seqm", sm)
    colm = np.zeros((128, 4, 64), np.float32)
    for s in range(4):
        colm[:, s, s * 16:(s + 1) * 16] = 1.0
    put("colm", colm)
    q = np.arange(128)
    dist = np.zeros((128, 2, 128), np.float32)
    am = np.zeros((128, 2, 128), np.float32)
    for kb in range(2):
        j = 128 * kb + p
        dist[:, kb, :] = np.abs(128 + q[None, :] - j[:, None])
        kc = j[:, None] // 64
        qc = q[None, :] // 64
        am[:, kb, :] = np.where((kc >= qc) & (kc <= qc + 2), 0.0, NEG)
    put("dist_P", dist)
    put("am_P", am)
    qs = np.arange(64)
    d = np.zeros((128, 4, 64), np.float32)
    a = np.zeros((128, 4, 64), np.float32)
    for s in range(4):
        i = qs - 16 * s
        d[:, s, :] = np.abs(128 + i[None, :] - p[:, None])
        a[:, s, :] = np.where((qs[None, :] // 16) == s, 0.0, NEG)
    put("dist_SC", d)
    put("am_SC", a)
    k64 = np.arange(64)
    put("dist_S1", np.abs(k64[:, None] % 16 - qs[None, :] % 16).astype(np.float32))
    put("am_S1", np.where((k64[:, None] // 16) == (qs[None, :] // 16), 0.0, NEG))
    t = np.arange(128)
    bc = np.zeros((128, 4, 128), np.float32)
    bp = np.zeros((128, 4, 128), np.float32)
    for gi, w in enumerate(POOLW):
        dd = t[None, :] - t[:, None]
        bc[:, gi, :] = ((dd >= 0) & (dd < w)) - w * np.eye(128)
        dd2 = t[None, :] - (t[:, None] - 128)
        bp[:, gi, :] = (dd2 < w)
    put("bc_P", bc)
    put("bp_P", bp)
    ts = np.arange(64)
    bcs = np.zeros((64, 4, 64), np.float32)
    bhs = np.zeros((60, 4, 64), np.float32)
    r = np.arange(60)
    for gi, w in enumerate(POOLW):
        same = (ts[:, None] // 16) == (ts[None, :] // 16)
        dd = ts[None, :] - ts[:, None]
        bcs[:, gi, :] = (same & (dd >= 0) & (dd < w)) - w * np.eye(64)
        sh = r[:, None] // 15
        rr = r[:, None] % 15
        i = ts[None, :] % 16
        bhs[:, gi, :] = (sh == (ts[None, :] // 16)) & ((i + 15 - rr) < w)
    put("bc_S", bcs)
    put("bh_S", bhs)
    ic = np.zeros((128, 4), np.float32)
    for gi, w in enumerate(POOLW):
        ic[:, gi] = 1.0 / np.minimum(t + 1, w)
    put("icnt0", ic)
    cc0 = np.zeros((128, 4), np.float32)
    for gi, w in enumerate(POOLW):
        cc0[:, gi] = w / np.minimum(t + 1, w) - 1.0
    put("ccor0", cc0)
    return c, c2, c3


PAR_ITEMS = [("gq", 64), ("gk", 64), ("onorm", 128), ("sinks", 8), ("alog", 4), ("dtb", 4),
             ("scale_pc", 8), ("convw", 48), ("norm_ab", 8), ("norm_c", 8)]
PAR_OFF, NPAR = _layout(PAR_ITEMS)


def build_program(NTP=16):
    nc = bass.Bass("TRN2", target_bir_lowering=False)
    SEQ = NTP * 128

    def din(name, shape):
        return nc.dram_tensor(name, list(shape), F32, kind="ExternalInput").ap()

    def dout(name, shape):
        return nc.dram_tensor(name, list(shape), F32, kind="ExternalOutput").ap()
    xp = din("xp", [SEQ, D]); xs = din("xs", [64, D])
    ck = din("ck", [4, 128, 128]); cv = din("cv", [4, 128, 128])
    sb = din("sb", [4, 4, 128, 128]); sconv = din("sconv", [12, 1536]); spool = din("spool", [60, D])
    wab = din("wab", [128, 8 * NAB]); woab = din("woab", [128, 8 * 1024])
    wic = din("wic", [128, 8 * 2048]); wgrp = din("wgrp", [128, 8 * 256]); woc = din("woc", [128, 8 * 1024])
    par_d = din("par", [128, NPAR]); cst_d = din("cst", [128, NCST]); cst2_d = din("cst2", [128, NCST2]); cst3_d = din("cst3", [128, NCST3])
    yp = dout("yp", [SEQ, D]); ys = dout("ys", [64, D])
    pak = dout("pak", [128, 128]); pav = dout("pav", [128, 128])
    pbs = dout("pbs", [4, 128, 128]); pbc = dout("pbc", [3, 1536]); pcp = dout("pcp", [15, D])
    sak = dout("sak", [64, 128]); sav = dout("sav", [64, 128])
    sbs = dout("sbs", [4, 4, 128, 128]); sbc = dout("sbc", [12, 1536]); scp = dout("scp", [60, D])

    with ExitStack() as es:
        S = Sched(nc, es)
        S.expand = {"yc": ["yc"] + ["yc%d" % i for i in range(6)], "tc": ["tc", "tcp"], "junk": ["junkA", "junkB0", "junkB1"], "junkB": ["junkB0", "junkB1"]}

        def T(name, shape, dt=F32):
            return es.enter_context(nc.sbuf_tensor("t_" + name, list(shape), dt))
        banks = [es.enter_context(nc.psum_tensor("psb%d" % i, [128, 512], F32)) for i in range(8)]
        bank_pool = [list(range(8))]
        bank_i = {}

        def bank():
            pl = bank_pool[0]
            k = tuple(pl)
            j = bank_i.get(k, 0)
            bank_i[k] = (j + 1) % len(pl)
            i = pl[j]
            return banks[i], "ps%d" % i

        cst = T("cst", [128, NCST]); par = T("par", [128, NPAR])
        yc = T("yc", [128, 1024]); tc_ = T("tc", [128, 1024]); junk = T("junk", [128, 1024])
        Wab = T("Wab", [128, 8, NAB], BF16); Woab = T("Woab", [128, 8, 1024], BF16)
        Wic = T("Wic", [128, 8, 2048], BF16); Wgrp = T("Wgrp", [128, 4, 2, 256], BF16)
        Woc = T("Woc", [128, 8, 1024], BF16)
        identb = T("identb", [128, 128], BF16)
        onesf = T("onesf", [128, 1])
        biasP = T("biasP", [128, 2, 8, 128], BF16); biasS1 = T("biasS1", [64, 8, 64], BF16)
        biasSC = biasP[:].rearrange("p a h q -> p (a h q)").rearrange("p (s h q) -> p s h q", s=4, h=8)
        esink = T("esink", [128, 8]); nega = T("nega", [128, 4]); mhalf = T("mhalf", [128, 16]); cneg = T("cneg", [128, 2])
        bands = T("bands", [128, 4 * 128 * 2 + 4 * 64 * 2], BF16)

        def C(name, rows=128):
            if name in CSTS_OFF:
                o, w = CSTS_OFF[name]
                o += CST_OFF["mS_P"][0]
            else:
                o, w = CST_OFF[name]
            return cst[0:rows, o:o + w]

        def C2(name, rows=128):
            o, w = CST2_OFF[name]
            t_ = (yc, tc_, junk)[o // 1024]
            assert (o + w - 1) // 1024 == o // 1024
            return t_[0:rows, o % 1024:o % 1024 + w]

        def P(name):
            o, w = PAR_OFF[name]
            return par[:, o:o + w]
        ident = C("ident")

        block = es.enter_context(nc.Block())
        op = S.op

        S.dma("sp", cst[:], cst_d, writes=["cst"])
        S.dma("sp", yc[:, 0:1024], cst2_d[:, 0:1024], writes=["yc"])
        S.dma("sp", tc_[:, 0:1024], cst2_d[:, 1024:2048], writes=["tc"])
        S.dma("sp", junk[:, 0:NCST2 - 2048], cst2_d[:, 2048:NCST2], writes=["junk"])
        S.dma("sp", par[:], par_d, writes=["par"])
        op("pool", lambda e: e.memset(onesf[:], 1.0), writes=["onesf"])
        op("pool", lambda e: e.memset(mhalf[:], -0.5), writes=["mhalf"])
        op("dve", lambda e: e.tensor_copy(out=identb[:], in_=ident), reads=["cst"], writes=["identb"])
        for bi_, nm in enumerate(("bc_P", "bp_P", "bc_S", "bh_S")):
            bo = (0, 512, 1024, 1280)[bi_]
            bw = CST2_OFF[nm][1]
            op("dve", lambda e, nm=nm, bo=bo, bw=bw: e.tensor_copy(out=bands[:, bo:bo + bw], in_=C2(nm)), reads=["yc", "tc", "junk"], writes=["bands"])

        bcP = bands[:, 0:512].rearrange("p (g t) -> p g t", g=4)
        bpP = bands[:, 512:1024].rearrange("p (g t) -> p g t", g=4)
        bcS = bands[:, 1024:1280].rearrange("p (g t) -> p g t", g=4)
        bhS = bands[:, 1280:1536].rearrange("p (g t) -> p g t", g=4)
        distP = C2("dist_P").rearrange("p (k q) -> p k q", k=2); amP = C2("am_P").rearrange("p (k q) -> p k q", k=2)
        distSC = junk[:, 0:256].rearrange("p (s q) -> p s q", s=4); amSC = junk[:, 256:512].rearrange("p (s q) -> p s q", s=4)
        for h in range(8):
            sl = -(2.0 ** (-(h + 1)))
            op("dve", lambda e, h=h, sl=sl: e.scalar_tensor_tensor(out=biasP[:, :, h, :], in0=distP, scalar=sl, in1=amP, op0=ALU.mult, op1=ALU.add), reads=["yc", "tc"], writes=["biasP"])
            op("dve", lambda e, h=h, sl=sl: e.scalar_tensor_tensor(out=biasS1[:, h, :], in0=C2("dist_S1", 64), scalar=sl, in1=C2("am_S1", 64), op0=ALU.mult, op1=ALU.add), reads=["yc", "tc", "junk"], writes=["biasS1"])
        op("dve", lambda e: e.tensor_reduce(out=cneg[:, 0:1], in_=P("gq"), axis=AX.X, op=ALU.max, apply_absolute_value=True), reads=["par"], writes=["cneg"])
        op("dve", lambda e: e.tensor_reduce(out=cneg[:, 1:2], in_=P("gk"), axis=AX.X, op=ALU.max, apply_absolute_value=True), reads=["par"], writes=["cneg"])
        op("dve", lambda e: e.scalar_tensor_tensor(out=cneg[:, 0:1], in0=cneg[:, 0:1], scalar=-8.0, in1=cneg[:, 1:2], op0=ALU.mult, op1=ALU.mult), reads=["cneg"], writes=["cneg"])
        op("act", lambda e: e.activation(out=esink[:], in_=P("sinks"), func=AF.Exp, bias=cneg[:, 0:1]), reads=["par", "cneg"], writes=["esink"])
        op("act", lambda e: e.activation(out=nega[:], in_=P("alog"), func=AF.Exp), reads=["par"], writes=["nega"])
        op("dve", lambda e: e.tensor_scalar(out=nega[:], in0=nega[:], scalar1=-1.0, scalar2=None, op0=ALU.mult), reads=["nega"], writes=["nega"])

        def load_weight(src, dst3, N, wname):
            keys = []
            for c in range(8):
                wk = "%s_%d" % (wname, c)
                keys.append(wk)
                S.dma("pool", dst3[:, c, :], src[:, c * N:(c + 1) * N], writes=[wk])
            return keys
        WK = {}
        WKB = {}
        for (c0_, n_) in ((CQ, 512), (CKV, 264), (CAG, 512), (CBG, 512), (CBQ, 768), (CBQ + 768, 768)):
            ks = []
            for c in range(8):
                wk = "Wab_%d_%d" % (c0_, c)
                ks.append(wk)
                S.dma("pool", Wab[:, c, c0_:c0_ + n_], wab[:, c * NAB + c0_:c * NAB + c0_ + n_], writes=[wk])
            WKB[c0_] = ks
        WK[id(Wab)] = [k for ks in WKB.values() for k in ks]
        WK[id(Woab)] = load_weight(woab, Woab, 1024, "Woab")
        WK[id(Wic)] = load_weight(wic, Wic, 2048, "Wic")
        WK[id(Wgrp)] = load_weight(wgrp, Wgrp[:].rearrange("p a b n -> p (a b) n"), 256, "Wgrp")
        WK[id(Woc)] = load_weight(woc, Woc, 1024, "Woc")
        S.nopool = True

        xtb = [T("xt0", [128, D]), T("xt1", [128, D])]
        xTA = T("xTA", [128, 8, 128], BF16)
        st8 = T("st8", [128, 32])
        b16a = T("b16a", [128, D], BF16)
        xn = cat = pooled = mg = b16a
        xT = T("xT", [128, 8, 128], BF16); catT = xT
        zq = T("zq", [128, 512]); zkv = T("zkv", [128, 264]); zag = T("zag", [128, 512]); zbg = T("zbg", [128, 512])
        ubt = T("ub", [128, 12 * 131]); uhist = T("uhist", [128, 12, 3])
        ubP = ubt[:].rearrange("p (c l) -> p c l", c=12)
        ubS = ubt[:, 0:12 * 4 * 19].rearrange("p (c s l) -> p c s l", c=12, s=4)
        qnb = T("qnb", [128, 512], BF16); kn32 = T("kn32", [128, 128]); knb = T("knb", [128, 128], BF16)
        qTa = T("qTa", [128, 4, 128], BF16)
        kTa = [T("kTa%d" % i, [128, 128], BF16) for i in range(2)]
        vaug = [T("vaug%d" % i, [128, 2, 65], BF16) for i in range(2)]
        kTc = T("kTc", [128, 4, 128], BF16); vcaug = T("vcaug", [128, 4, 2, 65], BF16)
        Es = T("Es", [128, 512]); sc_e = oacc = Es
        Ei = T("Ei", [128, 512]); ob = Ei
        oa = T("oa", [128, 512]); cvv = uu = oa
        cq = tc_[:, 0:512]; ckk = tc_[:, 512:1024]
        gt_ = T("gt", [128, 16]); gb = T("gb", [128, 32]); gts = T("gts", [128, 16]); gtot_t = T("gtot_t", [128, 16])
        Lb = T("Lb", [128, 512], BF16); Mb = [T("Mb%d" % i, [128, 512], BF16) for i in range(2)]
        Pb = [T("Pb%d" % i, [128, 512], BF16) for i in range(2)]; Xb1 = T("Xb", [128, 512], BF16)
        knB = T("knB", [128, 512], BF16); qnB = qnb
        kbg = T("kbg", [128, 512], BF16); kdec = T("kdec", [128, 512], BF16); vbb = T("vbb", [128, 512], BF16)
        kdm = Pb[1][0:64, :]
        kTB = T("kTB", [128, 4, 128], BF16); qTB = qTa
        wT = Mb[0][:].rearrange("p (h n) -> p h n", h=4)
        wTz = Pb[0][:, 0:256].rearrange("p (h n) -> p h n", h=4); qTz = Pb[0][:, 256:512].rearrange("p (h n) -> p h n", h=4)
        qkb = knB; qkT = T("qkT", [128, 512], BF16)
        vnew = Lb
        pT01 = Mb
        Sm = T("Sm", [128, 4, 128]); Sbf = T("Sbf", [128, 4, 128], BF16)
        u32 = yc; gate = tc_
        ubf = [T("ubf%d" % i, [128, 1024], BF16) for i in range(2)]
        hnew = Es[:, 0:144].rearrange("p (c r) -> p c r", c=12)
        gA = T("gA", [128, 512], BF16); gB = T("gB", [128, 512], BF16)
        xnA = T("xnA", [128, 1024], BF16)

        def rms_to_xT(src, srckey, NT, gain, early=False):
            if early:
                c0, tagk = 28, "A"
                xb, xk = xnA, "xnA"
                dst, dstkey = xTA, "xTA"
            else:
                c0, tagk = 0, ""
                xb, xk = xn, "b16a"
                dst, dstkey = xT, "xT"
            op("act", lambda e: e.activation(out=xb[0:NT, :], in_=src[0:NT, :], func=AF.Square, accum_out=st8[0:NT, c0:c0 + 1]),
               reads=[srckey], writes=[xk, "st8a" + tagk])
            if getattr(S, "nopool", False):
                op("act", lambda e: e.activation(out=st8[0:NT, c0 + 1:c0 + 2], in_=st8[0:NT, c0:c0 + 1], func=AF.Sqrt, scale=1.0 / D, bias=EPS),
                   reads=["st8a" + tagk], writes=["st8b" + tagk])
                op("dve", lambda e: e.reciprocal(out=st8[0:NT, c0 + 2:c0 + 3], in_=st8[0:NT, c0 + 1:c0 + 2]),
                   reads=["st8b" + tagk], writes=["st8c" + tagk])
            else:
                op("pool", lambda e: e.tensor_scalar(out=st8[0:NT, c0 + 1:c0 + 2], in0=st8[0:NT, c0:c0 + 1], scalar1=1.0 / D, scalar2=EPS, op0=ALU.mult, op1=ALU.add),
                   reads=["st8a" + tagk], writes=["st8b" + tagk])
                op("pool", lambda e: e.tensor_tensor(out=st8[0:NT, c0 + 2:c0 + 3], in0=st8[0:NT, c0 + 1:c0 + 2], in1=mhalf[0:NT, 0:1], op=ALU.pow),
                   reads=["st8b" + tagk, "mhalf"], writes=["st8c" + tagk])
            op("dve", lambda e: e.tensor_scalar(out=xb[0:NT, :], in0=src[0:NT, :], scalar1=st8[0:NT, c0 + 2:c0 + 3], scalar2=None, op0=ALU.mult),
               reads=[srckey, "st8c" + tagk], writes=[xk])
            transpose8(xb, xk, dst, dstkey, NT, gain)

        def transpose8(src, srckey, dst, dstkey, NT, gain=None, chunk=None):
            pb, pk = bank()
            pbb = pb[:].bitcast(BF16)
            if chunk is None:
                chunk = lambda c: src[0:NT, c * 128:(c + 1) * 128]
            srckeys = list(srckey) if isinstance(srckey, (list, tuple)) else [srckey]
            for c in range(8):
                op("pe", lambda e, c=c: e.transpose(out=pbb[:, c * NT:(c + 1) * NT], in_=chunk(c), identity=identb[0:NT, 0:NT]),
                   reads=srckeys + ["identb"], writes=[pk], acc=(c > 0))
            if gain is None:
                op("act", lambda e: e.activation(out=dst[:, :, 0:NT], in_=pbb[:, 0:8 * NT].rearrange("p (c n) -> p c n", c=8), func=AF.Copy),
                   reads=[pk], writes=[dstkey])
            else:
                op("dve", lambda e: e.tensor_tensor(out=dst[:, :, 0:NT], in0=pbb[:, 0:8 * NT].rearrange("p (c n) -> p c n", c=8),
                                                    in1=gain.unsqueeze(2).to_broadcast([128, 8, NT]), op=ALU.mult),
                   reads=[pk, "par"], writes=[dstkey])

        def proj_tm(W, c0, ncols, NT, lhs, lhskey):
            pb, pk = bank()
            for k in range(8):
                op("pe", lambda e, k=k: e.matmul(pb[0:NT, 0:ncols], lhsT=lhs[:, k, 0:NT], rhs=W[:, k, c0:c0 + ncols], start=(k == 0), stop=(k == 7)),
                   reads=[lhskey] + (WKB[c0] if (W is Wab and c0 in WKB) else WK[id(W)]), writes=[pk], acc=(k > 0))
            return pb, pk

        def silu2(dst, z, zkey, tmp, tmpkey, dstkey, NT, n):
            op("act", lambda e: e.activation(out=tmp[0:NT, 0:n], in_=z[0:NT, 0:n], func=AF.Tanh, scale=0.5), reads=[zkey], writes=[tmpkey])
            op("dve", lambda e: e.scalar_tensor_tensor(out=dst[0:NT, 0:n], in0=tmp[0:NT, 0:n], scalar=1.0, in1=z[0:NT, 0:n], op0=ALU.add, op1=ALU.mult),
               reads=[tmpkey, zkey], writes=[dstkey])

        def rsqrt_small(dst, src, key_src, key_dst, NT, n, mul, eps):
            if getattr(S, "nopool", False):
                op("act", lambda e: e.activation(out=dst, in_=src, func=AF.Sqrt, scale=mul, bias=eps), reads=[key_src], writes=[key_dst])
                op("dve", lambda e: e.reciprocal(out=dst, in_=dst), reads=[key_dst], writes=[key_dst])
                return
            op("pool", lambda e: e.tensor_scalar(out=dst, in0=src, scalar1=mul, scalar2=eps, op0=ALU.mult, op1=ALU.add), reads=[key_src], writes=[key_dst])
            op("pool", lambda e: e.tensor_tensor(out=dst, in0=dst, in1=mhalf[0:NT, 0:n], op=ALU.pow), reads=[key_dst, "mhalf"], writes=[key_dst])

        def sample_prologue():
            ckf = yc[:, 0:512].rearrange("p (s c) -> p s c", s=4)
            cvf = tc_[:, 0:512].rearrange("p (s c) -> p s c", s=4)
            ckb = kbg[:, 0:512].rearrange("p (s c) -> p s c", s=4)
            S.dma("sp", ckf, ck.rearrange("s k c -> k s c"), writes=["yc"])
            S.dma("sp", cvf, cv.rearrange("s k c -> k s c"), writes=["tc"])
            op("dve", lambda e: e.tensor_copy(out=ckb, in_=ckf), reads=["yc"], writes=["kbg"])
            pb, pk = bank()
            pbb = pb[:].bitcast(BF16)
            for s_ in range(4):
                op("pe", lambda e, s_=s_: e.transpose(out=pbb[:, s_ * 128:(s_ + 1) * 128], in_=ckb[:, s_, :], identity=identb[:]),
                   reads=["kbg", "identb"], writes=[pk], acc=(s_ > 0))
            op("act", lambda e: e.activation(out=kTc[:].rearrange("p s k -> p (s k)"), in_=pbb[:, 0:512], func=AF.Copy), reads=[pk], writes=["kTc"])
            op("pool", lambda e: e.memset(vcaug[:], 1.0), writes=["vcaug"])
            op("dve", lambda e: e.tensor_copy(out=vcaug[:, :, :, 0:64], in_=cvf.rearrange("p s (g d) -> p s g d", g=2)), reads=["tc", "vcaug"], writes=["vcaug"])
            S.dma("sp", junk[:, 0:512], cst3_d[:, 0:512], writes=["junk"])
            oM = CST_OFF["mS_P"][0]
            S.dma("sp", cst[:, oM:oM + 704], cst3_d[:, 512:512 + 704], writes=["cst"])
            for h in range(8):
                sl = -(2.0 ** (-(h + 1)))
                op("dve", lambda e, h=h, sl=sl: e.scalar_tensor_tensor(out=biasSC[:, :, h, :], in0=distSC, scalar=sl, in1=amSC, op0=ALU.mult, op1=ALU.add), reads=["junk"], writes=["biasP"])
        for i in range(2):
            op("pool", lambda e, i=i: e.memset(vaug[i][:], 1.0), writes=["vaug%d" % i])
        op("pool", lambda e: e.memset(Sm[:], 0.0), writes=["Sm"])
        op("pool", lambda e: e.memset(Sbf[:], 0.0), writes=["Sbf"])
        op("pool", lambda e: e.memset(uhist[:], 0.0), writes=["uhist"])

        def emit_A(t):
            sample = (t == NTP)
            NT = 64 if sample else 128
            BS = 16 if sample else 64
            NB = NT // BS
            tg = "S" if sample else "P"
            par_i = t % 2
            xin = xs if sample else xp[t * 128:(t + 1) * 128, :]
            yout = ys if sample else yp[t * 128:(t + 1) * 128, :]
            last_prompt = (t == NTP - 1)

            xt = xtb[t % 2]
            XT = "xt%d" % (t % 2)
            if sample:
                sample_prologue()
            S.dma("act", xt[0:NT, :], xin, writes=[XT])
            rms_to_xT(xt, XT, NT, P("norm_ab"), early=True)

            pq, pqk = proj_tm(Wab, CQ, 512, NT, xTA, "xTA")
            op("act", lambda e: e.activation(out=zq[0:NT, :], in_=pq[0:NT, 0:512], func=AF.Copy), reads=[pqk], writes=["zq"])
            pkv, pkvk = proj_tm(Wab, CKV, 264, NT, xTA, "xTA")
            op("dve", lambda e: e.tensor_copy(out=zkv[0:NT, :], in_=pkv[0:NT, 0:264]), reads=[pkvk], writes=["zkv"])
            pag, pagk = proj_tm(Wab, CAG, 512, NT, xTA, "xTA")
            op("act", lambda e: e.activation(out=zag[0:NT, :], in_=pag[0:NT, 0:512], func=AF.Copy), reads=[pagk], writes=["zag"])
            pbg, pbgk = proj_tm(Wab, CBG, 512, NT, xTA, "xTA")
            op("dve", lambda e: e.tensor_copy(out=zbg[0:NT, :], in_=pbg[0:NT, 0:512]), reads=[pbgk], writes=["zbg"])

            ucur, ukey = (ubS if sample else ubP), "ub"
            if not sample:
                op("pool", lambda e: e.tensor_copy(out=ubP[:, :, 0:3], in_=uhist[:]), reads=["uhist"], writes=["ub"])
            for c4 in range(3):
                pb, pk = bank()
                for cc in range(4):
                    ch = c4 * 4 + cc
                    for k in range(8):
                        op("pe", lambda e, k=k, ch=ch, cc=cc, pb=pb: e.matmul(pb[:, cc * NT:(cc + 1) * NT], lhsT=Wab[:, k, CBQ + ch * 128:CBQ + (ch + 1) * 128], rhs=xTA[:, k, 0:NT], start=(k == 0), stop=(k == 7)),
                           reads=["xTA"] + WKB[CBQ if ch < 6 else CBQ + 768], writes=[pk], acc=not (k == 0 and cc == 0))
                if sample:
                    op("act", lambda e, c4=c4, pb=pb: e.activation(out=ubS[:, c4 * 4:(c4 + 1) * 4, :, 3:19], in_=pb[:, 0:256].rearrange("p (c s l) -> p c s l", c=4, s=4), func=AF.Copy),
                       reads=[pk], writes=[ukey])
                else:
                    op("act", lambda e, c4=c4, pb=pb: e.activation(out=ucur[:, c4 * 4:(c4 + 1) * 4, 3:131], in_=pb[:, 0:512].rearrange("p (c l) -> p c l", c=4), func=AF.Copy),
                       reads=[pk], writes=[ukey])
            if sample:
                S.dma("sp", tc_[0:12, 0:1024], sconv[:, 0:1024], writes=["tc"])
                S.dma("sp", junk[0:12, 0:512], sconv[:, 1024:1536], writes=["junk"])
                pb, pk = bank()
                for ch in range(12):
                    op("pe", lambda e, ch=ch, pb=pb: e.transpose(out=pb[:, ch * 12:(ch + 1) * 12], in_=(tc_[0:12, ch * 128:(ch + 1) * 128] if ch < 8 else junk[0:12, (ch - 8) * 128:(ch - 7) * 128]), identity=ident[0:12, 0:12]),
                       reads=["tc", "junk", "cst"], writes=[pk], acc=(ch > 0))
                op("act", lambda e, pb=pb: e.activation(out=ubS[:, :, :, 0:3], in_=pb[:, 0:144].rearrange("p (c s r) -> p c s r", c=12, s=4), func=AF.Copy),
                   reads=[pk], writes=[ukey])
            else:
                op("pool", lambda e: e.tensor_copy(out=uhist[:], in_=ubP[:, :, 128:131]), reads=["ub"], writes=["uhist"])

        def emit_BCD(t, recE=None):
            sample = (t == NTP)
            NT = 64 if sample else 128
            BS = 16 if sample else 64
            NB = NT // BS
            tg = "S" if sample else "P"
            par_i = t % 2
            xin = xs if sample else xp[t * 128:(t + 1) * 128, :]
            yout = ys if sample else yp[t * 128:(t + 1) * 128, :]
            last_prompt = (t == NTP - 1)

            xt = xtb[t % 2]
            XT = "xt%d" % (t % 2)
            ucur, ukey = (ubS if sample else ubP), "ub"
            if sample or last_prompt:
                n = 12 if sample else 3
                if sample:
                    op("pool", lambda e: e.tensor_copy(out=hnew[:].rearrange("p c (s r) -> p c s r", s=4), in_=ubS[:, :, :, 16:19]), reads=[ukey], writes=["Es"])
                else:
                    op("pool", lambda e: e.tensor_copy(out=hnew[:, :, 0:3], in_=ucur[:, :, 128:131]), reads=[ukey], writes=["Es"])
                for c4 in range(3):
                    pb, pk = bank()
                    for cc in range(4):
                        ch = c4 * 4 + cc
                        op("pe", lambda e, ch=ch, cc=cc, pb=pb, n=n: e.transpose(out=pb[0:n, cc * 128:(cc + 1) * 128], in_=hnew[:, ch, 0:n], identity=ident),
                           reads=["Es", "cst"], writes=[pk], acc=(cc > 0))
                    op("act", lambda e, pb=pb, c4=c4, n=n: e.activation(out=junk[0:n, 0:512], in_=pb[0:n, 0:512], func=AF.Copy), reads=[pk], writes=["junk"])
                    S.dma("sp", (sbc if sample else pbc)[:, c4 * 512:(c4 + 1) * 512], junk[0:n, 0:512], reads=["junk"], is_output=True)

            tmaj = ((cq, "tc"), (ckk, "tc"), (cvv, "oa"))
            tb = []
            first = [True, True, True]

            def conv_half(half):
                cw = P("convw")

                def views(c6):
                    ch = half * 6 + c6
                    if sample:
                        return (lambda j, ch=ch: ubS[:, ch, :, j:j + 16]), yc[:, c6 * NT:(c6 + 1) * NT].rearrange("p (s l) -> p s l", s=4)
                    return (lambda j, ch=ch: ucur[:, ch, j:j + 128]), yc[:, c6 * NT:(c6 + 1) * NT]
                for j in range(4):
                    for c6 in range(6):
                        ch = half * 6 + c6
                        uv, yv = views(c6)
                        yk = "yc%d" % c6
                        if j == 0:
                            op("dve", lambda e, uv=uv, yv=yv, ch=ch: e.tensor_scalar(out=yv, in0=uv(0), scalar1=cw[:, ch * 4:ch * 4 + 1], scalar2=None, op0=ALU.mult), reads=[ukey, "par"], writes=[yk])
                        else:
                            op("dve", lambda e, uv=uv, yv=yv, ch=ch, j=j: e.scalar_tensor_tensor(out=yv, in0=uv(j), scalar=cw[:, ch * 4 + j:ch * 4 + j + 1], in1=yv, op0=ALU.mult, op1=ALU.add), reads=[ukey, "par", yk], writes=[yk])
                if half == 0:
                    silu2(yc, yc, "yc", tc_, "tc", "yc", 128, 6 * NT)
                else:
                    silu2(yc, yc, "yc", junk, "junk", "yc", 128, 6 * NT)
                for c6 in range(6):
                    ch = half * 6 + c6
                    qi, h = ch // 4, ch % 4
                    pb, pk = tb[qi]
                    op("pe", lambda e, pb=pb, h=h, c6=c6: e.transpose(out=pb[0:NT, h * 128:(h + 1) * 128], in_=yc[:, c6 * NT:(c6 + 1) * NT], identity=ident),
                       reads=["yc%d" % c6, "cst"], writes=[pk], acc=not first[qi])
                    first[qi] = False
                    if h == 3:
                        dst, dk_ = tmaj[qi]
                        op("act", lambda e, pb=pb, dst=dst: e.activation(out=dst[0:NT, :], in_=pb[0:NT, 0:512], func=AF.Copy), reads=[pk], writes=[dk_])
            silu2(gA, zag, "zag", junk, "junk", "gA", NT, 512)
            silu2(gB, zbg, "zbg", junk, "junk", "gB", NT, 512)
            bank_pool[0] = [2] if recE is not None else [3, 4]
            tb.extend((banks[i], "ps%d" % i) for i in (5, 6, 7))
            recY = S.record()
            op("act", lambda e: e.activation(out=gt_[0:NT, 0:4], in_=zkv[0:NT, 256:260], func=AF.Tanh, scale=0.5), reads=["zkv"], writes=["gtb"])
            op("dve", lambda e: e.tensor_scalar(out=gt_[0:NT, 0:4], in0=gt_[0:NT, 0:4], scalar1=1.0, scalar2=0.5, op0=ALU.add, op1=ALU.mult), reads=["gtb"], writes=["gtb"])
            op("dve", lambda e: e.tensor_tensor(out=gt_[0:NT, 4:8], in0=zkv[0:NT, 260:264], in1=P("dtb")[0:NT, :], op=ALU.add), reads=["zkv", "par"], writes=["gtg"])
            op("act", lambda e: e.activation(out=gt_[0:NT, 4:8], in_=gt_[0:NT, 4:8], func=AF.Exp), reads=["gtg"], writes=["gtg"])
            op("act", lambda e: e.activation(out=gt_[0:NT, 4:8], in_=gt_[0:NT, 4:8], func=AF.Ln, bias=1.0), reads=["gtg"], writes=["gtg"])
            op("dve", lambda e: e.tensor_tensor(out=gt_[0:NT, 4:8], in0=gt_[0:NT, 4:8], in1=nega[0:NT, :], op=ALU.mult), reads=["gtg", "nega"], writes=["gtg"])
            tri = C("tri_" + tg, NT); blk = C("blk_" + tg, NT); cm = C("cm_" + tg, NT).rearrange("p (b q) -> p b q", b=NB); mS = C("mS_" + tg, NT)
            pg, pgk = bank()
            op("pe", lambda e, pg=pg: e.matmul(pg[0:NT, 0:4], lhsT=tri, rhs=gt_[0:NT, 4:8], start=True, stop=True), reads=["cst", "gtg"], writes=[pgk])
            op("pe", lambda e, pg=pg: e.matmul(pg[0:NT, 4:8], lhsT=blk, rhs=gt_[0:NT, 4:8], start=True, stop=True), reads=["cst", "gtg"], writes=[pgk], acc=True)
            for b in range(NB):
                op("pe", lambda e, pg=pg, b=b: e.matmul(pg[:, 8 + 4 * b:12 + 4 * b], lhsT=cm[:, b, :], rhs=gt_[0:NT, 4:8], start=True, stop=True), reads=["cst", "gtg"], writes=[pgk], acc=True)
            op("dve", lambda e, pg=pg: e.tensor_copy(out=gts[0:NT, 0:4], in_=pg[0:NT, 0:4]), reads=[pgk], writes=["gts"])
            op("dve", lambda e, pg=pg: e.tensor_tensor(out=gts[0:NT, 4:8], in0=pg[0:NT, 4:8], in1=gts[0:NT, 0:4], op=ALU.subtract), reads=[pgk, "gts"], writes=["gts"])
            op("act", lambda e, pg=pg: e.activation(out=gtot_t[:, 0:4 * NB], in_=pg[:, 8:8 + 4 * NB], func=AF.Exp), reads=[pgk], writes=["gtot"])
            gtot_ap = gtot_t
            op("act", lambda e: e.activation(out=gts[0:NT, 8:16], in_=gts[0:NT, 0:8], func=AF.Exp), reads=["gts"], writes=["gtse"])
            for h in range(4):
                op("dve", lambda e, h=h: e.tensor_scalar(out=Es[0:NT, h * 128:h * 128 + NT], in0=onesf[0:NT, 0:1].to_broadcast([NT, NT]), scalar1=gt_[0:NT, 4 + h:5 + h], scalar2=None, op0=ALU.mult), reads=["onesf", "gtg"], writes=["Es"])
            op("dve", lambda e: e.tensor_scalar(out=Ei[0:NT, 0:512], in0=Es[0:NT, 0:512], scalar1=-1.0, scalar2=None, op0=ALU.mult), reads=["Es"], writes=["Ei"])
            pd, pdk = bank()
            for h in range(4):
                op("pe", lambda e, pd=pd, h=h: e.matmul(pd[0:NT, h * NT:(h + 1) * NT], lhsT=tri, rhs=Es[0:NT, h * 128:h * 128 + NT], start=True, stop=False), reads=["cst", "Es"], writes=[pdk], acc=(h > 0))
                op("pe", lambda e, pd=pd, h=h: e.matmul(pd[0:NT, h * NT:(h + 1) * NT], lhsT=Ei[0:NT, h * 128:h * 128 + NT], rhs=tri, start=False, stop=True), reads=["cst", "Ei"], writes=[pdk], acc=True)
            op("dve", lambda e, pd=pd: e.tensor_tensor(out=Es[0:NT, 0:4 * NT].rearrange("p (h j) -> p h j", h=4), in0=pd[0:NT, 0:4 * NT].rearrange("p (h j) -> p h j", h=4),
                                                       in1=mS.unsqueeze(1).to_broadcast([NT, 4, NT]), op=ALU.add), reads=[pdk, "cst"], writes=["Es"])
            op("act", lambda e: e.activation(out=Es[0:NT, 0:4 * NT], in_=Es[0:NT, 0:4 * NT], func=AF.Exp), reads=["Es"], writes=["Es"])
            op("dve", lambda e: e.tensor_tensor(out=Ei[0:NT, 0:4 * NT].rearrange("p (h j) -> p h j", h=4), in0=Es[0:NT, 0:4 * NT].rearrange("p (h j) -> p h j", h=4),
                                                in1=ident[0:NT, 0:NT].unsqueeze(1).to_broadcast([NT, 4, NT]), op=ALU.add), reads=["Es", "cst"], writes=["Ei"])

            conv_half(0)
            S.stop()
            bank_pool[0] = [0, 1] if recE is not None else [0, 1, 2]
            recX = S.record()
            op("dve", lambda e: e.tensor_tensor(out=junk[0:NT, 0:512], in0=zq[0:NT, :], in1=zq[0:NT, :], op=ALU.mult), reads=["zq"], writes=["junkA"])
            op("dve", lambda e: e.tensor_reduce(out=st8[0:NT, 4:12], in_=junk[0:NT, 0:512].rearrange("p (h d) -> p h d", h=8), axis=AX.X, op=ALU.add), reads=["junkA"], writes=["st8q"])
            op("dve", lambda e: e.tensor_tensor(out=kn32[0:NT, :], in0=zkv[0:NT, 0:128], in1=zkv[0:NT, 0:128], op=ALU.mult), reads=["zkv"], writes=["kn32"])
            op("dve", lambda e: e.tensor_reduce(out=st8[0:NT, 12:14], in_=kn32[0:NT, :].rearrange("p (h d) -> p h d", h=2), axis=AX.X, op=ALU.add), reads=["kn32"], writes=["st8q"])
            rsqrt_small(st8[0:NT, 4:14], st8[0:NT, 4:14], "st8q", "st8q", NT, 10, 1.0 / 64, EPS)
            op("dve", lambda e: e.tensor_tensor(out=junk[0:NT, 0:512].rearrange("p (h d) -> p h d", h=8), in0=zq[0:NT, :].rearrange("p (h d) -> p h d", h=8),
                                                in1=st8[0:NT, 4:12].unsqueeze(2).to_broadcast([NT, 8, 64]), op=ALU.mult), reads=["zq", "st8q"], writes=["junkA"])
            op("dve", lambda e: e.tensor_tensor(out=qnb[0:NT, :].rearrange("p (h d) -> p h d", h=8), in0=junk[0:NT, 0:512].rearrange("p (h d) -> p h d", h=8),
                                                in1=P("gq")[0:NT, :].unsqueeze(1).to_broadcast([NT, 8, 64]), op=ALU.mult), reads=["junkA", "par"], writes=["qnb"])
            op("dve", lambda e: e.tensor_tensor(out=kn32[0:NT, :].rearrange("p (h d) -> p h d", h=2), in0=zkv[0:NT, 0:128].rearrange("p (h d) -> p h d", h=2),
                                                in1=st8[0:NT, 12:14].unsqueeze(2).to_broadcast([NT, 2, 64]), op=ALU.mult), reads=["zkv", "st8q", "kn32"], writes=["kn32"])
            op("dve", lambda e: e.tensor_tensor(out=kn32[0:NT, :].rearrange("p (h d) -> p h d", h=2), in0=kn32[0:NT, :].rearrange("p (h d) -> p h d", h=2),
                                                in1=P("gk")[0:NT, :].unsqueeze(1).to_broadcast([NT, 2, 64]), op=ALU.mult), reads=["kn32", "par"], writes=["kn32"])
            op("pool", lambda e: e.tensor_copy(out=knb[0:NT, :], in_=kn32[0:NT, :]), reads=["kn32"], writes=["knb"])
            vcur, vkey = vaug[par_i], "vaug%d" % par_i
            op("pool", lambda e: e.tensor_copy(out=vcur[0:NT, :, 0:64], in_=zkv[0:NT, 128:256].rearrange("p (g d) -> p g d", g=2)), reads=["zkv"], writes=[vkey])
            if sample:
                S.dma("sp", sak, kn32[0:64, :], reads=["kn32"], is_output=True)
                S.dma("sp", sav, zkv[0:64, 128:256], reads=["zkv"], is_output=True)
            elif last_prompt:
                S.dma("sp", pak, kn32[:, :], reads=["kn32"], is_output=True)
                S.dma("sp", pav, zkv[:, 128:256], reads=["zkv"], is_output=True)
            kcur, kkey = kTa[par_i], "kTa%d" % par_i
            pb, pk = bank()
            pbb = pb[:].bitcast(BF16)
            for m in range(4):
                op("pe", lambda e, m=m, pbb=pbb: e.transpose(out=pbb[:, m * NT:(m + 1) * NT], in_=qnb[0:NT, m * 128:(m + 1) * 128], identity=identb[0:NT, 0:NT]),
                   reads=["qnb", "identb"], writes=[pk], acc=(m > 0))
            op("pe", lambda e, pbb=pbb: e.transpose(out=pbb[:, 4 * NT:5 * NT], in_=knb[0:NT, :], identity=identb[0:NT, 0:NT]), reads=["knb", "identb"], writes=[pk], acc=True)
            op("act", lambda e, pbb=pbb: e.activation(out=qTa[:, :, 0:NT], in_=pbb[:, 0:4 * NT].rearrange("p (m n) -> p m n", m=4), func=AF.Copy), reads=[pk], writes=["qTa"])
            op("act", lambda e, pbb=pbb: e.activation(out=kcur[:, 0:NT], in_=pbb[:, 4 * NT:5 * NT], func=AF.Copy), reads=[pk], writes=[kkey])
            kbs = []
            if sample:
                for s in range(4):
                    kbs.append((lambda g, s=s: kTc[g * 64:(g + 1) * 64, s, :], "kTc", lambda g, s=s: vcaug[:, s, g, :], "vcaug", 128,
                                lambda g, s=s: biasSC[:, s, g * 4:(g + 1) * 4, :], "biasP"))
                kbs.append((lambda g: kcur[g * 64:(g + 1) * 64, 0:64], kkey, lambda g: vcur[0:64, g, :], vkey, 64,
                            lambda g: biasS1[:, g * 4:(g + 1) * 4, :], "biasS1"))
            else:
                if t > 0:
                    kprev, vprev = kTa[1 - par_i], vaug[1 - par_i]
                    kbs.append((lambda g: kprev[g * 64:(g + 1) * 64, :], "kTa%d" % (1 - par_i), lambda g: vprev[:, g, :], "vaug%d" % (1 - par_i), 128,
                                lambda g: biasP[:, 0, g * 4:(g + 1) * 4, :], "biasP"))
                kbs.append((lambda g: kcur[g * 64:(g + 1) * 64, :], kkey, lambda g: vcur[:, g, :], vkey, 128,
                            lambda g: biasP[:, 1, g * 4:(g + 1) * 4, :], "biasP"))
            def pTv(bi):
                if bi < 2:
                    return pT01[bi]
                return (Pb[0][:, 0:256], Pb[0][:, 256:512], Pb[1][:, 0:256])[bi - 2]

            def pTk(bi):
                return ("Mb0", "Mb1", "Pb0", "Pb0", "Pb1")[bi]
            for g in range(2):
                for bi, (kf, kk, vf, vk, nk, bf, bk) in enumerate(kbs):
                    pb, pk = bank()
                    op("pe", lambda e, pb=pb, kf=kf, nk=nk, g=g: e.matmul(pb[0:nk, 0:4 * NT], lhsT=kf(g), rhs=qTa[g * 64:(g + 1) * 64, :, 0:NT], start=True, stop=True),
                       reads=[kk, "qTa"], writes=[pk])
                    so_ = 512 + (256 * (bi % 2) if sample else 0)
                    sk_ = ("junkB%d" % (bi % 2)) if sample else "junkB"
                    op("dve", lambda e, pb=pb, nk=nk, bf=bf, g=g, so_=so_: e.scalar_tensor_tensor(out=junk[0:nk, so_:so_ + 4 * NT].rearrange("p (m q) -> p m q", m=4), in0=pb[0:nk, 0:4 * NT].rearrange("p (m q) -> p m q", m=4),
                                                                                       scalar=0.125, in1=bf(g), op0=ALU.mult, op1=ALU.add), reads=[pk, bk], writes=[sk_])
                    op("act", lambda e, nk=nk, bi=bi, so_=so_: e.activation(out=pTv(bi)[0:nk, 0:4 * NT], in_=junk[0:nk, so_:so_ + 4 * NT], func=AF.Exp, bias=cneg[0:nk, 0:1]), reads=[sk_, "cneg"], writes=[pTk(bi)])
                po, pok = bank()
                for m in range(4):
                    for bi, (kf, kk, vf, vk, nk, bf, bk) in enumerate(kbs):
                        op("pe", lambda e, po=po, m=m, bi=bi, nk=nk, vf=vf, g=g: e.matmul(po[0:NT, m * 65:(m + 1) * 65], lhsT=pTv(bi)[0:nk, m * NT:(m + 1) * NT], rhs=vf(g), start=(bi == 0), stop=(bi == len(kbs) - 1)),
                           reads=[pTk(bi), vk], writes=[pok], acc=not (m == 0 and bi == 0))
                pov = po[0:NT, 0:260].rearrange("p (m c) -> p m c", m=4)
                op("dve", lambda e, pov=pov, g=g: e.tensor_tensor(out=st8[0:NT, 16 + g * 4:20 + g * 4], in0=pov[:, :, 64], in1=esink[0:NT, g * 4:(g + 1) * 4], op=ALU.add), reads=[pok, "esink"], writes=["st8d%d" % g])
                op("dve", lambda e, g=g: e.reciprocal(out=st8[0:NT, 16 + g * 4:20 + g * 4], in_=st8[0:NT, 16 + g * 4:20 + g * 4]), reads=["st8d%d" % g], writes=["st8d%d" % g])
                op("dve", lambda e, pov=pov, g=g: e.tensor_tensor(out=oa[0:NT, g * 256:(g + 1) * 256].rearrange("p (m d) -> p m d", m=4), in0=pov[:, :, 0:64],
                                                                  in1=st8[0:NT, 16 + g * 4:20 + g * 4].unsqueeze(2).to_broadcast([NT, 4, 64]), op=ALU.mult), reads=[pok, "st8d%d" % g], writes=["oa"])

            S.stop()
            if recE is not None:
                S.merge(recE, recY, recX)
            else:
                S.merge(recX, recY)
            if sample:
                SS = ((Sm, "Sm"), (zq[:].rearrange("p (h d) -> p h d", h=4), "zq"), (zag[:].rearrange("p (h d) -> p h d", h=4), "zag"), (zbg[:].rearrange("p (h d) -> p h d", h=4), "zbg"))
                for s_ in range(4):
                    S.dma("sp", SS[s_][0][:] if s_ == 0 else SS[s_][0], sb[s_].rearrange("h k v -> k h v"), writes=[SS[s_][1]])
            op("dve", lambda e: e.scalar_tensor_tensor(out=cat[0:NT, 0:512], in0=oa[0:NT, :], scalar=0.5, in1=gA[0:NT, 0:512], op0=ALU.mult, op1=ALU.mult), reads=["oa", "gA"], writes=["b16a"])
            bank_pool[0] = [0, 1, 2]
            recC = S.record()
            conv_half(1)
            for qi, (src, sk) in enumerate(((cq, "tc"), (ckk, "tc"))):
                op("dve", lambda e, src=src: e.tensor_tensor(out=junk[0:NT, 0:512], in0=src[0:NT, :], in1=src[0:NT, :], op=ALU.mult), reads=[sk], writes=["junk"])
                op("dve", lambda e, qi=qi: e.tensor_reduce(out=gt_[0:NT, 8 + qi * 4:12 + qi * 4], in_=junk[0:NT, 0:512].rearrange("p (h d) -> p h d", h=4), axis=AX.X, op=ALU.add), reads=["junk"], writes=["gtn"])
            rsqrt_small(gt_[0:NT, 8:16], gt_[0:NT, 8:16], "gtn", "gtn", NT, 8, 1.0, 4 * EPS)
            op("dve", lambda e: e.tensor_tensor(out=gb[0:NT, 0:4], in0=gt_[0:NT, 12:16], in1=gt_[0:NT, 0:4], op=ALU.mult), reads=["gtn", "gtb"], writes=["gbs"])
            op("dve", lambda e: e.tensor_tensor(out=gb[0:NT, 4:8], in0=gb[0:NT, 0:4], in1=gts[0:NT, 8:12], op=ALU.mult), reads=["gbs", "gtse"], writes=["gbs"])
            op("dve", lambda e: e.tensor_tensor(out=gb[0:NT, 8:12], in0=gt_[0:NT, 12:16], in1=gts[0:NT, 12:16], op=ALU.mult), reads=["gtn", "gtse", "gbs"], writes=["gbs"])
            op("dve", lambda e: e.tensor_scalar(out=gb[0:NT, 12:16], in0=gt_[0:NT, 8:12], scalar1=128.0 ** -0.5, scalar2=None, op0=ALU.mult), reads=["gtn", "gbs"], writes=["gbs"])
            op("dve", lambda e: e.tensor_scalar(out=gb[0:NT, 16:20], in0=gt_[0:NT, 0:4], scalar1=0.5, scalar2=None, op0=ALU.mult), reads=["gtb", "gbs"], writes=["gbs"])

            def bc4(a):
                return a.unsqueeze(2).to_broadcast([NT, 4, 128])

            def v3(x):
                return x[0:NT, :].rearrange("p (h d) -> p h d", h=4)
            op("dve", lambda e: e.tensor_tensor(out=v3(knB), in0=v3(ckk), in1=bc4(gt_[0:NT, 12:16]), op=ALU.mult), reads=["tc", "gtn"], writes=["knB"])
            op("pool", lambda e: e.tensor_tensor(out=v3(kbg), in0=v3(ckk), in1=bc4(gb[0:NT, 4:8]), op=ALU.mult), reads=["tc", "gbs"], writes=["kbg"])
            op("pool", lambda e: e.tensor_tensor(out=v3(kdec), in0=v3(ckk), in1=bc4(gb[0:NT, 8:12]), op=ALU.mult), reads=["tc", "gbs"], writes=["kdec"])
            op("dve", lambda e: e.tensor_tensor(out=v3(qnB), in0=v3(cq), in1=bc4(gb[0:NT, 12:16]), op=ALU.mult), reads=["tc", "gbs"], writes=["qnb"])
            op("pool", lambda e: e.tensor_tensor(out=v3(vbb), in0=v3(cvv), in1=bc4(gb[0:NT, 16:20]), op=ALU.mult), reads=["oa", "gbs"], writes=["vbb"])
            for src, sk, dst, dk_ in ((knB, "knB", kTB, "kTB"), (qnB, "qnb", qTB, "qTa")):
                pb, pk = bank()
                pbb = pb[:].bitcast(BF16)
                for h in range(4):
                    op("pe", lambda e, pbb=pbb, h=h, src=src: e.transpose(out=pbb[:, h * NT:(h + 1) * NT], in_=src[0:NT, h * 128:(h + 1) * 128], identity=identb[0:NT, 0:NT]),
                       reads=[sk, "identb"], writes=[pk], acc=(h > 0))
                op("act", lambda e, pbb=pbb, dst=dst: e.activation(out=dst[:, :, 0:NT], in_=pbb[:, 0:4 * NT].rearrange("p (h n) -> p h n", h=4), func=AF.Copy), reads=[pk], writes=[dk_])
            pkk, pkkk = bank()
            for h in range(4):
                op("pe", lambda e, pkk=pkk, h=h: e.matmul(pkk[0:NT, h * NT:(h + 1) * NT], lhsT=kTB[:, h, 0:NT], rhs=kTB[:, h, 0:NT], start=True, stop=True), reads=["kTB"], writes=[pkkk], acc=(h > 0))
            for h in range(4):
                op("dve", lambda e, pkk=pkk, h=h: e.scalar_tensor_tensor(out=Lb[0:NT, h * NT:(h + 1) * NT], in0=pkk[0:NT, h * NT:(h + 1) * NT], scalar=gt_[0:NT, h:h + 1], in1=Es[0:NT, h * NT:(h + 1) * NT], op0=ALU.mult, op1=ALU.mult),
                   reads=[pkkk, "gtb", "Es"], writes=["Lb"])
            pqk_, pqkk = bank()
            for h in range(4):
                op("pe", lambda e, h=h: e.matmul(pqk_[0:NT, h * NT:(h + 1) * NT], lhsT=qTB[:, h, 0:NT], rhs=kTB[:, h, 0:NT], start=True, stop=True), reads=["kTB", "qTa"], writes=[pqkk], acc=(h > 0))
            op("dve", lambda e: e.tensor_tensor(out=qkb[0:NT, 0:4 * NT], in0=pqk_[0:NT, 0:4 * NT], in1=Ei[0:NT, 0:4 * NT], op=ALU.mult), reads=[pqkk, "Ei"], writes=["knB"])
            pm, pmk = bank()
            pmb = pm[:].bitcast(BF16)
            for h in range(4):
                op("pe", lambda e, h=h: e.transpose(out=pmb[0:NT, h * NT:(h + 1) * NT], in_=Lb[0:NT, h * NT:(h + 1) * NT], identity=identb[0:NT, 0:NT]), reads=["Lb", "identb"], writes=[pmk], acc=(h > 0))
            op("act", lambda e: e.activation(out=Mb[0][0:NT, 0:4 * NT], in_=pmb[0:NT, 0:4 * NT], func=AF.Copy), reads=[pmk], writes=["Mb0"])
            op("dve", lambda e: e.tensor_tensor(out=Xb1[0:NT, 0:4 * NT].rearrange("p (h j) -> p h j", h=4), in0=identb[0:NT, 0:NT].unsqueeze(1).to_broadcast([NT, 4, NT]),
                                                in1=Mb[0][0:NT, 0:4 * NT].rearrange("p (h j) -> p h j", h=4), op=ALU.subtract), reads=["identb", "Mb0"], writes=["Xb"])
            pt_, ptk = bank()
            ptb = pt_[:].bitcast(BF16)
            for h in range(4):
                op("pe", lambda e, h=h: e.transpose(out=ptb[0:NT, h * NT:(h + 1) * NT], in_=qkb[0:NT, h * NT:(h + 1) * NT], identity=identb[0:NT, 0:NT]), reads=["knB", "identb"], writes=[ptk], acc=(h > 0))
            op("act", lambda e: e.activation(out=qkT[0:NT, 0:4 * NT], in_=ptb[0:NT, 0:4 * NT], func=AF.Copy), reads=[ptk], writes=["qkT"])
            J = 5 if BS == 64 else 3
            Pc, Pk_ = Lb, "Lb"
            Mc, Mk_ = Mb[0], "Mb0"
            xi = 0
            for j in range(1, J + 1):
                Pn, Pnk = Pb[j % 2], "Pb%d" % (j % 2)
                pp, ppk = bank()
                for h in range(4):
                    op("pe", lambda e, pp=pp, h=h, Mc=Mc, Pc=Pc: e.matmul(pp[0:NT, h * NT:(h + 1) * NT], lhsT=Mc[0:NT, h * NT:(h + 1) * NT], rhs=Pc[0:NT, h * NT:(h + 1) * NT], start=True, stop=True),
                       reads=[Mk_, Pk_], writes=[ppk], acc=(h > 0))
                op("act", lambda e, pp=pp, Pn=Pn: e.activation(out=Pn[0:NT, 0:4 * NT], in_=pp[0:NT, 0:4 * NT], func=AF.Copy), reads=[ppk], writes=[Pnk])
                if j < J:
                    Mn, Mnk = Mb[j % 2], "Mb%d" % (j % 2)
                    pm2, pm2k = bank()
                    for h in range(4):
                        op("pe", lambda e, pm2=pm2, h=h, Mc=Mc, Pc=Pc: e.matmul(pm2[0:NT, h * NT:(h + 1) * NT], lhsT=Pc[0:NT, h * NT:(h + 1) * NT], rhs=Mc[0:NT, h * NT:(h + 1) * NT], start=True, stop=True),
                           reads=[Mk_, Pk_], writes=[pm2k], acc=(h > 0))
                    op("dve", lambda e, pm2=pm2, Mn=Mn: e.tensor_copy(out=Mn[0:NT, 0:4 * NT], in_=pm2[0:NT, 0:4 * NT]), reads=[pm2k], writes=[Mnk])
                Xc, Xck = Xb1, "Xb"
                Xn, Xnk = Xb1, "Xb"
                px, pxk = bank()
                for h in range(4):
                    op("pe", lambda e, px=px, h=h, Pn=Pn, Xc=Xc: e.matmul(px[0:NT, h * NT:(h + 1) * NT], lhsT=Pn[0:NT, h * NT:(h + 1) * NT], rhs=Xc[0:NT, h * NT:(h + 1) * NT], start=True, stop=True),
                       reads=[Pnk, Xck], writes=[pxk], acc=(h > 0))
                op("dve", lambda e, px=px, Xc=Xc, Xn=Xn: e.tensor_tensor(out=Xn[0:NT, 0:4 * NT], in0=px[0:NT, 0:4 * NT], in1=Xc[0:NT, 0:4 * NT], op=ALU.add), reads=[pxk, Xck], writes=[Xnk])
                xi = 1 - xi
                Pc, Pk_ = Pn, Pnk
                if j < J:
                    Mc, Mk_ = Mn, Mnk
            X, Xk = Xb1, "Xb"
            pu, puk = bank()
            for h in range(4):
                op("pe", lambda e, h=h: e.matmul(pu[0:NT, h * 128:(h + 1) * 128], lhsT=X[0:NT, h * NT:(h + 1) * NT], rhs=vbb[0:NT, h * 128:(h + 1) * 128], start=True, stop=True), reads=[Xk, "vbb"], writes=[puk], acc=(h > 0))
            op("act", lambda e: e.activation(out=uu[0:NT, :], in_=pu[0:NT, 0:512], func=AF.Copy), reads=[puk], writes=["oa"])
            pw, pwk = bank()
            for h in range(4):
                op("pe", lambda e, h=h: e.matmul(pw[:, h * NT:(h + 1) * NT], lhsT=kbg[0:NT, h * 128:(h + 1) * 128], rhs=X[0:NT, h * NT:(h + 1) * NT], start=True, stop=True), reads=[Xk, "kbg"], writes=[pwk], acc=(h > 0))
            pwv = pw[:, 0:4 * NT].rearrange("p (h n) -> p h n", h=4)
            eG = gts[0:NT, 8:12]
            if not sample:
                op("act", lambda e: e.activation(out=wT[:, :, 0:NT], in_=pwv, func=AF.Copy), reads=[pwk], writes=["Mb0"])
                for b in range(2):
                    r0, r1 = b * 64, (b + 1) * 64
                    op("dve", lambda e, b=b: e.tensor_tensor(out=Sm[:], in0=Sm[:], in1=gtot_ap[:, 4 * b:4 * b + 4].unsqueeze(2).to_broadcast([128, 4, 128]), op=ALU.mult), reads=["Sm", "gtot"], writes=["Sm"])
                    pws, pwsk = bank()
                    for h in range(4):
                        op("pe", lambda e, pws=pws, h=h: e.matmul(pws[0:128, h * 128:(h + 1) * 128], lhsT=wT[:, h, 0:128], rhs=Sbf[:, h, :], start=True, stop=True), reads=["Mb0", "Sbf"], writes=[pwsk], acc=(h > 0))
                    po1, po1k = bank()
                    for h in range(4):
                        op("pe", lambda e, po1=po1, h=h: e.matmul(po1[0:128, h * 128:(h + 1) * 128], lhsT=qTB[:, h, 0:128], rhs=Sbf[:, h, :], start=True, stop=True), reads=["qTa", "Sbf"], writes=[po1k], acc=(h > 0))
                    op("dve", lambda e, pws=pws, r0=r0, r1=r1: e.tensor_tensor(out=vnew[r0:r1, :], in0=uu[r0:r1, :], in1=pws[r0:r1, 0:512], op=ALU.subtract), reads=["oa", pwsk], writes=["Lb"])
                    op("dve", lambda e, po1=po1, r0=r0, r1=r1: e.tensor_tensor(out=oacc[r0:r1, :].rearrange("p (h d) -> p h d", h=4), in0=po1[r0:r1, 0:512].rearrange("p (h d) -> p h d", h=4),
                                                                             in1=eG[r0:r1, :].unsqueeze(2).to_broadcast([64, 4, 128]), op=ALU.mult), reads=[po1k, "gtse"], writes=["Es"])
                    pds, pdsk = bank()
                    for h in range(4):
                        op("pe", lambda e, pds=pds, h=h, r0=r0, r1=r1: e.matmul(pds[:, h * 128:(h + 1) * 128], lhsT=kdec[r0:r1, h * 128:(h + 1) * 128], rhs=vnew[r0:r1, h * 128:(h + 1) * 128], start=True, stop=True), reads=["kdec", "Lb"], writes=[pdsk], acc=(h > 0))
                    op("dve", lambda e, pds=pds: e.tensor_tensor(out=Sm[:].rearrange("p h d -> p (h d)"), in0=Sm[:].rearrange("p h d -> p (h d)"), in1=pds[:, 0:512], op=ALU.add), reads=["Sm", pdsk], writes=["Sm"])
                    op("act", lambda e: e.activation(out=Sbf[:], in_=Sm[:], func=AF.Copy), reads=["Sm"], writes=["Sbf"])
                if last_prompt:
                    S.dma("sp", pbs.rearrange("h k v -> k h v"), Sm[:], reads=["Sm"], is_output=True)
            else:
                oc_ = CST2_OFF["colm"][0]
                S.dma("sp", junk[:, 0:256], cst2_d[:, oc_:oc_ + 256], writes=["junk"])
                colmb = junk[:, 0:256].rearrange("p (a b) -> p a b", a=4)
                pws, pwsk = bank()
                po1, po1k = bank()
                for s_ in range(4):
                    Ss_, Sk_ = SS[s_]
                    Ssv = Ss_[:] if s_ == 0 else Ss_
                    op("act", lambda e, Ssv=Ssv: e.activation(out=Sbf[:], in_=Ssv, func=AF.Copy), reads=[Sk_], writes=["Sbf"])
                    op("dve", lambda e, s_=s_: e.tensor_tensor(out=wTz[:], in0=pwv, in1=colmb[:, s_, :].unsqueeze(1).to_broadcast([128, 4, 64]), op=ALU.mult), reads=[pwk, "junk"], writes=["Pb0"])
                    op("pool", lambda e, s_=s_: e.tensor_tensor(out=qTz[:], in0=qTB[:, :, 0:64], in1=colmb[:, s_, :].unsqueeze(1).to_broadcast([128, 4, 64]), op=ALU.mult), reads=["qTa", "junk"], writes=["Pb0"])
                    for h in range(4):
                        op("pe", lambda e, h=h, s_=s_: e.matmul(pws[0:64, h * 128:(h + 1) * 128], lhsT=wTz[:, h, :], rhs=Sbf[:, h, :], start=(s_ == 0 and h == 0), stop=(s_ == 3 and h == 3), skip_group_check=True), reads=["Pb0", "Sbf"], writes=[pwsk], acc=not (h == 0 and s_ == 0))
                    for h in range(4):
                        op("pe", lambda e, h=h, s_=s_: e.matmul(po1[0:64, h * 128:(h + 1) * 128], lhsT=qTz[:, h, :], rhs=Sbf[:, h, :], start=(s_ == 0 and h == 0), stop=(s_ == 3 and h == 3), skip_group_check=True), reads=["Pb0", "Sbf"], writes=[po1k], acc=not (h == 0 and s_ == 0))
                op("dve", lambda e: e.tensor_tensor(out=vnew[0:64, :], in0=uu[0:64, :], in1=pws[0:64, 0:512], op=ALU.subtract), reads=["oa", pwsk], writes=["Lb"])
                op("dve", lambda e: e.tensor_tensor(out=oacc[0:64, :].rearrange("p (h d) -> p h d", h=4), in0=po1[0:64, 0:512].rearrange("p (h d) -> p h d", h=4),
                                                    in1=eG.unsqueeze(2).to_broadcast([64, 4, 128]), op=ALU.mult), reads=[po1k, "gtse"], writes=["Es"])
                for s_ in range(4):
                    op("pool", lambda e, s_=s_: e.tensor_scalar(out=kdm[:], in0=kdec[0:64, :], scalar1=C("seqm", 64)[:, s_:s_ + 1], scalar2=None, op0=ALU.mult), reads=["kdec", "cst"], writes=["Pb1"])
                    pds, pdsk = bank()
                    for h in range(4):
                        op("pe", lambda e, pds=pds, h=h: e.matmul(pds[:, h * 128:(h + 1) * 128], lhsT=kdm[:, h * 128:(h + 1) * 128], rhs=vnew[0:64, h * 128:(h + 1) * 128], start=True, stop=True), reads=["Pb1", "Lb"], writes=[pdsk], acc=(h > 0))
                    Ss_, Sk_ = SS[s_]
                    Ssv = Ss_[:] if s_ == 0 else Ss_
                    op("dve", lambda e, s_=s_, Ssv=Ssv: e.tensor_tensor(out=Ssv, in0=Ssv, in1=gtot_ap[:, 4 * s_:4 * s_ + 4].unsqueeze(2).to_broadcast([128, 4, 128]), op=ALU.mult), reads=[Sk_, "gtot"], writes=[Sk_])
                    op("dve", lambda e, pds=pds, Ssv=Ssv: e.tensor_tensor(out=Ssv, in0=Ssv, in1=pds[:, 0:512].rearrange("p (h d) -> p h d", h=4), op=ALU.add), reads=[Sk_, pdsk], writes=[Sk_])
                    S.dma("sp", sbs[s_].rearrange("h k v -> k h v"), Ssv, reads=[Sk_], is_output=True)
            po2, po2k = bank()
            for h in range(4):
                op("pe", lambda e, h=h: e.matmul(po2[0:NT, h * 128:(h + 1) * 128], lhsT=qkT[0:NT, h * NT:(h + 1) * NT], rhs=vnew[0:NT, h * 128:(h + 1) * 128], start=True, stop=True), reads=["qkT", "Lb"], writes=[po2k], acc=(h > 0))
            op("dve", lambda e: e.tensor_tensor(out=ob[0:NT, :], in0=po2[0:NT, 0:512], in1=oacc[0:NT, :], op=ALU.add), reads=[po2k, "Es"], writes=["Ei"])
            op("dve", lambda e: e.tensor_tensor(out=junk[0:NT, 0:512], in0=ob[0:NT, :], in1=ob[0:NT, :], op=ALU.mult), reads=["Ei"], writes=["junk"])
            op("dve", lambda e: e.tensor_reduce(out=gts[0:NT, 0:4], in_=junk[0:NT, 0:512].rearrange("p (h d) -> p h d", h=4), axis=AX.X, op=ALU.add), reads=["junk", "gtse"], writes=["gts"])
            rsqrt_small(gts[0:NT, 0:4], gts[0:NT, 0:4], "gts", "gts", NT, 4, 1.0 / 128, EPS)
            op("dve", lambda e: e.tensor_tensor(out=v3(ob), in0=v3(ob), in1=bc4(gts[0:NT, 0:4]), op=ALU.mult), reads=["Ei", "gts"], writes=["Ei"])
            op("dve", lambda e: e.tensor_tensor(out=v3(ob), in0=v3(ob), in1=P("onorm")[0:NT, :].unsqueeze(1).to_broadcast([NT, 4, 128]), op=ALU.mult), reads=["Ei", "par"], writes=["Ei"])
            op("dve", lambda e: e.scalar_tensor_tensor(out=cat[0:NT, 512:1024], in0=ob[0:NT, :], scalar=0.5, in1=gB[0:NT, 0:512], op0=ALU.mult, op1=ALU.mult), reads=["Ei", "gB"], writes=["b16a"])

            transpose8(cat, "b16a", catT, "xT", NT)
            for nb_ in range(2):
                pb, pk = proj_tm(Woab, nb_ * 512, 512, NT, catT, "xT")
                op("dve", lambda e, pb=pb, nb_=nb_: e.tensor_tensor(out=xt[0:NT, nb_ * 512:(nb_ + 1) * 512], in0=pb[0:NT, 0:512], in1=xt[0:NT, nb_ * 512:(nb_ + 1) * 512], op=ALU.add), reads=[pk, XT], writes=[XT])


            S.stop()
            if t + 1 < NTP:
                bank_pool[0] = [3, 4]
                recA = S.record()
                emit_A(t + 1)
                S.stop()
                S.merge(recC, recA)
            else:
                S.merge(recC)
            bank_pool[0] = list(range(8))

        def emit_E(t):
            sample = (t == NTP)
            NT = 64 if sample else 128
            BS = 16 if sample else 64
            NB = NT // BS
            tg = "S" if sample else "P"
            par_i = t % 2
            xin = xs if sample else xp[t * 128:(t + 1) * 128, :]
            yout = ys if sample else yp[t * 128:(t + 1) * 128, :]
            last_prompt = (t == NTP - 1)

            xt = xtb[t % 2]
            XT = "xt%d" % (t % 2)
            ucur, ukey = (ubS if sample else ubP), "ub"
            need_u32 = sample
            Smv = Sm[:].rearrange("p h d -> p (h d)")
            if sample:
                S.dma("sp", yc[0:60, 0:1024], spool, writes=["yc"])
                op("dve", lambda e: e.tensor_copy(out=gA[0:60, :], in_=yc[0:60, 0:512]), reads=["yc"], writes=["gA"])
                op("dve", lambda e: e.tensor_copy(out=gB[0:60, :], in_=yc[0:60, 512:1024]), reads=["yc"], writes=["gB"])
            rms_to_xT(xt, XT, NT, P("norm_c"))
            ucb, uck = ubf[par_i], "ubf%d" % par_i
            for nb_ in range(2):
                pb, pk = proj_tm(Wic, nb_ * 512, 512, NT, xT, "xT")
                op("act", lambda e, pb=pb, nb_=nb_: e.activation(out=ucb[0:NT, nb_ * 512:(nb_ + 1) * 512], in_=pb[0:NT, 0:512], func=AF.Copy), reads=[pk], writes=[uck])
                if need_u32:
                    op("dve", lambda e, pb=pb, nb_=nb_: e.tensor_copy(out=u32[0:NT, nb_ * 512:(nb_ + 1) * 512], in_=pb[0:NT, 0:512]), reads=[pk], writes=["yc"])
                if last_prompt:
                    op("dve", lambda e, pb=pb: e.tensor_copy(out=Smv[96:128, :], in_=pb[96:128, 0:512]), reads=[pk], writes=["Sm"])
                    S.dma("sp", pcp[:, nb_ * 512:(nb_ + 1) * 512], Smv[113:128, :], reads=["Sm"], is_output=True)
            g2 = (xnA[:, 0:512], xnA[:, 512:1024])
            gtmp = ((zbg, "zbg"), (zag, "zag"))
            for nb_ in range(2):
                pb, pk = proj_tm(Wic, 1024 + nb_ * 512, 512, NT, xT, "xT")
                tmp, tk = gtmp[nb_]
                op("act", lambda e, pb=pb, tmp=tmp: e.activation(out=tmp[0:NT, 0:512], in_=pb[0:NT, 0:512], func=AF.Tanh, scale=0.5), reads=[pk], writes=[tk])
                op("dve", lambda e, pb=pb, tmp=tmp, nb_=nb_: e.scalar_tensor_tensor(out=g2[nb_][0:NT, 0:512], in0=tmp[0:NT, 0:512], scalar=1.0, in1=pb[0:NT, 0:512], op0=ALU.add, op1=ALU.mult),
                   reads=[pk, tk], writes=["xnA"])
            if sample:
                for s_ in range(4):
                    S.dma("sp", scp[s_ * 15:(s_ + 1) * 15, :], u32[s_ * 16 + 1:s_ * 16 + 16, :], reads=["yc"], is_output=True)
            for half in range(2):
                pb, pk = bank()
                for gg in range(2):
                    gi = half * 2 + gg
                    cols = slice(gi * 256, (gi + 1) * 256)
                    if sample:
                        op("pe", lambda e, pb=pb, gg=gg, gi=gi, cols=cols: e.matmul(pb[0:64, gg * 256:(gg + 1) * 256], lhsT=bcS[0:64, gi, :], rhs=ucb[0:64, cols], start=True, stop=False), reads=["bands", uck], writes=[pk], acc=(gg > 0))
                        op("pe", lambda e, pb=pb, gg=gg, gi=gi, cols=cols: e.matmul(pb[0:64, gg * 256:(gg + 1) * 256], lhsT=bhS[0:60, gi, :], rhs=(gA if gi < 2 else gB)[0:60, (gi % 2) * 256:(gi % 2 + 1) * 256], start=False, stop=True), reads=["bands", "gA", "gB"], writes=[pk], acc=True)
                    else:
                        op("pe", lambda e, pb=pb, gg=gg, gi=gi, cols=cols: e.matmul(pb[0:128, gg * 256:(gg + 1) * 256], lhsT=bcP[:, gi, :], rhs=ucb[:, cols], start=True, stop=(t == 0)), reads=["bands", uck], writes=[pk], acc=(gg > 0))
                        if t > 0:
                            op("pe", lambda e, pb=pb, gg=gg, gi=gi, cols=cols: e.matmul(pb[0:128, gg * 256:(gg + 1) * 256], lhsT=bpP[:, gi, :], rhs=ubf[1 - par_i][:, cols], start=False, stop=True), reads=["bands", "ubf%d" % (1 - par_i)], writes=[pk], acc=True)
                for gg in range(2):
                    gi = half * 2 + gg
                    cols = slice(gi * 256, (gi + 1) * 256)
                    if t == 0 and not sample:
                        op("dve", lambda e, pb=pb, gg=gg, cols=cols, gi=gi: e.tensor_scalar(out=pooled[0:NT, cols], in0=pb[0:NT, gg * 256:(gg + 1) * 256], scalar1=C("icnt0")[:, gi:gi + 1], scalar2=None, op0=ALU.mult),
                           reads=[pk, "cst"], writes=["b16a"])
                        op("dve", lambda e, cols=cols, gi=gi: e.scalar_tensor_tensor(out=pooled[0:NT, cols], in0=ucb[0:NT, cols], scalar=C("ccor0")[:, gi:gi + 1], in1=pooled[0:NT, cols], op0=ALU.mult, op1=ALU.add),
                           reads=[uck, "cst", "b16a"], writes=["b16a"])
                    else:
                        op("act", lambda e, pb=pb, gg=gg, cols=cols, gi=gi: e.activation(out=pooled[0:NT, cols], in_=pb[0:NT, gg * 256:(gg + 1) * 256], func=AF.Copy, scale=1.0 / POOLW[gi]),
                           reads=[pk], writes=["b16a"])
            transpose8(pooled, "b16a", catT, "xT", NT)
            for half in range(2):
                pb, pk = bank()
                for gg in range(2):
                    gi = half * 2 + gg
                    for kk in range(2):
                        op("pe", lambda e, pb=pb, gg=gg, gi=gi, kk=kk: e.matmul(pb[0:NT, gg * 256:(gg + 1) * 256], lhsT=catT[:, 2 * gi + kk, 0:NT], rhs=Wgrp[:, gi, kk, :], start=(kk == 0), stop=(kk == 1)),
                           reads=["xT"] + WK[id(Wgrp)], writes=[pk], acc=not (gg == 0 and kk == 0))
                cols = slice(half * 512, (half + 1) * 512)
                op("dve", lambda e, pb=pb, cols=cols, half=half: e.scalar_tensor_tensor(out=mg[0:NT, cols], in0=pb[0:NT, 0:512], scalar=0.5, in1=g2[half][0:NT, 0:512], op0=ALU.mult, op1=ALU.mult), reads=[pk, "xnA"], writes=["b16a"])
            transpose8(mg, "b16a", catT, "xT", NT, P("scale_pc"))
            for nb_ in range(2):
                pb, pk = proj_tm(Woc, nb_ * 512, 512, NT, catT, "xT")
                op("dve", lambda e, pb=pb, nb_=nb_: e.tensor_tensor(out=xt[0:NT, nb_ * 512:(nb_ + 1) * 512], in0=pb[0:NT, 0:512], in1=xt[0:NT, nb_ * 512:(nb_ + 1) * 512], op=ALU.add), reads=[pk, XT], writes=[XT])
            S.dma("sp", yout, xt[0:NT, :], reads=[XT], is_output=True)

        emit_A(0)
        emit_BCD(0)
        S.nopool = False
        for t in range(1, NTP):
            bank_pool[0] = [3, 4]
            recE = S.record()
            emit_E(t - 1)
            S.stop()
            emit_BCD(t, recE)
        bank_pool[0] = list(range(8))
        emit_A(NTP)
        bank_pool[0] = [3, 4]
        recE = S.record()
        emit_E(NTP - 1)
        S.stop()
        emit_BCD(NTP, recE)
        bank_pool[0] = list(range(8))
        emit_E(NTP)
        S.emit(block)
    return nc


def _prep_weights(inp):
    w = np.asarray(inp["w_in_ab"][0], np.float32)
    a_q = w[:, 0:512].reshape(1024, 8, 64)
    perm = [g * 4 + m for m in range(4) for g in range(2)]
    a_qp = a_q[:, perm, :].reshape(1024, 512)
    cols = np.concatenate([a_qp, w[:, 512:640], w[:, 640:768], w[:, 3328:3332], w[:, 3332:3336],
                           w[:, 768:1280], w[:, 2816:3328], w[:, 1280:2816]], axis=1)
    assert cols.shape[1] == NAB

    def pcn(a):
        n = a.shape[1]
        return np.ascontiguousarray(a.reshape(8, 128, n).transpose(1, 0, 2).reshape(128, 8 * n))
    wab = pcn(cols)
    woab = pcn(np.asarray(inp["w_out_ab"][0], np.float32))
    wic = pcn(np.asarray(inp["w_in_c"][0], np.float32))
    woc = pcn(np.asarray(inp["w_out_c"][0], np.float32))
    wg = np.asarray(inp["w_grp_c"][0], np.float32)
    wgrp = np.ascontiguousarray(wg.reshape(4, 2, 128, 256).transpose(2, 0, 1, 3).reshape(128, 8 * 256))
    par = np.zeros((128, NPAR), np.float32)

    def put(name, arr):
        o, wd = PAR_OFF[name]
        par[:, o:o + wd] = np.broadcast_to(np.asarray(arr, np.float32).reshape(-1, wd) if np.asarray(arr).ndim > 1 else np.asarray(arr, np.float32)[None, :], (128, wd))
    put("gq", inp["q_norm_a"][0]); put("gk", inp["k_norm_a"][0]); put("onorm", inp["o_norm_b"][0])
    put("sinks", inp["sinks_a"][0]); put("alog", inp["a_log_b"][0]); put("dtb", inp["dt_bias_b"][0])
    o, wd = PAR_OFF["scale_pc"]
    par[:, o:o + wd] = np.asarray(inp["scale_c"][0], np.float32).reshape(8, 128).T
    cw = np.asarray(inp["conv_b"][0], np.float32)
    o, wd = PAR_OFF["convw"]
    par[:, o:o + wd] = cw.reshape(4, 12, 128).transpose(2, 1, 0).reshape(128, 48)
    o, wd = PAR_OFF["norm_ab"]
    par[:, o:o + wd] = np.asarray(inp["norm_ab"][0], np.float32).reshape(8, 128).T
    o, wd = PAR_OFF["norm_c"]
    par[:, o:o + wd] = np.asarray(inp["norm_c"][0], np.float32).reshape(8, 128).T
    return dict(wab=wab, woab=woab, wic=wic, wgrp=wgrp, woc=woc, par=par)


_CACHE = {}


def make_in_maps(inp):
    xp = np.asarray(inp["x_prompt"], np.float32)
    shared = _prep_weights(inp)
    shared["cst"], shared["cst2"], shared["cst3"] = build_consts()
    xs = np.asarray(inp["x_sample"], np.float32)
    in_maps = []
    for c in range(8):
        m = dict(shared)
        m["xp"] = np.ascontiguousarray(xp[c])
        m["xs"] = np.ascontiguousarray(xs[4 * c:4 * c + 4].reshape(64, D))
        m["ck"] = np.ascontiguousarray(np.asarray(inp["cache_a_k"], np.float32)[0, 4 * c:4 * c + 4].reshape(4, 128, 128))
        m["cv"] = np.ascontiguousarray(np.asarray(inp["cache_a_v"], np.float32)[0, 4 * c:4 * c + 4].reshape(4, 128, 128))
        m["sb"] = np.ascontiguousarray(np.asarray(inp["state_b_s"], np.float32)[0, 4 * c:4 * c + 4])
        m["sconv"] = np.ascontiguousarray(np.asarray(inp["state_b_conv"], np.float32)[0, 4 * c:4 * c + 4].reshape(12, 1536))
        m["spool"] = np.ascontiguousarray(np.asarray(inp["state_c_pool"], np.float32)[0, 4 * c:4 * c + 4].reshape(60, D))
        in_maps.append(m)
    return in_maps


def assemble(res, SEQ):
    nb = len(res)

    def cat(name, shape_per):
        return np.stack([np.asarray(r[name]).reshape(shape_per) for r in res])
    y_p = cat("yp", (SEQ, D))
    y_s = cat("ys", (4, 16, D)).reshape(4 * nb, 16, D)
    pa_k = cat("pak", (128, 2, 64))[None]
    pa_v = cat("pav", (128, 2, 64))[None]
    pb_s = cat("pbs", (4, 128, 128))[None]
    pb_c = cat("pbc", (3, 1536))[None]
    pc = cat("pcp", (15, D))[None]
    sa_k = cat("sak", (4, 16, 2, 64)).reshape(4 * nb, 16, 2, 64)[None]
    sa_v = cat("sav", (4, 16, 2, 64)).reshape(4 * nb, 16, 2, 64)[None]
    sb_s = cat("sbs", (4, 4, 128, 128)).reshape(4 * nb, 4, 128, 128)[None]
    sb_c = cat("sbc", (4, 3, 1536)).reshape(4 * nb, 3, 1536)[None]
    sc = cat("scp", (4, 15, D)).reshape(4 * nb, 15, D)[None]
    return (y_p, y_s, pa_k, pa_v, pb_s, pb_c, pc, sa_k, sa_v, sb_s, sb_c, sc)


def kernel(**inp):
    SEQ = np.asarray(inp["x_prompt"]).shape[1]
    NTP = SEQ // 128
    if NTP not in _CACHE:
        _CACHE[NTP] = build_program(NTP)
    nc = _CACHE[NTP]
    in_maps = make_in_maps(inp)
    res = run_bass_kernel_spmd(nc, in_maps, core_ids=list(range(8))).results
    return assemble(res, SEQ)
```

```python
import numpy as np
from contextlib import ExitStack
import concourse.bass as bass
import concourse.mybir as mybir
from concourse.bass_utils import run_bass_kernel_spmd

F32 = mybir.dt.float32
BF16 = mybir.dt.bfloat16
AF = mybir.ActivationFunctionType
ALU = mybir.AluOpType
AX = mybir.AxisListType

D = 1024
NEG = -30000.0
EPS = 1e-6
CQ, CKV, CAG, CBG, CBQ = 0, 512, 776, 1288, 1800
NAB = 3336


class Sched:
    ENGS = ("pe", "act", "dve", "pool", "sp")

    def __init__(self, nc, es, n_dma_sems=24):
        self.nc = nc
        self.ops = {e: [] for e in self.ENGS}
        self.sem = {e: es.enter_context(nc.semaphore("s_" + e)) for e in ("pe", "act", "dve", "pool")}
        self.cnt = {e: 0 for e in ("pe", "act", "dve", "pool")}
        self.dsem = [es.enter_context(nc.semaphore("s_dma%d" % i)) for i in range(n_dma_sems)]
        self.dval = [0] * n_dma_sems
        self.dpool = {"sp": list(range(0, 12)), "act": list(range(12, 16)), "pool": list(range(16, n_dma_sems))}
        self.dnext = {"sp": 0, "act": 0, "pool": 0}
        self.seen = {e: {} for e in self.ENGS}
        self.lastw = {}
        self.readers = {}
        self.out_tokens = []
        self.expand = {}

    def _x(self, keys):
        out = []
        for k in keys:
            out.extend(self.expand.get(k, (k,)))
        return out

    def _deps(self, eng, reads, writes):
        toks = []
        for k in reads:
            w = self.lastw.get(k)
            if w is not None:
                toks.append(w)
        for k in writes:
            w = self.lastw.get(k)
            if w is not None:
                toks.append(w)
            toks.extend(self.readers.get(k, ()))
        best = {}
        for (s, v) in toks:
            if best.get(id(s), (None, -1))[1] < v:
                best[id(s)] = (s, v)
        waits = []
        seen = self.seen[eng]
        for sid, (s, v) in best.items():
            if seen.get(sid, -1) >= v:
                continue
            seen[sid] = v
            waits.append((s, v))
        return waits

    def _commit(self, tok, reads, writes):
        for k in reads:
            self.readers.setdefault(k, []).append(tok)
        for k in writes:
            self.lastw[k] = tok
            self.readers[k] = []

    @staticmethod
    def _split(reads, writes):
        r, w = [], list(writes)
        for k in reads:
            if k.startswith("ps") and k not in w:
                w.append(k)
            elif not k.startswith("ps"):
                r.append(k)
        return r, w

    def record(self):
        self.rec = []
        return self.rec

    def stop(self):
        self.rec = None

    COST = {"pe": 0.16, "act": 0.7, "dve": 0.6, "pool": 1.0, "sp": 3.0}
    HOP = 0.3
    SLACK = 0.5

    def merge(self, *streams):
        self.rec = None
        pos = [0] * len(streams)
        eng_free = {}
        wtime, rtime = {}, {}

        def keys_of(item):
            kind, args, kw = item
            if kind == "op":
                eng, fn, reads, writes = args
                is_dma = False
            else:
                eng = args[0]
                reads, writes = kw["reads"], kw["writes"]
                is_dma = True
            r, w = self._split(self._x(reads), self._x(writes))
            return eng, r, w, is_dma

        def start_time(item):
            eng, r, w, is_dma = keys_of(item)
            t = 0.0
            for k in r:
                t = max(t, wtime.get(k, 0.0))
            for k in w:
                t = max(t, wtime.get(k, 0.0), rtime.get(k, 0.0))
            return max(eng_free.get(eng, 0.0), t + self.HOP)

        exposed = []
        for st in streams:
            cnt, written = {}, set()
            for it in st:
                _, r_, w_, _ = keys_of(it)
                for k in r_:
                    if k not in written:
                        cnt[k] = cnt.get(k, 0) + 1
                written.update(w_)
            exposed.append(cnt)
        wrote = [set() for _ in streams]

        while True:
            cand = []
            for i, st in enumerate(streams):
                if pos[i] < len(st):
                    cand.append((start_time(st[pos[i]]), i))
            if not cand:
                break
            tmin = min(c[0] for c in cand)
            best = None
            for t0_, i in cand:
                if t0_ <= tmin + self.SLACK:
                    best, bi = (t0_, i), i
                    break
            item = streams[bi][pos[bi]]
            eng, r, w, is_dma = keys_of(item)
            blocked = [k for k in w if not k.startswith("ps") and any(exposed[j].get(k, 0) > 0 for j in range(len(streams)) if j != bi)]
            if blocked:
                alt = [j for j in range(len(streams)) if j != bi and pos[j] < len(streams[j]) and any(exposed[j].get(k, 0) > 0 for k in blocked)]
                assert alt, ("merge hazard", blocked)
                bi = alt[0]
                item = streams[bi][pos[bi]]
                eng, r, w, is_dma = keys_of(item)
                best = (start_time(item), bi)
            pos[bi] += 1
            for k in r:
                if k not in wrote[bi] and exposed[bi].get(k, 0) > 0:
                    exposed[bi][k] -= 1
            wrote[bi].update(w)
            t0 = best[0]
            if is_dma:
                fin = t0 + 3.0
                eng_free[eng] = t0 + 0.1
            else:
                fin = t0 + self.COST.get(eng, 0.5)
                eng_free[eng] = fin
            for k in r:
                rtime[k] = max(rtime.get(k, 0.0), fin)
            for k in w:
                wtime[k] = fin
                rtime[k] = 0.0
            kind, args, kw = item
            if kind == "op":
                self.op(*args, **kw)
            else:
                self.dma(*args, **kw)

    def op(self, eng, fn, reads=(), writes=(), acc=False):
        if eng == "pool" and getattr(self, "nopool", False):
            eng = "dve"
        if getattr(self, "rec", None) is not None:
            self.rec.append(("op", (eng, fn, list(reads), list(writes)), {"acc": acc}))
            return None
        reads, writes = self._split(self._x(reads), self._x(writes))
        if acc:
            waits = self._deps(eng, reads, [k for k in writes if not k.startswith("ps")])
        else:
            waits = self._deps(eng, reads, writes)
        self.cnt[eng] += 1
        tok = (self.sem[eng], self.cnt[eng])
        self.ops[eng].append((fn, waits, tok, 1))
        self._commit(tok, reads, writes)
        return tok

    def dma(self, q, out, in_, reads=(), writes=(), is_output=False, **kw):
        if getattr(self, "rec", None) is not None:
            kw2 = dict(kw)
            kw2.update(reads=list(reads), writes=list(writes), is_output=is_output)
            self.rec.append(("dma", (q, out, in_), kw2))
            return None
        reads = self._x(reads)
        writes = self._x(writes)
        pl = self.dpool[q]
        i = pl[self.dnext[q] % len(pl)]
        self.dnext[q] += 1
        s = self.dsem[i]
        waits = self._deps(q, reads, writes)
        prev = self.dval[i]
        if prev > 0 and self.seen[q].get(id(s), -1) < prev:
            self.seen[q][id(s)] = prev
            waits.append((s, prev))
        self.dval[i] += 16
        tok = (s, self.dval[i])

        def fn(e, out=out, in_=in_, kw=kw):
            return e.dma_start(out=out, in_=in_, **kw)
        self.ops[q].append((fn, waits, tok, 16))
        self._commit(tok, reads, writes)
        if is_output:
            self.out_tokens.append(tok)
        return tok

    def emit(self, block):
        sched = self

        def run(engname):
            def body(e):
                for (fn, waits, tok, inc) in sched.ops[engname]:
                    for (s, v) in waits:
                        e.wait_ge(s, v)
                    ins = fn(e)
                    ins.then_inc(tok[0], inc)
                if engname == "sp":
                    best = {}
                    for (s, v) in sched.out_tokens:
                        if best.get(id(s), (None, -1))[1] < v:
                            best[id(s)] = (s, v)
                    for (s, v) in best.values():
                        e.wait_ge(s, v)
                    for en in ("pe", "act", "dve", "pool"):
                        if sched.cnt[en] > 0:
                            e.wait_ge(sched.sem[en], sched.cnt[en])
            return body
        block.tensor(run("pe"))
        block.scalar(run("act"))
        block.vector(run("dve"))
        block.gpsimd(run("pool"))
        block.sync(run("sp"))


def _layout(items):
    off = {}
    o = 0
    for name, w in items:
        off[name] = (o, w)
        o += w
    return off, o


CST_ITEMS = [
    ("ident", 128),
    ("mS_P", 128), ("tri_P", 128), ("blk_P", 128), ("cm_P", 2 * 128), ("padP", 64),
    ("seqm", 4), ("icnt0", 4), ("ccor0", 4),
]
CSTS_ITEMS = [("mS_S", 64), ("tri_S", 64), ("blk_S", 64), ("cm_S", 4 * 128)]
CST2_ITEMS = [
    ("colm", 4 * 64), ("dist_P", 2 * 128), ("am_P", 2 * 128), ("bc_S", 4 * 64),
    ("bc_P", 4 * 128), ("bp_P", 4 * 128),
    ("bh_S", 4 * 64), ("dist_S1", 64), ("am_S1", 64),
]
CST3_ITEMS = [("dist_SC", 4 * 64), ("am_SC", 4 * 64)] + CSTS_ITEMS
CST_OFF, NCST = _layout(CST_ITEMS)
CST2_OFF, NCST2 = _layout(CST2_ITEMS)
CST3_OFF, NCST3 = _layout(CST3_ITEMS)
CSTS_OFF, NCSTS = _layout(CSTS_ITEMS)
assert NCSTS == 704
POOLW = (2, 4, 8, 16)


def build_consts():
    c = np.zeros((128, NCST), np.float32)
    c2 = np.zeros((128, NCST2), np.float32)
    c3 = np.zeros((128, NCST3), np.float32)

    def put(name, arr):
        if name in CST_OFF:
            o, w = CST_OFF[name]
            dst = c
        elif name in CST3_OFF:
            o, w = CST3_OFF[name]
            dst = c3
        else:
            o, w = CST2_OFF[name]
            dst = c2
        a = np.asarray(arr, np.float32).reshape(arr.shape[0], -1)
        assert a.shape[1] == w, (name, a.shape, w)
        dst[:a.shape[0], o:o + w] = a
    p = np.arange(128)
    put("ident", np.eye(128))
    for tag, n, bs in (("P", 128, 64), ("S", 64, 16)):
        i = np.arange(n)
        same = (i[:, None] // bs) == (i[None, :] // bs)
        put("mS_" + tag, np.where(same & (i[None, :] < i[:, None]), 0.0, NEG))
        put("tri_" + tag, (same & (i[:, None] <= i[None, :])).astype(np.float32))
        put("blk_" + tag, same.astype(np.float32))
        nb = n // bs
        cm = np.zeros((n, nb, 128), np.float32)
        for b in range(nb):
            cm[b * bs:(b + 1) * bs, b, :] = 1.0
        put("cm_" + tag, cm)
    sm = np.zeros((64, 4), np.float32)
    for s in range(4):
        sm[s * 16:(s + 1) * 16, s] = 1.0
    put("seqm", sm)
    colm = np.zeros((128, 4, 64), np.float32)
    for s in range(4):
        colm[:, s, s * 16:(s + 1) * 16] = 1.0
    put("colm", colm)
    q = np.arange(128)
    dist = np.zeros((128, 2, 128), np.float32)
    am = np.zeros((128, 2, 128), np.float32)
    for kb in range(2):
        j = 128 * kb + p
        dist[:, kb, :] = np.abs(128 + q[None, :] - j[:, None])
        kc = j[:, None] // 64
        qc = q[None, :] // 64
        am[:, kb, :] = np.where((kc >= qc) & (kc <= qc + 2), 0.0, NEG)
    put("dist_P", dist)
    put("am_P", am)
    qs = np.arange(64)
    d = np.zeros((128, 4, 64), np.float32)
    a = np.zeros((128, 4, 64), np.float32)
    for s in range(4):
        i = qs - 16 * s
        d[:, s, :] = np.abs(128 + i[None, :] - p[:, None])
        a[:, s, :] = np.where((qs[None, :] // 16) == s, 0.0, NEG)
    put("dist_SC", d)
    put("am_SC", a)
    k64 = np.arange(64)
    put("dist_S1", np.abs(k64[:, None] % 16 - qs[None, :] % 16).astype(np.float32))
    put("am_S1", np.where((k64[:, None] // 16) == (qs[None, :] // 16), 0.0, NEG))
    t = np.arange(128)
    bc = np.zeros((128, 4, 128), np.float32)
    bp = np.zeros((128, 4, 128), np.float32)
    for gi, w in enumerate(POOLW):
        dd = t[None, :] - t[:, None]
        bc[:, gi, :] = ((dd >= 0) & (dd < w)) - w * np.eye(128)
        dd2 = t[None, :] - (t[:, None] - 128)
        bp[:, gi, :] = (dd2 < w)
    put("bc_P", bc)
    put("bp_P", bp)
    ts = np.arange(64)
    bcs = np.zeros((64, 4, 64), np.float32)
    bhs = np.zeros((60, 4, 64), np.float32)
    r = np.arange(60)
    for gi, w in enumerate(POOLW):
        same = (ts[:, None] // 16) == (ts[None, :] // 16)
        dd = ts[None, :] - ts[:, None]
        bcs[:, gi, :] = (same & (dd >= 0) & (dd < w)) - w * np.eye(64)
        sh = r[:, None] // 15
        rr = r[:, None] % 15
        i = ts[None, :] % 16
        bhs[:, gi, :] = (sh == (ts[None, :] // 16)) & ((i + 15 - rr) < w)
    put("bc_S", bcs)
    put("bh_S", bhs)
    ic = np.zeros((128, 4), np.float32)
    for gi, w in enumerate(POOLW):
        ic[:, gi] = 1.0 / np.minimum(t + 1, w)
    put("icnt0", ic)
    cc0 = np.zeros((128, 4), np.float32)
    for gi, w in enumerate(POOLW):
        cc0[:, gi] = w / np.minimum(t + 1, w) - 1.0
    put("ccor0", cc0)
    return c, c2, c3


PAR_ITEMS = [("gq", 64), ("gk", 64), ("onorm", 128), ("sinks", 8), ("alog", 4), ("dtb", 4),
             ("scale_pc", 8), ("convw", 48), ("norm_ab", 8), ("norm_c", 8)]
PAR_OFF, NPAR = _layout(PAR_ITEMS)


def build_program(NTP=16):
    nc = bass.Bass("TRN2", target_bir_lowering=False)
    SEQ = NTP * 128

    def din(name, shape):
        return nc.dram_tensor(name, list(shape), F32, kind="ExternalInput").ap()

    def dout(name, shape):
        return nc.dram_tensor(name, list(shape), F32, kind="ExternalOutput").ap()
    xp = din("xp", [SEQ, D]); xs = din("xs", [64, D])
    ck = din("ck", [4, 128, 128]); cv = din("cv", [4, 128, 128])
    sb = din("sb", [4, 4, 128, 128]); sconv = din("sconv", [12, 1536]); spool = din("spool", [60, D])
    wab = din("wab", [128, 8 * NAB]); woab = din("woab", [128, 8 * 1024])
    wic = din("wic", [128, 8 * 2048]); wgrp = din("wgrp", [128, 8 * 256]); woc = din("woc", [128, 8 * 1024])
    par_d = din("par", [128, NPAR]); cst_d = din("cst", [128, NCST]); cst2_d = din("cst2", [128, NCST2]); cst3_d = din("cst3", [128, NCST3])
    yp = dout("yp", [SEQ, D]); ys = dout("ys", [64, D])
    pak = dout("pak", [128, 128]); pav = dout("pav", [128, 128])
    pbs = dout("pbs", [4, 128, 128]); pbc = dout("pbc", [3, 1536]); pcp = dout("pcp", [15, D])
    sak = dout("sak", [64, 128]); sav = dout("sav", [64, 128])
    sbs = dout("sbs", [4, 4, 128, 128]); sbc = dout("sbc", [12, 1536]); scp = dout("scp", [60, D])

    with ExitStack() as es:
        S = Sched(nc, es)
        S.expand = {"yc": ["yc"] + ["yc%d" % i for i in range(6)], "tc": ["tc", "tcp"], "junk": ["junkA", "junkB0", "junkB1"], "junkB": ["junkB0", "junkB1"]}

        def T(name, shape, dt=F32):
            return es.enter_context(nc.sbuf_tensor("t_" + name, list(shape), dt))
        banks = [es.enter_context(nc.psum_tensor("psb%d" % i, [128, 512], F32)) for i in range(8)]
        bank_pool = [list(range(8))]
        bank_i = {}

        def bank():
            pl = bank_pool[0]
            k = tuple(pl)
            j = bank_i.get(k, 0)
            bank_i[k] = (j + 1) % len(pl)
            i = pl[j]
            return banks[i], "ps%d" % i

        cst = T("cst", [128, NCST]); par = T("par", [128, NPAR])
        yc = T("yc", [128, 1024]); tc_ = T("tc", [128, 1024]); junk = T("junk", [128, 1024])
        Wab = T("Wab", [128, 8, NAB], BF16); Woab = T("Woab", [128, 8, 1024], BF16)
        Wic = T("Wic", [128, 8, 2048], BF16); Wgrp = T("Wgrp", [128, 4, 2, 256], BF16)
        Woc = T("Woc", [128, 8, 1024], BF16)
        identb = T("identb", [128, 128], BF16)
        onesf = T("onesf", [128, 1])
        biasP = T("biasP", [128, 2, 8, 128], BF16); biasS1 = T("biasS1", [64, 8, 64], BF16)
        biasSC = biasP[:].rearrange("p a h q -> p (a h q)").rearrange("p (s h q) -> p s h q", s=4, h=8)
        esink = T("esink", [128, 8]); nega = T("nega", [128, 4]); mhalf = T("mhalf", [128, 16]); cneg = T("cneg", [128, 2])
        bands = T("bands", [128, 4 * 128 * 2 + 4 * 64 * 2], BF16)

        def C(name, rows=128):
            if name in CSTS_OFF:
                o, w = CSTS_OFF[name]
                o += CST_OFF["mS_P"][0]
            else:
                o, w = CST_OFF[name]
            return cst[0:rows, o:o + w]

        def C2(name, rows=128):
            o, w = CST2_OFF[name]
            t_ = (yc, tc_, junk)[o // 1024]
            assert (o + w - 1) // 1024 == o // 1024
            return t_[0:rows, o % 1024:o % 1024 + w]

        def P(name):
            o, w = PAR_OFF[name]
            return par[:, o:o + w]
        ident = C("ident")

        block = es.enter_context(nc.Block())
        op = S.op

        S.dma("sp", cst[:], cst_d, writes=["cst"])
        S.dma("sp", yc[:, 0:1024], cst2_d[:, 0:1024], writes=["yc"])
        S.dma("sp", tc_[:, 0:1024], cst2_d[:, 1024:2048], writes=["tc"])
        S.dma("sp", junk[:, 0:NCST2 - 2048], cst2_d[:, 2048:NCST2], writes=["junk"])
        S.dma("sp", par[:], par_d, writes=["par"])
        op("pool", lambda e: e.memset(onesf[:], 1.0), writes=["onesf"])
        op("pool", lambda e: e.memset(mhalf[:], -0.5), writes=["mhalf"])
        op("dve", lambda e: e.tensor_copy(out=identb[:], in_=ident), reads=["cst"], writes=["identb"])
        for bi_, nm in enumerate(("bc_P", "bp_P", "bc_S", "bh_S")):
            bo = (0, 512, 1024, 1280)[bi_]
            bw = CST2_OFF[nm][1]
            op("dve", lambda e, nm=nm, bo=bo, bw=bw: e.tensor_copy(out=bands[:, bo:bo + bw], in_=C2(nm)), reads=["yc", "tc", "junk"], writes=["bands"])

        bcP = bands[:, 0:512].rearrange("p (g t) -> p g t", g=4)
        bpP = bands[:, 512:1024].rearrange("p (g t) -> p g t", g=4)
        bcS = bands[:, 1024:1280].rearrange("p (g t) -> p g t", g=4)
        bhS = bands[:, 1280:1536].rearrange("p (g t) -> p g t", g=4)
        distP = C2("dist_P").rearrange("p (k q) -> p k q", k=2); amP = C2("am_P").rearrange("p (k q) -> p k q", k=2)
        distSC = junk[:, 0:256].rearrange("p (s q) -> p s q", s=4); amSC = junk[:, 256:512].rearrange("p (s q) -> p s q", s=4)
        for h in range(8):
            sl = -(2.0 ** (-(h + 1)))
            op("dve", lambda e, h=h, sl=sl: e.scalar_tensor_tensor(out=biasP[:, :, h, :], in0=distP, scalar=sl, in1=amP, op0=ALU.mult, op1=ALU.add), reads=["yc", "tc"], writes=["biasP"])
            op("dve", lambda e, h=h, sl=sl: e.scalar_tensor_tensor(out=biasS1[:, h, :], in0=C2("dist_S1", 64), scalar=sl, in1=C2("am_S1", 64), op0=ALU.mult, op1=ALU.add), reads=["yc", "tc", "junk"], writes=["biasS1"])
        op("dve", lambda e: e.tensor_reduce(out=cneg[:, 0:1], in_=P("gq"), axis=AX.X, op=ALU.max, apply_absolute_value=True), reads=["par"], writes=["cneg"])
        op("dve", lambda e: e.tensor_reduce(out=cneg[:, 1:2], in_=P("gk"), axis=AX.X, op=ALU.max, apply_absolute_value=True), reads=["par"], writes=["cneg"])
        op("dve", lambda e: e.scalar_tensor_tensor(out=cneg[:, 0:1], in0=cneg[:, 0:1], scalar=-8.0, in1=cneg[:, 1:2], op0=ALU.mult, op1=ALU.mult), reads=["cneg"], writes=["cneg"])
        op("act", lambda e: e.activation(out=esink[:], in_=P("sinks"), func=AF.Exp, bias=cneg[:, 0:1]), reads=["par", "cneg"], writes=["esink"])
        op("act", lambda e: e.activation(out=nega[:], in_=P("alog"), func=AF.Exp), reads=["par"], writes=["nega"])
        op("dve", lambda e: e.tensor_scalar(out=nega[:], in0=nega[:], scalar1=-1.0, scalar2=None, op0=ALU.mult), reads=["nega"], writes=["nega"])

        def load_weight(src, dst3, N, wname):
            keys = []
            for c in range(8):
                wk = "%s_%d" % (wname, c)
                keys.append(wk)
                S.dma("pool", dst3[:, c, :], src[:, c * N:(c + 1) * N], writes=[wk])
            return keys
        WK = {}
        WKB = {}
        for (c0_, n_) in ((CQ, 512), (CKV, 264), (CAG, 512), (CBG, 512), (CBQ, 768), (CBQ + 768, 768)):
            ks = []
            for c in range(8):
                wk = "Wab_%d_%d" % (c0_, c)
                ks.append(wk)
                S.dma("pool", Wab[:, c, c0_:c0_ + n_], wab[:, c * NAB + c0_:c * NAB + c0_ + n_], writes=[wk])
            WKB[c0_] = ks
        WK[id(Wab)] = [k for ks in WKB.values() for k in ks]
        WK[id(Woab)] = load_weight(woab, Woab, 1024, "Woab")
        WK[id(Wic)] = load_weight(wic, Wic, 2048, "Wic")
        WK[id(Wgrp)] = load_weight(wgrp, Wgrp[:].rearrange("p a b n -> p (a b) n"), 256, "Wgrp")
        WK[id(Woc)] = load_weight(woc, Woc, 1024, "Woc")
        S.nopool = True

        xtb = [T("xt0", [128, D]), T("xt1", [128, D])]
        xTA = T("xTA", [128, 8, 128], BF16)
        st8 = T("st8", [128, 32])
        b16a = T("b16a", [128, D], BF16)
        xn = cat = pooled = mg = b16a
        xT = T("xT", [128, 8, 128], BF16); catT = xT
        zq = T("zq", [128, 512]); zkv = T("zkv", [128, 264]); zag = T("zag", [128, 512]); zbg = T("zbg", [128, 512])
        ubt = T("ub", [128, 12 * 131]); uhist = T("uhist", [128, 12, 3])
        ubP = ubt[:].rearrange("p (c l) -> p c l", c=12)
        ubS = ubt[:, 0:12 * 4 * 19].rearrange("p (c s l) -> p c s l", c=12, s=4)
        qnb = T("qnb", [128, 512], BF16); kn32 = T("kn32", [128, 128]); knb = T("knb", [128, 128], BF16)
        qTa = T("qTa", [128, 4, 128], BF16)
        kTa = [T("kTa%d" % i, [128, 128], BF16) for i in range(2)]
        vaug = [T("vaug%d" % i, [128, 2, 65], BF16) for i in range(2)]
        kTc = T("kTc", [128, 4, 128], BF16); vcaug = T("vcaug", [128, 4, 2, 65], BF16)
        Es = T("Es", [128, 512]); sc_e = oacc = Es
        Ei = T("Ei", [128, 512]); ob = Ei
        oa = T("oa", [128, 512]); cvv = uu = oa
        cq = tc_[:, 0:512]; ckk = tc_[:, 512:1024]
        gt_ = T("gt", [128, 16]); gb = T("gb", [128, 32]); gts = T("gts", [128, 16]); gtot_t = T("gtot_t", [128, 16])
        Lb = T("Lb", [128, 512], BF16); Mb = [T("Mb%d" % i, [128, 512], BF16) for i in range(2)]
        Pb = [T("Pb%d" % i, [128, 512], BF16) for i in range(2)]; Xb1 = T("Xb", [128, 512], BF16)
        knB = T("knB", [128, 512], BF16); qnB = qnb
        kbg = T("kbg", [128, 512], BF16); kdec = T("kdec", [128, 512], BF16); vbb = T("vbb", [128, 512], BF16)
        kdm = Pb[1][0:64, :]
        kTB = T("kTB", [128, 4, 128], BF16); qTB = qTa
        wT = Mb[0][:].rearrange("p (h n) -> p h n", h=4)
        wTz = Pb[0][:, 0:256].rearrange("p (h n) -> p h n", h=4); qTz = Pb[0][:, 256:512].rearrange("p (h n) -> p h n", h=4)
        qkb = knB; qkT = T("qkT", [128, 512], BF16)
        vnew = Lb
        pT01 = Mb
        Sm = T("Sm", [128, 4, 128]); Sbf = T("Sbf", [128, 4, 128], BF16)
        u32 = yc; gate = tc_
        ubf = [T("ubf%d" % i, [128, 1024], BF16) for i in range(2)]
        hnew = Es[:, 0:144].rearrange("p (c r) -> p c r", c=12)
        gA = T("gA", [128, 512], BF16); gB = T("gB", [128, 512], BF16)
        xnA = T("xnA", [128, 1024], BF16)

        def rms_to_xT(src, srckey, NT, gain, early=False):
            if early:
                c0, tagk = 28, "A"
                xb, xk = xnA, "xnA"
                dst, dstkey = xTA, "xTA"
            else:
                c0, tagk = 0, ""
                xb, xk = xn, "b16a"
                dst, dstkey = xT, "xT"
            op("act", lambda e: e.activation(out=xb[0:NT, :], in_=src[0:NT, :], func=AF.Square, accum_out=st8[0:NT, c0:c0 + 1]),
               reads=[srckey], writes=[xk, "st8a" + tagk])
            if getattr(S, "nopool", False):
                op("act", lambda e: e.activation(out=st8[0:NT, c0 + 1:c0 + 2], in_=st8[0:NT, c0:c0 + 1], func=AF.Sqrt, scale=1.0 / D, bias=EPS),
                   reads=["st8a" + tagk], writes=["st8b" + tagk])
                op("dve", lambda e: e.reciprocal(out=st8[0:NT, c0 + 2:c0 + 3], in_=st8[0:NT, c0 + 1:c0 + 2]),
                   reads=["st8b" + tagk], writes=["st8c" + tagk])
            else:
                op("pool", lambda e: e.tensor_scalar(out=st8[0:NT, c0 + 1:c0 + 2], in0=st8[0:NT, c0:c0 + 1], scalar1=1.0 / D, scalar2=EPS, op0=ALU.mult, op1=ALU.add),
                   reads=["st8a" + tagk], writes=["st8b" + tagk])
                op("pool", lambda e: e.tensor_tensor(out=st8[0:NT, c0 + 2:c0 + 3], in0=st8[0:NT, c0 + 1:c0 + 2], in1=mhalf[0:NT, 0:1], op=ALU.pow),
                   reads=["st8b" + tagk, "mhalf"], writes=["st8c" + tagk])
            op("dve", lambda e: e.tensor_scalar(out=xb[0:NT, :], in0=src[0:NT, :], scalar1=st8[0:NT, c0 + 2:c0 + 3], scalar2=None, op0=ALU.mult),
               reads=[srckey, "st8c" + tagk], writes=[xk])
            transpose8(xb, xk, dst, dstkey, NT, gain)

        def transpose8(src, srckey, dst, dstkey, NT, gain=None, chunk=None):
            pb, pk = bank()
            pbb = pb[:].bitcast(BF16)
            if chunk is None:
                chunk = lambda c: src[0:NT, c * 128:(c + 1) * 128]
            srckeys = list(srckey) if isinstance(srckey, (list, tuple)) else [srckey]
            for c in range(8):
                op("pe", lambda e, c=c: e.transpose(out=pbb[:, c * NT:(c + 1) * NT], in_=chunk(c), identity=identb[0:NT, 0:NT]),
                   reads=srckeys + ["identb"], writes=[pk], acc=(c > 0))
            if gain is None:
                op("act", lambda e: e.activation(out=dst[:, :, 0:NT], in_=pbb[:, 0:8 * NT].rearrange("p (c n) -> p c n", c=8), func=AF.Copy),
                   reads=[pk], writes=[dstkey])
            else:
                op("dve", lambda e: e.tensor_tensor(out=dst[:, :, 0:NT], in0=pbb[:, 0:8 * NT].rearrange("p (c n) -> p c n", c=8),
                                                    in1=gain.unsqueeze(2).to_broadcast([128, 8, NT]), op=ALU.mult),
                   reads=[pk, "par"], writes=[dstkey])

        def proj_tm(W, c0, ncols, NT, lhs, lhskey):
            pb, pk = bank()
            for k in range(8):
                op("pe", lambda e, k=k: e.matmul(pb[0:NT, 0:ncols], lhsT=lhs[:, k, 0:NT], rhs=W[:, k, c0:c0 + ncols], start=(k == 0), stop=(k == 7)),
                   reads=[lhskey] + (WKB[c0] if (W is Wab and c0 in WKB) else WK[id(W)]), writes=[pk], acc=(k > 0))
            return pb, pk

        def silu2(dst, z, zkey, tmp, tmpkey, dstkey, NT, n):
            op("act", lambda e: e.activation(out=tmp[0:NT, 0:n], in_=z[0:NT, 0:n], func=AF.Tanh, scale=0.5), reads=[zkey], writes=[tmpkey])
            op("dve", lambda e: e.scalar_tensor_tensor(out=dst[0:NT, 0:n], in0=tmp[0:NT, 0:n], scalar=1.0, in1=z[0:NT, 0:n], op0=ALU.add, op1=ALU.mult),
               reads=[tmpkey, zkey], writes=[dstkey])

        def rsqrt_small(dst, src, key_src, key_dst, NT, n, mul, eps):
            if getattr(S, "nopool", False):
                op("act", lambda e: e.activation(out=dst, in_=src, func=AF.Sqrt, scale=mul, bias=eps), reads=[key_src], writes=[key_dst])
                op("dve", lambda e: e.reciprocal(out=dst, in_=dst), reads=[key_dst], writes=[key_dst])
                return
            op("pool", lambda e: e.tensor_scalar(out=dst, in0=src, scalar1=mul, scalar2=eps, op0=ALU.mult, op1=ALU.add), reads=[key_src], writes=[key_dst])
            op("pool", lambda e: e.tensor_tensor(out=dst, in0=dst, in1=mhalf[0:NT, 0:n], op=ALU.pow), reads=[key_dst, "mhalf"], writes=[key_dst])

        def sample_prologue():
            ckf = yc[:, 0:512].rearrange("p (s c) -> p s c", s=4)
            cvf = tc_[:, 0:512].rearrange("p (s c) -> p s c", s=4)
            ckb = kbg[:, 0:512].rearrange("p (s c) -> p s c", s=4)
            S.dma("sp", ckf, ck.rearrange("s k c -> k s c"), writes=["yc"])
            S.dma("sp", cvf, cv.rearrange("s k c -> k s c"), writes=["tc"])
            op("dve", lambda e: e.tensor_copy(out=ckb, in_=ckf), reads=["yc"], writes=["kbg"])
            pb, pk = bank()
            pbb = pb[:].bitcast(BF16)
            for s_ in range(4):
                op("pe", lambda e, s_=s_: e.transpose(out=pbb[:, s_ * 128:(s_ + 1) * 128], in_=ckb[:, s_, :], identity=identb[:]),
                   reads=["kbg", "identb"], writes=[pk], acc=(s_ > 0))
            op("act", lambda e: e.activation(out=kTc[:].rearrange("p s k -> p (s k)"), in_=pbb[:, 0:512], func=AF.Copy), reads=[pk], writes=["kTc"])
            op("pool", lambda e: e.memset(vcaug[:], 1.0), writes=["vcaug"])
            op("dve", lambda e: e.tensor_copy(out=vcaug[:, :, :, 0:64], in_=cvf.rearrange("p s (g d) -> p s g d", g=2)), reads=["tc", "vcaug"], writes=["vcaug"])
            S.dma("sp", junk[:, 0:512], cst3_d[:, 0:512], writes=["junk"])
            oM = CST_OFF["mS_P"][0]
            S.dma("sp", cst[:, oM:oM + 704], cst3_d[:, 512:512 + 704], writes=["cst"])
            for h in range(8):
                sl = -(2.0 ** (-(h + 1)))
                op("dve", lambda e, h=h, sl=sl: e.scalar_tensor_tensor(out=biasSC[:, :, h, :], in0=distSC, scalar=sl, in1=amSC, op0=ALU.mult, op1=ALU.add), reads=["junk"], writes=["biasP"])
        for i in range(2):
            op("pool", lambda e, i=i: e.memset(vaug[i][:], 1.0), writes=["vaug%d" % i])
        op("pool", lambda e: e.memset(Sm[:], 0.0), writes=["Sm"])
        op("pool", lambda e: e.memset(Sbf[:], 0.0), writes=["Sbf"])
        op("pool", lambda e: e.memset(uhist[:], 0.0), writes=["uhist"])

        def emit_A(t):
            sample = (t == NTP)
            NT = 64 if sample else 128
            BS = 16 if sample else 64
            NB = NT // BS
            tg = "S" if sample else "P"
            par_i = t % 2
            xin = xs if sample else xp[t * 128:(t + 1) * 128, :]
            yout = ys if sample else yp[t * 128:(t + 1) * 128, :]
            last_prompt = (t == NTP - 1)

            xt = xtb[t % 2]
            XT = "xt%d" % (t % 2)
            if sample:
                sample_prologue()
            S.dma("act", xt[0:NT, :], xin, writes=[XT])
            rms_to_xT(xt, XT, NT, P("norm_ab"), early=True)

            pq, pqk = proj_tm(Wab, CQ, 512, NT, xTA, "xTA")
            op("act", lambda e: e.activation(out=zq[0:NT, :], in_=pq[0:NT, 0:512], func=AF.Copy), reads=[pqk], writes=["zq"])
            pkv, pkvk = proj_tm(Wab, CKV, 264, NT, xTA, "xTA")
            op("dve", lambda e: e.tensor_copy(out=zkv[0:NT, :], in_=pkv[0:NT, 0:264]), reads=[pkvk], writes=["zkv"])
            pag, pagk = proj_tm(Wab, CAG, 512, NT, xTA, "xTA")
            op("act", lambda e: e.activation(out=zag[0:NT, :], in_=pag[0:NT, 0:512], func=AF.Copy), reads=[pagk], writes=["zag"])
            pbg, pbgk = proj_tm(Wab, CBG, 512, NT, xTA, "xTA")
            op("dve", lambda e: e.tensor_copy(out=zbg[0:NT, :], in_=pbg[0:NT, 0:512]), reads=[pbgk], writes=["zbg"])

            ucur, ukey = (ubS if sample else ubP), "ub"
            if not sample:
                op("pool", lambda e: e.tensor_copy(out=ubP[:, :, 0:3], in_=uhist[:]), reads=["uhist"], writes=["ub"])
            for c4 in range(3):
                pb, pk = bank()
                for cc in range(4):
                    ch = c4 * 4 + cc
                    for k in range(8):
                        op("pe", lambda e, k=k, ch=ch, cc=cc, pb=pb: e.matmul(pb[:, cc * NT:(cc + 1) * NT], lhsT=Wab[:, k, CBQ + ch * 128:CBQ + (ch + 1) * 128], rhs=xTA[:, k, 0:NT], start=(k == 0), stop=(k == 7)),
                           reads=["xTA"] + WKB[CBQ if ch < 6 else CBQ + 768], writes=[pk], acc=not (k == 0 and cc == 0))
                if sample:
                    op("act", lambda e, c4=c4, pb=pb: e.activation(out=ubS[:, c4 * 4:(c4 + 1) * 4, :, 3:19], in_=pb[:, 0:256].rearrange("p (c s l) -> p c s l", c=4, s=4), func=AF.Copy),
                       reads=[pk], writes=[ukey])
                else:
                    op("act", lambda e, c4=c4, pb=pb: e.activation(out=ucur[:, c4 * 4:(c4 + 1) * 4, 3:131], in_=pb[:, 0:512].rearrange("p (c l) -> p c l", c=4), func=AF.Copy),
                       reads=[pk], writes=[ukey])
            if sample:
                S.dma("sp", tc_[0:12, 0:1024], sconv[:, 0:1024], writes=["tc"])
                S.dma("sp", junk[0:12, 0:512], sconv[:, 1024:1536], writes=["junk"])
                pb, pk = bank()
                for ch in range(12):
                    op("pe", lambda e, ch=ch, pb=pb: e.transpose(out=pb[:, ch * 12:(ch + 1) * 12], in_=(tc_[0:12, ch * 128:(ch + 1) * 128] if ch < 8 else junk[0:12, (ch - 8) * 128:(ch - 7) * 128]), identity=ident[0:12, 0:12]),
                       reads=["tc", "junk", "cst"], writes=[pk], acc=(ch > 0))
                op("act", lambda e, pb=pb: e.activation(out=ubS[:, :, :, 0:3], in_=pb[:, 0:144].rearrange("p (c s r) -> p c s r", c=12, s=4), func=AF.Copy),
                   reads=[pk], writes=[ukey])
            else:
                op("pool", lambda e: e.tensor_copy(out=uhist[:], in_=ubP[:, :, 128:131]), reads=["ub"], writes=["uhist"])

        def emit_BCD(t, recE=None):
            sample = (t == NTP)
            NT = 64 if sample else 128
            BS = 16 if sample else 64
            NB = NT // BS
            tg = "S" if sample else "P"
            par_i = t % 2
            xin = xs if sample else xp[t * 128:(t + 1) * 128, :]
            yout = ys if sample else yp[t * 128:(t + 1) * 128, :]
            last_prompt = (t == NTP - 1)

            xt = xtb[t % 2]
            XT = "xt%d" % (t % 2)
            ucur, ukey = (ubS if sample else ubP), "ub"
            if sample or last_prompt:
                n = 12 if sample else 3
                if sample:
                    op("pool", lambda e: e.tensor_copy(out=hnew[:].rearrange("p c (s r) -> p c s r", s=4), in_=ubS[:, :, :, 16:19]), reads=[ukey], writes=["Es"])
                else:
                    op("pool", lambda e: e.tensor_copy(out=hnew[:, :, 0:3], in_=ucur[:, :, 128:131]), reads=[ukey], writes=["Es"])
                for c4 in range(3):
                    pb, pk = bank()
                    for cc in range(4):
                        ch = c4 * 4 + cc
                        op("pe", lambda e, ch=ch, cc=cc, pb=pb, n=n: e.transpose(out=pb[0:n, cc * 128:(cc + 1) * 128], in_=hnew[:, ch, 0:n], identity=ident),
                           reads=["Es", "cst"], writes=[pk], acc=(cc > 0))
                    op("act", lambda e, pb=pb, c4=c4, n=n: e.activation(out=junk[0:n, 0:512], in_=pb[0:n, 0:512], func=AF.Copy), reads=[pk], writes=["junk"])
                    S.dma("sp", (sbc if sample else pbc)[:, c4 * 512:(c4 + 1) * 512], junk[0:n, 0:512], reads=["junk"], is_output=True)

            tmaj = ((cq, "tc"), (ckk, "tc"), (cvv, "oa"))
            tb = []
            first = [True, True, True]

            def conv_half(half, part="all"):
                cw = P("convw")
                if part == "finish":
                    return conv_finish(half)

                def views(c6):
                    ch = half * 6 + c6
                    if sample:
                        return (lambda j, ch=ch: ubS[:, ch, :, j:j + 16]), yc[:, c6 * NT:(c6 + 1) * NT].rearrange("p (s l) -> p s l", s=4)
                    return (lambda j, ch=ch: ucur[:, ch, j:j + 128]), yc[:, c6 * NT:(c6 + 1) * NT]
                for j in range(4):
                    for c6 in range(6):
                        ch = half * 6 + c6
                        uv, yv = views(c6)
                        yk = "yc%d" % c6
                        if j == 0:
                            op("dve", lambda e, uv=uv, yv=yv, ch=ch: e.tensor_scalar(out=yv, in0=uv(0), scalar1=cw[:, ch * 4:ch * 4 + 1], scalar2=None, op0=ALU.mult), reads=[ukey, "par"], writes=[yk])
                        else:
                            op("dve", lambda e, uv=uv, yv=yv, ch=ch, j=j: e.scalar_tensor_tensor(out=yv, in0=uv(j), scalar=cw[:, ch * 4 + j:ch * 4 + j + 1], in1=yv, op0=ALU.mult, op1=ALU.add), reads=[ukey, "par", yk], writes=[yk])
                if part == "taps":
                    return
                conv_finish(half)

            def conv_finish(half):
                if half == 0:
                    silu2(yc, yc, "yc", tc_, "tc", "yc", 128, 6 * NT)
                else:
                    silu2(yc, yc, "yc", junk, "junk", "yc", 128, 6 * NT)
                for c6 in range(6):
                    ch = half * 6 + c6
                    qi, h = ch // 4, ch % 4
                    pb, pk = tb[qi]
                    op("pe", lambda e, pb=pb, h=h, c6=c6: e.transpose(out=pb[0:NT, h * 128:(h + 1) * 128], in_=yc[:, c6 * NT:(c6 + 1) * NT], identity=ident),
                       reads=["yc%d" % c6, "cst"], writes=[pk], acc=not first[qi])
                    first[qi] = False
                    if h == 3:
                        dst, dk_ = tmaj[qi]
                        op("act", lambda e, pb=pb, dst=dst: e.activation(out=dst[0:NT, :], in_=pb[0:NT, 0:512], func=AF.Copy), reads=[pk], writes=[dk_])
            silu2(gA, zag, "zag", junk, "junk", "gA", NT, 512)
            silu2(gB, zbg, "zbg", junk, "junk", "gB", NT, 512)
            bank_pool[0] = [2] if recE is not None else [3, 4]
            tb.extend((banks[i], "ps%d" % i) for i in (5, 6, 7))
            recY = S.record()
            op("act", lambda e: e.activation(out=gt_[0:NT, 0:4], in_=zkv[0:NT, 256:260], func=AF.Tanh, scale=0.5), reads=["zkv"], writes=["gtb"])
            op("dve", lambda e: e.tensor_scalar(out=gt_[0:NT, 0:4], in0=gt_[0:NT, 0:4], scalar1=1.0, scalar2=0.5, op0=ALU.add, op1=ALU.mult), reads=["gtb"], writes=["gtb"])
            op("dve", lambda e: e.tensor_tensor(out=gt_[0:NT, 4:8], in0=zkv[0:NT, 260:264], in1=P("dtb")[0:NT, :], op=ALU.add), reads=["zkv", "par"], writes=["gtg"])
            op("act", lambda e: e.activation(out=gt_[0:NT, 4:8], in_=gt_[0:NT, 4:8], func=AF.Exp), reads=["gtg"], writes=["gtg"])
            op("act", lambda e: e.activation(out=gt_[0:NT, 4:8], in_=gt_[0:NT, 4:8], func=AF.Ln, bias=1.0), reads=["gtg"], writes=["gtg"])
            op("dve", lambda e: e.tensor_tensor(out=gt_[0:NT, 4:8], in0=gt_[0:NT, 4:8], in1=nega[0:NT, :], op=ALU.mult), reads=["gtg", "nega"], writes=["gtg"])
            tri = C("tri_" + tg, NT); blk = C("blk_" + tg, NT); cm = C("cm_" + tg, NT).rearrange("p (b q) -> p b q", b=NB); mS = C("mS_" + tg, NT)
            pg, pgk = bank()
            op("pe", lambda e, pg=pg: e.matmul(pg[0:NT, 0:4], lhsT=tri, rhs=gt_[0:NT, 4:8], start=True, stop=True), reads=["cst", "gtg"], writes=[pgk])
            op("pe", lambda e, pg=pg: e.matmul(pg[0:NT, 4:8], lhsT=blk, rhs=gt_[0:NT, 4:8], start=True, stop=True), reads=["cst", "gtg"], writes=[pgk], acc=True)
            for b in range(NB):
                op("pe", lambda e, pg=pg, b=b: e.matmul(pg[:, 8 + 4 * b:12 + 4 * b], lhsT=cm[:, b, :], rhs=gt_[0:NT, 4:8], start=True, stop=True), reads=["cst", "gtg"], writes=[pgk], acc=True)
            op("dve", lambda e, pg=pg: e.tensor_copy(out=gts[0:NT, 0:4], in_=pg[0:NT, 0:4]), reads=[pgk], writes=["gts"])
            op("dve", lambda e, pg=pg: e.tensor_tensor(out=gts[0:NT, 4:8], in0=pg[0:NT, 4:8], in1=gts[0:NT, 0:4], op=ALU.subtract), reads=[pgk, "gts"], writes=["gts"])
            op("act", lambda e, pg=pg: e.activation(out=gtot_t[:, 0:4 * NB], in_=pg[:, 8:8 + 4 * NB], func=AF.Exp), reads=[pgk], writes=["gtot"])
            gtot_ap = gtot_t
            op("act", lambda e: e.activation(out=gts[0:NT, 8:16], in_=gts[0:NT, 0:8], func=AF.Exp), reads=["gts"], writes=["gtse"])
            for h in range(4):
                op("dve", lambda e, h=h: e.tensor_scalar(out=Es[0:NT, h * 128:h * 128 + NT], in0=onesf[0:NT, 0:1].to_broadcast([NT, NT]), scalar1=gt_[0:NT, 4 + h:5 + h], scalar2=None, op0=ALU.mult), reads=["onesf", "gtg"], writes=["Es"])
            op("dve", lambda e: e.tensor_scalar(out=Ei[0:NT, 0:512], in0=Es[0:NT, 0:512], scalar1=-1.0, scalar2=None, op0=ALU.mult), reads=["Es"], writes=["Ei"])
            pd, pdk = bank()
            for h in range(4):
                op("pe", lambda e, pd=pd, h=h: e.matmul(pd[0:NT, h * NT:(h + 1) * NT], lhsT=tri, rhs=Es[0:NT, h * 128:h * 128 + NT], start=True, stop=False), reads=["cst", "Es"], writes=[pdk], acc=(h > 0))
                op("pe", lambda e, pd=pd, h=h: e.matmul(pd[0:NT, h * NT:(h + 1) * NT], lhsT=Ei[0:NT, h * 128:h * 128 + NT], rhs=tri, start=False, stop=True), reads=["cst", "Ei"], writes=[pdk], acc=True)
            op("dve", lambda e, pd=pd: e.tensor_tensor(out=Es[0:NT, 0:4 * NT].rearrange("p (h j) -> p h j", h=4), in0=pd[0:NT, 0:4 * NT].rearrange("p (h j) -> p h j", h=4),
                                                       in1=mS.unsqueeze(1).to_broadcast([NT, 4, NT]), op=ALU.add), reads=[pdk, "cst"], writes=["Es"])
            op("act", lambda e: e.activation(out=Es[0:NT, 0:4 * NT], in_=Es[0:NT, 0:4 * NT], func=AF.Exp), reads=["Es"], writes=["Es"])
            op("dve", lambda e: e.tensor_tensor(out=Ei[0:NT, 0:4 * NT].rearrange("p (h j) -> p h j", h=4), in0=Es[0:NT, 0:4 * NT].rearrange("p (h j) -> p h j", h=4),
                                                in1=ident[0:NT, 0:NT].unsqueeze(1).to_broadcast([NT, 4, NT]), op=ALU.add), reads=["Es", "cst"], writes=["Ei"])

            conv_half(0)
            conv_half(1, "taps")
            S.stop()
            bank_pool[0] = [0, 1] if recE is not None else [0, 1, 2]
            recX = S.record()
            op("dve", lambda e: e.tensor_tensor(out=junk[0:NT, 0:512], in0=zq[0:NT, :], in1=zq[0:NT, :], op=ALU.mult), reads=["zq"], writes=["junkA"])
            op("dve", lambda e: e.tensor_reduce(out=st8[0:NT, 4:12], in_=junk[0:NT, 0:512].rearrange("p (h d) -> p h d", h=8), axis=AX.X, op=ALU.add), reads=["junkA"], writes=["st8q"])
            op("dve", lambda e: e.tensor_tensor(out=kn32[0:NT, :], in0=zkv[0:NT, 0:128], in1=zkv[0:NT, 0:128], op=ALU.mult), reads=["zkv"], writes=["kn32"])
            op("dve", lambda e: e.tensor_reduce(out=st8[0:NT, 12:14], in_=kn32[0:NT, :].rearrange("p (h d) -> p h d", h=2), axis=AX.X, op=ALU.add), reads=["kn32"], writes=["st8q"])
            rsqrt_small(st8[0:NT, 4:14], st8[0:NT, 4:14], "st8q", "st8q", NT, 10, 1.0 / 64, EPS)
            op("dve", lambda e: e.tensor_tensor(out=junk[0:NT, 0:512].rearrange("p (h d) -> p h d", h=8), in0=zq[0:NT, :].rearrange("p (h d) -> p h d", h=8),
                                                in1=st8[0:NT, 4:12].unsqueeze(2).to_broadcast([NT, 8, 64]), op=ALU.mult), reads=["zq", "st8q"], writes=["junkA"])
            op("dve", lambda e: e.tensor_tensor(out=qnb[0:NT, :].rearrange("p (h d) -> p h d", h=8), in0=junk[0:NT, 0:512].rearrange("p (h d) -> p h d", h=8),
                                                in1=P("gq")[0:NT, :].unsqueeze(1).to_broadcast([NT, 8, 64]), op=ALU.mult), reads=["junkA", "par"], writes=["qnb"])
            op("dve", lambda e: e.tensor_tensor(out=kn32[0:NT, :].rearrange("p (h d) -> p h d", h=2), in0=zkv[0:NT, 0:128].rearrange("p (h d) -> p h d", h=2),
                                                in1=st8[0:NT, 12:14].unsqueeze(2).to_broadcast([NT, 2, 64]), op=ALU.mult), reads=["zkv", "st8q", "kn32"], writes=["kn32"])
            op("dve", lambda e: e.tensor_tensor(out=kn32[0:NT, :].rearrange("p (h d) -> p h d", h=2), in0=kn32[0:NT, :].rearrange("p (h d) -> p h d", h=2),
                                                in1=P("gk")[0:NT, :].unsqueeze(1).to_broadcast([NT, 2, 64]), op=ALU.mult), reads=["kn32", "par"], writes=["kn32"])
            op("pool", lambda e: e.tensor_copy(out=knb[0:NT, :], in_=kn32[0:NT, :]), reads=["kn32"], writes=["knb"])
            vcur, vkey = vaug[par_i], "vaug%d" % par_i
            op("pool", lambda e: e.tensor_copy(out=vcur[0:NT, :, 0:64], in_=zkv[0:NT, 128:256].rearrange("p (g d) -> p g d", g=2)), reads=["zkv"], writes=[vkey])
            if sample:
                S.dma("sp", sak, kn32[0:64, :], reads=["kn32"], is_output=True)
                S.dma("sp", sav, zkv[0:64, 128:256], reads=["zkv"], is_output=True)
            elif last_prompt:
                S.dma("sp", pak, kn32[:, :], reads=["kn32"], is_output=True)
                S.dma("sp", pav, zkv[:, 128:256], reads=["zkv"], is_output=True)
            kcur, kkey = kTa[par_i], "kTa%d" % par_i
            pb, pk = bank()
            pbb = pb[:].bitcast(BF16)
            for m in range(4):
                op("pe", lambda e, m=m, pbb=pbb: e.transpose(out=pbb[:, m * NT:(m + 1) * NT], in_=qnb[0:NT, m * 128:(m + 1) * 128], identity=identb[0:NT, 0:NT]),
                   reads=["qnb", "identb"], writes=[pk], acc=(m > 0))
            op("pe", lambda e, pbb=pbb: e.transpose(out=pbb[:, 4 * NT:5 * NT], in_=knb[0:NT, :], identity=identb[0:NT, 0:NT]), reads=["knb", "identb"], writes=[pk], acc=True)
            op("act", lambda e, pbb=pbb: e.activation(out=qTa[:, :, 0:NT], in_=pbb[:, 0:4 * NT].rearrange("p (m n) -> p m n", m=4), func=AF.Copy), reads=[pk], writes=["qTa"])
            op("act", lambda e, pbb=pbb: e.activation(out=kcur[:, 0:NT], in_=pbb[:, 4 * NT:5 * NT], func=AF.Copy), reads=[pk], writes=[kkey])
            kbs = []
            if sample:
                for s in range(4):
                    kbs.append((lambda g, s=s: kTc[g * 64:(g + 1) * 64, s, :], "kTc", lambda g, s=s: vcaug[:, s, g, :], "vcaug", 128,
                                lambda g, s=s: biasSC[:, s, g * 4:(g + 1) * 4, :], "biasP"))
                kbs.append((lambda g: kcur[g * 64:(g + 1) * 64, 0:64], kkey, lambda g: vcur[0:64, g, :], vkey, 64,
                            lambda g: biasS1[:, g * 4:(g + 1) * 4, :], "biasS1"))
            else:
                if t > 0:
                    kprev, vprev = kTa[1 - par_i], vaug[1 - par_i]
                    kbs.append((lambda g: kprev[g * 64:(g + 1) * 64, :], "kTa%d" % (1 - par_i), lambda g: vprev[:, g, :], "vaug%d" % (1 - par_i), 128,
                                lambda g: biasP[:, 0, g * 4:(g + 1) * 4, :], "biasP"))
                kbs.append((lambda g: kcur[g * 64:(g + 1) * 64, :], kkey, lambda g: vcur[:, g, :], vkey, 128,
                            lambda g: biasP[:, 1, g * 4:(g + 1) * 4, :], "biasP"))
            def pTv(bi):
                if bi < 2:
                    return pT01[bi]
                return (Pb[0][:, 0:256], Pb[0][:, 256:512], Pb[1][:, 0:256])[bi - 2]

            def pTk(bi):
                return ("Mb0", "Mb1", "Pb0", "Pb0", "Pb1")[bi]
            for g in range(2):
                for bi, (kf, kk, vf, vk, nk, bf, bk) in enumerate(kbs):
                    pb, pk = bank()
                    op("pe", lambda e, pb=pb, kf=kf, nk=nk, g=g: e.matmul(pb[0:nk, 0:4 * NT], lhsT=kf(g), rhs=qTa[g * 64:(g + 1) * 64, :, 0:NT], start=True, stop=True),
                       reads=[kk, "qTa"], writes=[pk])
                    so_ = 512 + (256 * (bi % 2) if sample else 0)
                    sk_ = ("junkB%d" % (bi % 2)) if sample else "junkB"
                    op("dve", lambda e, pb=pb, nk=nk, bf=bf, g=g, so_=so_: e.scalar_tensor_tensor(out=junk[0:nk, so_:so_ + 4 * NT].rearrange("p (m q) -> p m q", m=4), in0=pb[0:nk, 0:4 * NT].rearrange("p (m q) -> p m q", m=4),
                                                                                       scalar=0.125, in1=bf(g), op0=ALU.mult, op1=ALU.add), reads=[pk, bk], writes=[sk_])
                    op("act", lambda e, nk=nk, bi=bi, so_=so_: e.activation(out=pTv(bi)[0:nk, 0:4 * NT], in_=junk[0:nk, so_:so_ + 4 * NT], func=AF.Exp, bias=cneg[0:nk, 0:1]), reads=[sk_, "cneg"], writes=[pTk(bi)])
                po, pok = bank()
                for m in range(4):
                    for bi, (kf, kk, vf, vk, nk, bf, bk) in enumerate(kbs):
                        op("pe", lambda e, po=po, m=m, bi=bi, nk=nk, vf=vf, g=g: e.matmul(po[0:NT, m * 65:(m + 1) * 65], lhsT=pTv(bi)[0:nk, m * NT:(m + 1) * NT], rhs=vf(g), start=(bi == 0), stop=(bi == len(kbs) - 1)),
                           reads=[pTk(bi), vk], writes=[pok], acc=not (m == 0 and bi == 0))
                pov = po[0:NT, 0:260].rearrange("p (m c) -> p m c", m=4)
                op("dve", lambda e, pov=pov, g=g: e.tensor_tensor(out=st8[0:NT, 16 + g * 4:20 + g * 4], in0=pov[:, :, 64], in1=esink[0:NT, g * 4:(g + 1) * 4], op=ALU.add), reads=[pok, "esink"], writes=["st8d%d" % g])
                op("dve", lambda e, g=g: e.reciprocal(out=st8[0:NT, 16 + g * 4:20 + g * 4], in_=st8[0:NT, 16 + g * 4:20 + g * 4]), reads=["st8d%d" % g], writes=["st8d%d" % g])
                op("dve", lambda e, pov=pov, g=g: e.tensor_tensor(out=oa[0:NT, g * 256:(g + 1) * 256].rearrange("p (m d) -> p m d", m=4), in0=pov[:, :, 0:64],
                                                                  in1=st8[0:NT, 16 + g * 4:20 + g * 4].unsqueeze(2).to_broadcast([NT, 4, 64]), op=ALU.mult), reads=[pok, "st8d%d" % g], writes=["oa"])

            S.stop()
            if recE is not None:
                S.merge(recE, recY, recX)
            else:
                S.merge(recX, recY)
            if sample:
                SS = ((Sm, "Sm"), (zq[:].rearrange("p (h d) -> p h d", h=4), "zq"), (zag[:].rearrange("p (h d) -> p h d", h=4), "zag"), (zbg[:].rearrange("p (h d) -> p h d", h=4), "zbg"))
                for s_ in range(4):
                    S.dma("sp", SS[s_][0][:] if s_ == 0 else SS[s_][0], sb[s_].rearrange("h k v -> k h v"), writes=[SS[s_][1]])
            op("dve", lambda e: e.scalar_tensor_tensor(out=cat[0:NT, 0:512], in0=oa[0:NT, :], scalar=0.5, in1=gA[0:NT, 0:512], op0=ALU.mult, op1=ALU.mult), reads=["oa", "gA"], writes=["b16a"])
            bank_pool[0] = [0, 1, 2]
            recC = S.record()
            conv_half(1, "finish")
            for qi, (src, sk) in enumerate(((cq, "tc"), (ckk, "tc"))):
                op("dve", lambda e, src=src: e.tensor_tensor(out=junk[0:NT, 0:512], in0=src[0:NT, :], in1=src[0:NT, :], op=ALU.mult), reads=[sk], writes=["junk"])
                op("dve", lambda e, qi=qi: e.tensor_reduce(out=gt_[0:NT, 8 + qi * 4:12 + qi * 4], in_=junk[0:NT, 0:512].rearrange("p (h d) -> p h d", h=4), axis=AX.X, op=ALU.add), reads=["junk"], writes=["gtn"])
            rsqrt_small(gt_[0:NT, 8:16], gt_[0:NT, 8:16], "gtn", "gtn", NT, 8, 1.0, 4 * EPS)
            op("dve", lambda e: e.tensor_tensor(out=gb[0:NT, 0:4], in0=gt_[0:NT, 12:16], in1=gt_[0:NT, 0:4], op=ALU.mult), reads=["gtn", "gtb"], writes=["gbs"])
            op("dve", lambda e: e.tensor_tensor(out=gb[0:NT, 4:8], in0=gb[0:NT, 0:4], in1=gts[0:NT, 8:12], op=ALU.mult), reads=["gbs", "gtse"], writes=["gbs"])
            op("dve", lambda e: e.tensor_tensor(out=gb[0:NT, 8:12], in0=gt_[0:NT, 12:16], in1=gts[0:NT, 12:16], op=ALU.mult), reads=["gtn", "gtse", "gbs"], writes=["gbs"])
            op("dve", lambda e: e.tensor_scalar(out=gb[0:NT, 12:16], in0=gt_[0:NT, 8:12], scalar1=128.0 ** -0.5, scalar2=None, op0=ALU.mult), reads=["gtn", "gbs"], writes=["gbs"])
            op("dve", lambda e: e.tensor_scalar(out=gb[0:NT, 16:20], in0=gt_[0:NT, 0:4], scalar1=0.5, scalar2=None, op0=ALU.mult), reads=["gtb", "gbs"], writes=["gbs"])

            def bc4(a):
                return a.unsqueeze(2).to_broadcast([NT, 4, 128])

            def v3(x):
                return x[0:NT, :].rearrange("p (h d) -> p h d", h=4)
            op("dve", lambda e: e.tensor_tensor(out=v3(knB), in0=v3(ckk), in1=bc4(gt_[0:NT, 12:16]), op=ALU.mult), reads=["tc", "gtn"], writes=["knB"])
            op("pool", lambda e: e.tensor_tensor(out=v3(kbg), in0=v3(ckk), in1=bc4(gb[0:NT, 4:8]), op=ALU.mult), reads=["tc", "gbs"], writes=["kbg"])
            op("pool", lambda e: e.tensor_tensor(out=v3(kdec), in0=v3(ckk), in1=bc4(gb[0:NT, 8:12]), op=ALU.mult), reads=["tc", "gbs"], writes=["kdec"])
            op("dve", lambda e: e.tensor_tensor(out=v3(qnB), in0=v3(cq), in1=bc4(gb[0:NT, 12:16]), op=ALU.mult), reads=["tc", "gbs"], writes=["qnb"])
            op("pool", lambda e: e.tensor_tensor(out=v3(vbb), in0=v3(cvv), in1=bc4(gb[0:NT, 16:20]), op=ALU.mult), reads=["oa", "gbs"], writes=["vbb"])
            for src, sk, dst, dk_ in ((knB, "knB", kTB, "kTB"), (qnB, "qnb", qTB, "qTa")):
                pb, pk = bank()
                pbb = pb[:].bitcast(BF16)
                for h in range(4):
                    op("pe", lambda e, pbb=pbb, h=h, src=src: e.transpose(out=pbb[:, h * NT:(h + 1) * NT], in_=src[0:NT, h * 128:(h + 1) * 128], identity=identb[0:NT, 0:NT]),
                       reads=[sk, "identb"], writes=[pk], acc=(h > 0))
                op("act", lambda e, pbb=pbb, dst=dst: e.activation(out=dst[:, :, 0:NT], in_=pbb[:, 0:4 * NT].rearrange("p (h n) -> p h n", h=4), func=AF.Copy), reads=[pk], writes=[dk_])
            pkk, pkkk = bank()
            for h in range(4):
                op("pe", lambda e, pkk=pkk, h=h: e.matmul(pkk[0:NT, h * NT:(h + 1) * NT], lhsT=kTB[:, h, 0:NT], rhs=kTB[:, h, 0:NT], start=True, stop=True), reads=["kTB"], writes=[pkkk], acc=(h > 0))
            for h in range(4):
                op("dve", lambda e, pkk=pkk, h=h: e.scalar_tensor_tensor(out=Lb[0:NT, h * NT:(h + 1) * NT], in0=pkk[0:NT, h * NT:(h + 1) * NT], scalar=gt_[0:NT, h:h + 1], in1=Es[0:NT, h * NT:(h + 1) * NT], op0=ALU.mult, op1=ALU.mult),
                   reads=[pkkk, "gtb", "Es"], writes=["Lb"])
            pqk_, pqkk = bank()
            for h in range(4):
                op("pe", lambda e, h=h: e.matmul(pqk_[0:NT, h * NT:(h + 1) * NT], lhsT=qTB[:, h, 0:NT], rhs=kTB[:, h, 0:NT], start=True, stop=True), reads=["kTB", "qTa"], writes=[pqkk], acc=(h > 0))
            op("dve", lambda e: e.tensor_tensor(out=qkb[0:NT, 0:4 * NT], in0=pqk_[0:NT, 0:4 * NT], in1=Ei[0:NT, 0:4 * NT], op=ALU.mult), reads=[pqkk, "Ei"], writes=["knB"])
            pm, pmk = bank()
            pmb = pm[:].bitcast(BF16)
            for h in range(4):
                op("pe", lambda e, h=h: e.transpose(out=pmb[0:NT, h * NT:(h + 1) * NT], in_=Lb[0:NT, h * NT:(h + 1) * NT], identity=identb[0:NT, 0:NT]), reads=["Lb", "identb"], writes=[pmk], acc=(h > 0))
            op("act", lambda e: e.activation(out=Mb[0][0:NT, 0:4 * NT], in_=pmb[0:NT, 0:4 * NT], func=AF.Copy), reads=[pmk], writes=["Mb0"])
            op("dve", lambda e: e.tensor_tensor(out=Xb1[0:NT, 0:4 * NT].rearrange("p (h j) -> p h j", h=4), in0=identb[0:NT, 0:NT].unsqueeze(1).to_broadcast([NT, 4, NT]),
                                                in1=Mb[0][0:NT, 0:4 * NT].rearrange("p (h j) -> p h j", h=4), op=ALU.subtract), reads=["identb", "Mb0"], writes=["Xb"])
            pt_, ptk = bank()
            ptb = pt_[:].bitcast(BF16)
            for h in range(4):
                op("pe", lambda e, h=h: e.transpose(out=ptb[0:NT, h * NT:(h + 1) * NT], in_=qkb[0:NT, h * NT:(h + 1) * NT], identity=identb[0:NT, 0:NT]), reads=["knB", "identb"], writes=[ptk], acc=(h > 0))
            op("act", lambda e: e.activation(out=qkT[0:NT, 0:4 * NT], in_=ptb[0:NT, 0:4 * NT], func=AF.Copy), reads=[ptk], writes=["qkT"])
            J = 5 if BS == 64 else 3
            Pc, Pk_ = Lb, "Lb"
            Mc, Mk_ = Mb[0], "Mb0"
            xi = 0
            for j in range(1, J + 1):
                Pn, Pnk = Pb[j % 2], "Pb%d" % (j % 2)
                pp, ppk = bank()
                for h in range(4):
                    op("pe", lambda e, pp=pp, h=h, Mc=Mc, Pc=Pc: e.matmul(pp[0:NT, h * NT:(h + 1) * NT], lhsT=Mc[0:NT, h * NT:(h + 1) * NT], rhs=Pc[0:NT, h * NT:(h + 1) * NT], start=True, stop=True),
                       reads=[Mk_, Pk_], writes=[ppk], acc=(h > 0))
                op("act", lambda e, pp=pp, Pn=Pn: e.activation(out=Pn[0:NT, 0:4 * NT], in_=pp[0:NT, 0:4 * NT], func=AF.Copy), reads=[ppk], writes=[Pnk])
                if j < J:
                    Mn, Mnk = Mb[j % 2], "Mb%d" % (j % 2)
                    pm2, pm2k = bank()
                    for h in range(4):
                        op("pe", lambda e, pm2=pm2, h=h, Mc=Mc, Pc=Pc: e.matmul(pm2[0:NT, h * NT:(h + 1) * NT], lhsT=Pc[0:NT, h * NT:(h + 1) * NT], rhs=Mc[0:NT, h * NT:(h + 1) * NT], start=True, stop=True),
                           reads=[Mk_, Pk_], writes=[pm2k], acc=(h > 0))
                    op("dve", lambda e, pm2=pm2, Mn=Mn: e.tensor_copy(out=Mn[0:NT, 0:4 * NT], in_=pm2[0:NT, 0:4 * NT]), reads=[pm2k], writes=[Mnk])
                Xc, Xck = Xb1, "Xb"
                Xn, Xnk = Xb1, "Xb"
                px, pxk = bank()
                for h in range(4):
                    op("pe", lambda e, px=px, h=h, Pn=Pn, Xc=Xc: e.matmul(px[0:NT, h * NT:(h + 1) * NT], lhsT=Pn[0:NT, h * NT:(h + 1) * NT], rhs=Xc[0:NT, h * NT:(h + 1) * NT], start=True, stop=True),
                       reads=[Pnk, Xck], writes=[pxk], acc=(h > 0))
                op("dve", lambda e, px=px, Xc=Xc, Xn=Xn: e.tensor_tensor(out=Xn[0:NT, 0:4 * NT], in0=px[0:NT, 0:4 * NT], in1=Xc[0:NT, 0:4 * NT], op=ALU.add), reads=[pxk, Xck], writes=[Xnk])
                xi = 1 - xi
                Pc, Pk_ = Pn, Pnk
                if j < J:
                    Mc, Mk_ = Mn, Mnk
            X, Xk = Xb1, "Xb"
            pu, puk = bank()
            for h in range(4):
                op("pe", lambda e, h=h: e.matmul(pu[0:NT, h * 128:(h + 1) * 128], lhsT=X[0:NT, h * NT:(h + 1) * NT], rhs=vbb[0:NT, h * 128:(h + 1) * 128], start=True, stop=True), reads=[Xk, "vbb"], writes=[puk], acc=(h > 0))
            op("act", lambda e: e.activation(out=uu[0:NT, :], in_=pu[0:NT, 0:512], func=AF.Copy), reads=[puk], writes=["oa"])
            pw, pwk = bank()
            for h in range(4):
                op("pe", lambda e, h=h: e.matmul(pw[:, h * NT:(h + 1) * NT], lhsT=kbg[0:NT, h * 128:(h + 1) * 128], rhs=X[0:NT, h * NT:(h + 1) * NT], start=True, stop=True), reads=[Xk, "kbg"], writes=[pwk], acc=(h > 0))
            pwv = pw[:, 0:4 * NT].rearrange("p (h n) -> p h n", h=4)
            eG = gts[0:NT, 8:12]
            if not sample:
                op("act", lambda e: e.activation(out=wT[:, :, 0:NT], in_=pwv, func=AF.Copy), reads=[pwk], writes=["Mb0"])
                for b in range(2):
                    r0, r1 = b * 64, (b + 1) * 64
                    op("dve", lambda e, b=b: e.tensor_tensor(out=Sm[:], in0=Sm[:], in1=gtot_ap[:, 4 * b:4 * b + 4].unsqueeze(2).to_broadcast([128, 4, 128]), op=ALU.mult), reads=["Sm", "gtot"], writes=["Sm"])
                    pws, pwsk = bank()
                    for h in range(4):
                        op("pe", lambda e, pws=pws, h=h: e.matmul(pws[0:128, h * 128:(h + 1) * 128], lhsT=wT[:, h, 0:128], rhs=Sbf[:, h, :], start=True, stop=True), reads=["Mb0", "Sbf"], writes=[pwsk], acc=(h > 0))
                    po1, po1k = bank()
                    for h in range(4):
                        op("pe", lambda e, po1=po1, h=h: e.matmul(po1[0:128, h * 128:(h + 1) * 128], lhsT=qTB[:, h, 0:128], rhs=Sbf[:, h, :], start=True, stop=True), reads=["qTa", "Sbf"], writes=[po1k], acc=(h > 0))
                    op("dve", lambda e, pws=pws, r0=r0, r1=r1: e.tensor_tensor(out=vnew[r0:r1, :], in0=uu[r0:r1, :], in1=pws[r0:r1, 0:512], op=ALU.subtract), reads=["oa", pwsk], writes=["Lb"])
                    op("dve", lambda e, po1=po1, r0=r0, r1=r1: e.tensor_tensor(out=oacc[r0:r1, :].rearrange("p (h d) -> p h d", h=4), in0=po1[r0:r1, 0:512].rearrange("p (h d) -> p h d", h=4),
                                                                             in1=eG[r0:r1, :].unsqueeze(2).to_broadcast([64, 4, 128]), op=ALU.mult), reads=[po1k, "gtse"], writes=["Es"])
                    pds, pdsk = bank()
                    for h in range(4):
                        op("pe", lambda e, pds=pds, h=h, r0=r0, r1=r1: e.matmul(pds[:, h * 128:(h + 1) * 128], lhsT=kdec[r0:r1, h * 128:(h + 1) * 128], rhs=vnew[r0:r1, h * 128:(h + 1) * 128], start=True, stop=True), reads=["kdec", "Lb"], writes=[pdsk], acc=(h > 0))
                    op("dve", lambda e, pds=pds: e.tensor_tensor(out=Sm[:].rearrange("p h d -> p (h d)"), in0=Sm[:].rearrange("p h d -> p (h d)"), in1=pds[:, 0:512], op=ALU.add), reads=["Sm", pdsk], writes=["Sm"])
                    op("act", lambda e: e.activation(out=Sbf[:], in_=Sm[:], func=AF.Copy), reads=["Sm"], writes=["Sbf"])
                if last_prompt:
                    S.dma("sp", pbs.rearrange("h k v -> k h v"), Sm[:], reads=["Sm"], is_output=True)
            else:
                oc_ = CST2_OFF["colm"][0]
                S.dma("sp", junk[:, 0:256], cst2_d[:, oc_:oc_ + 256], writes=["junk"])
                colmb = junk[:, 0:256].rearrange("p (a b) -> p a b", a=4)
                pws, pwsk = bank()
                po1, po1k = bank()
                for s_ in range(4):
                    Ss_, Sk_ = SS[s_]
                    Ssv = Ss_[:] if s_ == 0 else Ss_
                    op("act", lambda e, Ssv=Ssv: e.activation(out=Sbf[:], in_=Ssv, func=AF.Copy), reads=[Sk_], writes=["Sbf"])
                    op("dve", lambda e, s_=s_: e.tensor_tensor(out=wTz[:], in0=pwv, in1=colmb[:, s_, :].unsqueeze(1).to_broadcast([128, 4, 64]), op=ALU.mult), reads=[pwk, "junk"], writes=["Pb0"])
                    op("pool", lambda e, s_=s_: e.tensor_tensor(out=qTz[:], in0=qTB[:, :, 0:64], in1=colmb[:, s_, :].unsqueeze(1).to_broadcast([128, 4, 64]), op=ALU.mult), reads=["qTa", "junk"], writes=["Pb0"])
                    for h in range(4):
                        op("pe", lambda e, h=h, s_=s_: e.matmul(pws[0:64, h * 128:(h + 1) * 128], lhsT=wTz[:, h, :], rhs=Sbf[:, h, :], start=(s_ == 0 and h == 0), stop=(s_ == 3 and h == 3), skip_group_check=True), reads=["Pb0", "Sbf"], writes=[pwsk], acc=not (h == 0 and s_ == 0))
                    for h in range(4):
                        op("pe", lambda e, h=h, s_=s_: e.matmul(po1[0:64, h * 128:(h + 1) * 128], lhsT=qTz[:, h, :], rhs=Sbf[:, h, :], start=(s_ == 0 and h == 0), stop=(s_ == 3 and h == 3), skip_group_check=True), reads=["Pb0", "Sbf"], writes=[po1k], acc=not (h == 0 and s_ == 0))
                op("dve", lambda e: e.tensor_tensor(out=vnew[0:64, :], in0=uu[0:64, :], in1=pws[0:64, 0:512], op=ALU.subtract), reads=["oa", pwsk], writes=["Lb"])
                op("dve", lambda e: e.tensor_tensor(out=oacc[0:64, :].rearrange("p (h d) -> p h d", h=4), in0=po1[0:64, 0:512].rearrange("p (h d) -> p h d", h=4),
                                                    in1=eG.unsqueeze(2).to_broadcast([64, 4, 128]), op=ALU.mult), reads=[po1k, "gtse"], writes=["Es"])
                for s_ in range(4):
                    op("pool", lambda e, s_=s_: e.tensor_scalar(out=kdm[:], in0=kdec[0:64, :], scalar1=C("seqm", 64)[:, s_:s_ + 1], scalar2=None, op0=ALU.mult), reads=["kdec", "cst"], writes=["Pb1"])
                    pds, pdsk = bank()
                    for h in range(4):
                        op("pe", lambda e, pds=pds, h=h: e.matmul(pds[:, h * 128:(h + 1) * 128], lhsT=kdm[:, h * 128:(h + 1) * 128], rhs=vnew[0:64, h * 128:(h + 1) * 128], start=True, stop=True), reads=["Pb1", "Lb"], writes=[pdsk], acc=(h > 0))
                    Ss_, Sk_ = SS[s_]
                    Ssv = Ss_[:] if s_ == 0 else Ss_
                    op("dve", lambda e, s_=s_, Ssv=Ssv: e.tensor_tensor(out=Ssv, in0=Ssv, in1=gtot_ap[:, 4 * s_:4 * s_ + 4].unsqueeze(2).to_broadcast([128, 4, 128]), op=ALU.mult), reads=[Sk_, "gtot"], writes=[Sk_])
                    op("dve", lambda e, pds=pds, Ssv=Ssv: e.tensor_tensor(out=Ssv, in0=Ssv, in1=pds[:, 0:512].rearrange("p (h d) -> p h d", h=4), op=ALU.add), reads=[Sk_, pdsk], writes=[Sk_])
                    S.dma("sp", sbs[s_].rearrange("h k v -> k h v"), Ssv, reads=[Sk_], is_output=True)
            po2, po2k = bank()
            for h in range(4):
                op("pe", lambda e, h=h: e.matmul(po2[0:NT, h * 128:(h + 1) * 128], lhsT=qkT[0:NT, h * NT:(h + 1) * NT], rhs=vnew[0:NT, h * 128:(h + 1) * 128], start=True, stop=True), reads=["qkT", "Lb"], writes=[po2k], acc=(h > 0))
            op("dve", lambda e: e.tensor_tensor(out=ob[0:NT, :], in0=po2[0:NT, 0:512], in1=oacc[0:NT, :], op=ALU.add), reads=[po2k, "Es"], writes=["Ei"])
            op("dve", lambda e: e.tensor_tensor(out=junk[0:NT, 0:512], in0=ob[0:NT, :], in1=ob[0:NT, :], op=ALU.mult), reads=["Ei"], writes=["junk"])
            op("dve", lambda e: e.tensor_reduce(out=gts[0:NT, 0:4], in_=junk[0:NT, 0:512].rearrange("p (h d) -> p h d", h=4), axis=AX.X, op=ALU.add), reads=["junk", "gtse"], writes=["gts"])
            rsqrt_small(gts[0:NT, 0:4], gts[0:NT, 0:4], "gts", "gts", NT, 4, 1.0 / 128, EPS)
            op("dve", lambda e: e.tensor_tensor(out=v3(ob), in0=v3(ob), in1=bc4(gts[0:NT, 0:4]), op=ALU.mult), reads=["Ei", "gts"], writes=["Ei"])
            op("dve", lambda e: e.tensor_tensor(out=v3(ob), in0=v3(ob), in1=P("onorm")[0:NT, :].unsqueeze(1).to_broadcast([NT, 4, 128]), op=ALU.mult), reads=["Ei", "par"], writes=["Ei"])
            op("dve", lambda e: e.scalar_tensor_tensor(out=cat[0:NT, 512:1024], in0=ob[0:NT, :], scalar=0.5, in1=gB[0:NT, 0:512], op0=ALU.mult, op1=ALU.mult), reads=["Ei", "gB"], writes=["b16a"])

            transpose8(cat, "b16a", catT, "xT", NT)
            for nb_ in range(2):
                pb, pk = proj_tm(Woab, nb_ * 512, 512, NT, catT, "xT")
                op("dve", lambda e, pb=pb, nb_=nb_: e.tensor_tensor(out=xt[0:NT, nb_ * 512:(nb_ + 1) * 512], in0=pb[0:NT, 0:512], in1=xt[0:NT, nb_ * 512:(nb_ + 1) * 512], op=ALU.add), reads=[pk, XT], writes=[XT])


            S.stop()
            if t + 1 < NTP:
                bank_pool[0] = [3, 4]
                recA = S.record()
                emit_A(t + 1)
                S.stop()
                S.merge(recC, recA)
            else:
                S.merge(recC)
            bank_pool[0] = list(range(8))

        def emit_E(t):
            sample = (t == NTP)
            NT = 64 if sample else 128
            BS = 16 if sample else 64
            NB = NT // BS
            tg = "S" if sample else "P"
            par_i = t % 2
            xin = xs if sample else xp[t * 128:(t + 1) * 128, :]
            yout = ys if sample else yp[t * 128:(t + 1) * 128, :]
            last_prompt = (t == NTP - 1)

            xt = xtb[t % 2]
            XT = "xt%d" % (t % 2)
            ucur, ukey = (ubS if sample else ubP), "ub"
            need_u32 = sample
            Smv = Sm[:].rearrange("p h d -> p (h d)")
            if sample:
                S.dma("sp", yc[0:60, 0:1024], spool, writes=["yc"])
                op("dve", lambda e: e.tensor_copy(out=gA[0:60, :], in_=yc[0:60, 0:512]), reads=["yc"], writes=["gA"])
                op("dve", lambda e: e.tensor_copy(out=gB[0:60, :], in_=yc[0:60, 512:1024]), reads=["yc"], writes=["gB"])
            rms_to_xT(xt, XT, NT, P("norm_c"))
            ucb, uck = ubf[par_i], "ubf%d" % par_i
            for nb_ in range(2):
                pb, pk = proj_tm(Wic, nb_ * 512, 512, NT, xT, "xT")
                op("act", lambda e, pb=pb, nb_=nb_: e.activation(out=ucb[0:NT, nb_ * 512:(nb_ + 1) * 512], in_=pb[0:NT, 0:512], func=AF.Copy), reads=[pk], writes=[uck])
                if need_u32:
                    op("dve", lambda e, pb=pb, nb_=nb_: e.tensor_copy(out=u32[0:NT, nb_ * 512:(nb_ + 1) * 512], in_=pb[0:NT, 0:512]), reads=[pk], writes=["yc"])
                if last_prompt:
                    op("dve", lambda e, pb=pb: e.tensor_copy(out=Smv[96:128, :], in_=pb[96:128, 0:512]), reads=[pk], writes=["Sm"])
                    S.dma("sp", pcp[:, nb_ * 512:(nb_ + 1) * 512], Smv[113:128, :], reads=["Sm"], is_output=True)
            g2 = (xnA[:, 0:512], xnA[:, 512:1024])
            gtmp = ((zbg, "zbg"), (zag, "zag"))
            for nb_ in range(2):
                pb, pk = proj_tm(Wic, 1024 + nb_ * 512, 512, NT, xT, "xT")
                tmp, tk = gtmp[nb_]
                op("act", lambda e, pb=pb, tmp=tmp: e.activation(out=tmp[0:NT, 0:512], in_=pb[0:NT, 0:512], func=AF.Tanh, scale=0.5), reads=[pk], writes=[tk])
                op("dve", lambda e, pb=pb, tmp=tmp, nb_=nb_: e.scalar_tensor_tensor(out=g2[nb_][0:NT, 0:512], in0=tmp[0:NT, 0:512], scalar=1.0, in1=pb[0:NT, 0:512], op0=ALU.add, op1=ALU.mult),
                   reads=[pk, tk], writes=["xnA"])
            if sample:
                for s_ in range(4):
                    S.dma("sp", scp[s_ * 15:(s_ + 1) * 15, :], u32[s_ * 16 + 1:s_ * 16 + 16, :], reads=["yc"], is_output=True)
            for half in range(2):
                pb, pk = bank()
                for gg in range(2):
                    gi = half * 2 + gg
                    cols = slice(gi * 256, (gi + 1) * 256)
                    if sample:
                        op("pe", lambda e, pb=pb, gg=gg, gi=gi, cols=cols: e.matmul(pb[0:64, gg * 256:(gg + 1) * 256], lhsT=bcS[0:64, gi, :], rhs=ucb[0:64, cols], start=True, stop=False), reads=["bands", uck], writes=[pk], acc=(gg > 0))
                        op("pe", lambda e, pb=pb, gg=gg, gi=gi, cols=cols: e.matmul(pb[0:64, gg * 256:(gg + 1) * 256], lhsT=bhS[0:60, gi, :], rhs=(gA if gi < 2 else gB)[0:60, (gi % 2) * 256:(gi % 2 + 1) * 256], start=False, stop=True), reads=["bands", "gA", "gB"], writes=[pk], acc=True)
                    else:
                        op("pe", lambda e, pb=pb, gg=gg, gi=gi, cols=cols: e.matmul(pb[0:128, gg * 256:(gg + 1) * 256], lhsT=bcP[:, gi, :], rhs=ucb[:, cols], start=True, stop=(t == 0)), reads=["bands", uck], writes=[pk], acc=(gg > 0))
                        if t > 0:
                            op("pe", lambda e, pb=pb, gg=gg, gi=gi, cols=cols: e.matmul(pb[0:128, gg * 256:(gg + 1) * 256], lhsT=bpP[:, gi, :], rhs=ubf[1 - par_i][:, cols], start=False, stop=True), reads=["bands", "ubf%d" % (1 - par_i)], writes=[pk], acc=True)
                for gg in range(2):
                    gi = half * 2 + gg
                    cols = slice(gi * 256, (gi + 1) * 256)
                    if t == 0 and not sample:
                        op("dve", lambda e, pb=pb, gg=gg, cols=cols, gi=gi: e.tensor_scalar(out=pooled[0:NT, cols], in0=pb[0:NT, gg * 256:(gg + 1) * 256], scalar1=C("icnt0")[:, gi:gi + 1], scalar2=None, op0=ALU.mult),
                           reads=[pk, "cst"], writes=["b16a"])
                        op("dve", lambda e, cols=cols, gi=gi: e.scalar_tensor_tensor(out=pooled[0:NT, cols], in0=ucb[0:NT, cols], scalar=C("ccor0")[:, gi:gi + 1], in1=pooled[0:NT, cols], op0=ALU.mult, op1=ALU.add),
                           reads=[uck, "cst", "b16a"], writes=["b16a"])
                    else:
                        op("act", lambda e, pb=pb, gg=gg, cols=cols, gi=gi: e.activation(out=pooled[0:NT, cols], in_=pb[0:NT, gg * 256:(gg + 1) * 256], func=AF.Copy, scale=1.0 / POOLW[gi]),
                           reads=[pk], writes=["b16a"])
            transpose8(pooled, "b16a", catT, "xT", NT)
            for half in range(2):
                pb, pk = bank()
                for gg in range(2):
                    gi = half * 2 + gg
                    for kk in range(2):
                        op("pe", lambda e, pb=pb, gg=gg, gi=gi, kk=kk: e.matmul(pb[0:NT, gg * 256:(gg + 1) * 256], lhsT=catT[:, 2 * gi + kk, 0:NT], rhs=Wgrp[:, gi, kk, :], start=(kk == 0), stop=(kk == 1)),
                           reads=["xT"] + WK[id(Wgrp)], writes=[pk], acc=not (gg == 0 and kk == 0))
                cols = slice(half * 512, (half + 1) * 512)
                op("dve", lambda e, pb=pb, cols=cols, half=half: e.scalar_tensor_tensor(out=mg[0:NT, cols], in0=pb[0:NT, 0:512], scalar=0.5, in1=g2[half][0:NT, 0:512], op0=ALU.mult, op1=ALU.mult), reads=[pk, "xnA"], writes=["b16a"])
            transpose8(mg, "b16a", catT, "xT", NT, P("scale_pc"))
            for nb_ in range(2):
                pb, pk = proj_tm(Woc, nb_ * 512, 512, NT, catT, "xT")
                op("dve", lambda e, pb=pb, nb_=nb_: e.tensor_tensor(out=xt[0:NT, nb_ * 512:(nb_ + 1) * 512], in0=pb[0:NT, 0:512], in1=xt[0:NT, nb_ * 512:(nb_ + 1) * 512], op=ALU.add), reads=[pk, XT], writes=[XT])
            S.dma("sp", yout, xt[0:NT, :], reads=[XT], is_output=True)

        emit_A(0)
        emit_BCD(0)
        S.nopool = False
        for t in range(1, NTP):
            bank_pool[0] = [3, 4]
            recE = S.record()
            emit_E(t - 1)
            S.stop()
            emit_BCD(t, recE)
        bank_pool[0] = list(range(8))
        emit_A(NTP)
        bank_pool[0] = [3, 4]
        recE = S.record()
        emit_E(NTP - 1)
        S.stop()
        emit_BCD(NTP, recE)
        bank_pool[0] = list(range(8))
        emit_E(NTP)
        S.emit(block)
    return nc


def _prep_weights(inp):
    w = np.asarray(inp["w_in_ab"][0], np.float32)
    a_q = w[:, 0:512].reshape(1024, 8, 64)
    perm = [g * 4 + m for m in range(4) for g in range(2)]
    a_qp = a_q[:, perm, :].reshape(1024, 512)
    cols = np.concatenate([a_qp, w[:, 512:640], w[:, 640:768], w[:, 3328:3332], w[:, 3332:3336],
                           w[:, 768:1280], w[:, 2816:3328], w[:, 1280:2816]], axis=1)
    assert cols.shape[1] == NAB

    def pcn(a):
        n = a.shape[1]
        return np.ascontiguousarray(a.reshape(8, 128, n).transpose(1, 0, 2).reshape(128, 8 * n))
    wab = pcn(cols)
    woab = pcn(np.asarray(inp["w_out_ab"][0], np.float32))
    wic = pcn(np.asarray(inp["w_in_c"][0], np.float32))
    woc = pcn(np.asarray(inp["w_out_c"][0], np.float32))
    wg = np.asarray(inp["w_grp_c"][0], np.float32)
    wgrp = np.ascontiguousarray(wg.reshape(4, 2, 128, 256).transpose(2, 0, 1, 3).reshape(128, 8 * 256))
    par = np.zeros((128, NPAR), np.float32)

    def put(name, arr):
        o, wd = PAR_OFF[name]
        par[:, o:o + wd] = np.broadcast_to(np.asarray(arr, np.float32).reshape(-1, wd) if np.asarray(arr).ndim > 1 else np.asarray(arr, np.float32)[None, :], (128, wd))
    put("gq", inp["q_norm_a"][0]); put("gk", inp["k_norm_a"][0]); put("onorm", inp["o_norm_b"][0])
    put("sinks", inp["sinks_a"][0]); put("alog", inp["a_log_b"][0]); put("dtb", inp["dt_bias_b"][0])
    o, wd = PAR_OFF["scale_pc"]
    par[:, o:o + wd] = np.asarray(inp["scale_c"][0], np.float32).reshape(8, 128).T
    cw = np.asarray(inp["conv_b"][0], np.float32)
    o, wd = PAR_OFF["convw"]
    par[:, o:o + wd] = cw.reshape(4, 12, 128).transpose(2, 1, 0).reshape(128, 48)
    o, wd = PAR_OFF["norm_ab"]
    par[:, o:o + wd] = np.asarray(inp["norm_ab"][0], np.float32).reshape(8, 128).T
    o, wd = PAR_OFF["norm_c"]
    par[:, o:o + wd] = np.asarray(inp["norm_c"][0], np.float32).reshape(8, 128).T
    return dict(wab=wab, woab=woab, wic=wic, wgrp=wgrp, woc=woc, par=par)


_CACHE = {}


def make_in_maps(inp):
    xp = np.asarray(inp["x_prompt"], np.float32)
    shared = _prep_weights(inp)
    shared["cst"], shared["cst2"], shared["cst3"] = build_consts()
    xs = np.asarray(inp["x_sample"], np.float32)
    in_maps = []
    for c in range(8):
        m = dict(shared)
        m["xp"] = np.ascontiguousarray(xp[c])
        m["xs"] = np.ascontiguousarray(xs[4 * c:4 * c + 4].reshape(64, D))
        m["ck"] = np.ascontiguousarray(np.asarray(inp["cache_a_k"], np.float32)[0, 4 * c:4 * c + 4].reshape(4, 128, 128))
        m["cv"] = np.ascontiguousarray(np.asarray(inp["cache_a_v"], np.float32)[0, 4 * c:4 * c + 4].reshape(4, 128, 128))
        m["sb"] = np.ascontiguousarray(np.asarray(inp["state_b_s"], np.float32)[0, 4 * c:4 * c + 4])
        m["sconv"] = np.ascontiguousarray(np.asarray(inp["state_b_conv"], np.float32)[0, 4 * c:4 * c + 4].reshape(12, 1536))
        m["spool"] = np.ascontiguousarray(np.asarray(inp["state_c_pool"], np.float32)[0, 4 * c:4 * c + 4].reshape(60, D))
        in_maps.append(m)
    return in_maps


def assemble(res, SEQ):
    nb = len(res)

    def cat(name, shape_per):
        return np.stack([np.asarray(r[name]).reshape(shape_per) for r in res])
    y_p = cat("yp", (SEQ, D))
    y_s = cat("ys", (4, 16, D)).reshape(4 * nb, 16, D)
    pa_k = cat("pak", (128, 2, 64))[None]
    pa_v = cat("pav", (128, 2, 64))[None]
    pb_s = cat("pbs", (4, 128, 128))[None]
    pb_c = cat("pbc", (3, 1536))[None]
    pc = cat("pcp", (15, D))[None]
    sa_k = cat("sak", (4, 16, 2, 64)).reshape(4 * nb, 16, 2, 64)[None]
    sa_v = cat("sav", (4, 16, 2, 64)).reshape(4 * nb, 16, 2, 64)[None]
    sb_s = cat("sbs", (4, 4, 128, 128)).reshape(4 * nb, 4, 128, 128)[None]
    sb_c = cat("sbc", (4, 3, 1536)).reshape(4 * nb, 3, 1536)[None]
    sc = cat("scp", (4, 15, D)).reshape(4 * nb, 15, D)[None]
    return (y_p, y_s, pa_k, pa_v, pb_s, pb_c, pc, sa_k, sa_v, sb_s, sb_c, sc)


def kernel(**inp):
    SEQ = np.asarray(inp["x_prompt"]).shape[1]
    NTP = SEQ // 128
    if NTP not in _CACHE:
        _CACHE[NTP] = build_program(NTP)
    nc = _CACHE[NTP]
    in_maps = make_in_maps(inp)
    res = run_bass_kernel_spmd(nc, in_maps, core_ids=list(range(8))).results
    return assemble(res, SEQ)
```

```python
import numpy as np
from contextlib import ExitStack
import concourse.bass as bass
import concourse.mybir as mybir
from concourse.bass_utils import run_bass_kernel_spmd

F32 = mybir.dt.float32
BF16 = mybir.dt.bfloat16
AF = mybir.ActivationFunctionType
ALU = mybir.AluOpType
AX = mybir.AxisListType

D = 1024
NEG = -30000.0
EPS = 1e-6
CQ, CKV, CAG, CBG, CBQ = 0, 512, 776, 1288, 1800
NAB = 3336


class Sched:
    ENGS = ("pe", "act", "dve", "pool", "sp")

    def __init__(self, nc, es, n_dma_sems=24):
        self.nc = nc
        self.ops = {e: [] for e in self.ENGS}
        self.sem = {e: es.enter_context(nc.semaphore("s_" + e)) for e in ("pe", "act", "dve", "pool")}
        self.cnt = {e: 0 for e in ("pe", "act", "dve", "pool")}
        self.dsem = [es.enter_context(nc.semaphore("s_dma%d" % i)) for i in range(n_dma_sems)]
        self.dval = [0] * n_dma_sems
        self.dpool = {"sp": list(range(0, 12)), "act": list(range(12, 16)), "pool": list(range(16, n_dma_sems))}
        self.dnext = {"sp": 0, "act": 0, "pool": 0}
        self.seen = {e: {} for e in self.ENGS}
        self.lastw = {}
        self.readers = {}
        self.out_tokens = []
        self.expand = {}

    def _x(self, keys):
        out = []
        for k in keys:
            out.extend(self.expand.get(k, (k,)))
        return out

    def _deps(self, eng, reads, writes):
        toks = []
        for k in reads:
            w = self.lastw.get(k)
            if w is not None:
                toks.append(w)
        for k in writes:
            w = self.lastw.get(k)
            if w is not None:
                toks.append(w)
            toks.extend(self.readers.get(k, ()))
        best = {}
        for (s, v) in toks:
            if best.get(id(s), (None, -1))[1] < v:
                best[id(s)] = (s, v)
        waits = []
        seen = self.seen[eng]
        for sid, (s, v) in best.items():
            if seen.get(sid, -1) >= v:
                continue
            seen[sid] = v
            waits.append((s, v))
        return waits

    def _commit(self, tok, reads, writes):
        for k in reads:
            self.readers.setdefault(k, []).append(tok)
        for k in writes:
            self.lastw[k] = tok
            self.readers[k] = []

    @staticmethod
    def _split(reads, writes):
        r, w = [], list(writes)
        for k in reads:
            if k.startswith("ps") and k not in w:
                w.append(k)
            elif not k.startswith("ps"):
                r.append(k)
        return r, w

    def record(self):
        self.rec = []
        return self.rec

    def stop(self):
        self.rec = None

    COST = {"pe": 0.16, "act": 0.7, "dve": 0.6, "pool": 1.0, "sp": 3.0}
    HOP = 0.3
    SLACK = 0.5

    def merge(self, *streams):
        self.rec = None
        pos = [0] * len(streams)
        eng_free = {}
        wtime, rtime = {}, {}

        def keys_of(item):
            kind, args, kw = item
            if kind == "op":
                eng, fn, reads, writes = args
                is_dma = False
            else:
                eng = args[0]
                reads, writes = kw["reads"], kw["writes"]
                is_dma = True
            r, w = self._split(self._x(reads), self._x(writes))
            return eng, r, w, is_dma

        def start_time(item):
            eng, r, w, is_dma = keys_of(item)
            t = 0.0
            for k in r:
                t = max(t, wtime.get(k, 0.0))
            for k in w:
                t = max(t, wtime.get(k, 0.0), rtime.get(k, 0.0))
            return max(eng_free.get(eng, 0.0), t + self.HOP)

        exposed = []
        for st in streams:
            cnt, written = {}, set()
            for it in st:
                _, r_, w_, _ = keys_of(it)
                for k in r_:
                    if k not in written:
                        cnt[k] = cnt.get(k, 0) + 1
                written.update(w_)
            exposed.append(cnt)
        wrote = [set() for _ in streams]

        while True:
            cand = []
            for i, st in enumerate(streams):
                if pos[i] < len(st):
                    cand.append((start_time(st[pos[i]]), i))
            if not cand:
                break
            tmin = min(c[0] for c in cand)
            best = None
            for t0_, i in cand:
                if t0_ <= tmin + self.SLACK:
                    best, bi = (t0_, i), i
                    break
            item = streams[bi][pos[bi]]
            eng, r, w, is_dma = keys_of(item)
            blocked = [k for k in w if not k.startswith("ps") and any(exposed[j].get(k, 0) > 0 for j in range(len(streams)) if j != bi)]
            if blocked:
                alt = [j for j in range(len(streams)) if j != bi and pos[j] < len(streams[j]) and any(exposed[j].get(k, 0) > 0 for k in blocked)]
                assert alt, ("merge hazard", blocked)
                bi = alt[0]
                item = streams[bi][pos[bi]]
                eng, r, w, is_dma = keys_of(item)
                best = (start_time(item), bi)
            pos[bi] += 1
            for k in r:
                if k not in wrote[bi] and exposed[bi].get(k, 0) > 0:
                    exposed[bi][k] -= 1
            wrote[bi].update(w)
            t0 = best[0]
            if is_dma:
                fin = t0 + 3.0
                eng_free[eng] = t0 + 0.1
            else:
                fin = t0 + self.COST.get(eng, 0.5)
                eng_free[eng] = fin
            for k in r:
                rtime[k] = max(rtime.get(k, 0.0), fin)
            for k in w:
                wtime[k] = fin
                rtime[k] = 0.0
            kind, args, kw = item
            if kind == "op":
                self.op(*args, **kw)
            else:
                self.dma(*args, **kw)

    def op(self, eng, fn, reads=(), writes=(), acc=False):
        if eng == "pool" and getattr(self, "nopool", False):
            eng = "dve"
        if getattr(self, "rec", None) is not None:
            self.rec.append(("op", (eng, fn, list(reads), list(writes)), {"acc": acc}))
            return None
        reads, writes = self._split(self._x(reads), self._x(writes))
        if acc:
            waits = self._deps(eng, reads, [k for k in writes if not k.startswith("ps")])
        else:
            waits = self._deps(eng, reads, writes)
        self.cnt[eng] += 1
        tok = (self.sem[eng], self.cnt[eng])
        self.ops[eng].append((fn, waits, tok, 1))
        self._commit(tok, reads, writes)
        return tok

    def dma(self, q, out, in_, reads=(), writes=(), is_output=False, **kw):
        if getattr(self, "rec", None) is not None:
            kw2 = dict(kw)
            kw2.update(reads=list(reads), writes=list(writes), is_output=is_output)
            self.rec.append(("dma", (q, out, in_), kw2))
            return None
        reads = self._x(reads)
        writes = self._x(writes)
        pl = self.dpool[q]
        i = pl[self.dnext[q] % len(pl)]
        self.dnext[q] += 1
        s = self.dsem[i]
        waits = self._deps(q, reads, writes)
        prev = self.dval[i]
        if prev > 0 and self.seen[q].get(id(s), -1) < prev:
            self.seen[q][id(s)] = prev
            waits.append((s, prev))
        self.dval[i] += 16
        tok = (s, self.dval[i])

        def fn(e, out=out, in_=in_, kw=kw):
            return e.dma_start(out=out, in_=in_, **kw)
        self.ops[q].append((fn, waits, tok, 16))
        self._commit(tok, reads, writes)
        if is_output:
            self.out_tokens.append(tok)
        return tok

    def emit(self, block):
        sched = self

        def run(engname):
            def body(e):
                for (fn, waits, tok, inc) in sched.ops[engname]:
                    for (s, v) in waits:
                        e.wait_ge(s, v)
                    ins = fn(e)
                    ins.then_inc(tok[0], inc)
                if engname == "sp":
                    best = {}
                    for (s, v) in sched.out_tokens:
                        if best.get(id(s), (None, -1))[1] < v:
                            best[id(s)] = (s, v)
                    for (s, v) in best.values():
                        e.wait_ge(s, v)
                    for en in ("pe", "act", "dve", "pool"):
                        if sched.cnt[en] > 0:
                            e.wait_ge(sched.sem[en], sched.cnt[en])
            return body
        block.tensor(run("pe"))
        block.scalar(run("act"))
        block.vector(run("dve"))
        block.gpsimd(run("pool"))
        block.sync(run("sp"))


def _layout(items):
    off = {}
    o = 0
    for name, w in items:
        off[name] = (o, w)
        o += w
    return off, o


CST_ITEMS = [
    ("ident", 128),
    ("mS_P", 128), ("tri_P", 128), ("blk_P", 128), ("cm_P", 2 * 128), ("padP", 64),
    ("seqm", 4), ("icnt0", 4), ("ccor0", 4),
]
CSTS_ITEMS = [("mS_S", 64), ("tri_S", 64), ("blk_S", 64), ("cm_S", 4 * 128)]
CST2_ITEMS = [
    ("colm", 4 * 64), ("dist_P", 2 * 128), ("am_P", 2 * 128), ("bc_S", 4 * 64),
    ("bc_P", 4 * 128), ("bp_P", 4 * 128),
    ("bh_S", 4 * 64), ("dist_S1", 64), ("am_S1", 64),
]
CST3_ITEMS = [("dist_SC", 4 * 64), ("am_SC", 4 * 64)] + CSTS_ITEMS
CST_OFF, NCST = _layout(CST_ITEMS)
CST2_OFF, NCST2 = _layout(CST2_ITEMS)
CST3_OFF, NCST3 = _layout(CST3_ITEMS)
CSTS_OFF, NCSTS = _layout(CSTS_ITEMS)
assert NCSTS == 704
POOLW = (2, 4, 8, 16)


def build_consts():
    c = np.zeros((128, NCST), np.float32)
    c2 = np.zeros((128, NCST2), np.float32)
    c3 = np.zeros((128, NCST3), np.float32)

    def put(name, arr):
        if name in CST_OFF:
            o, w = CST_OFF[name]
            dst = c
        elif name in CST3_OFF:
            o, w = CST3_OFF[name]
            dst = c3
        else:
            o, w = CST2_OFF[name]
            dst = c2
        a = np.asarray(arr, np.float32).reshape(arr.shape[0], -1)
        assert a.shape[1] == w, (name, a.shape, w)
        dst[:a.shape[0], o:o + w] = a
    p = np.arange(128)
    put("ident", np.eye(128))
    for tag, n, bs in (("P", 128, 64), ("S", 64, 16)):
        i = np.arange(n)
        same = (i[:, None] // bs) == (i[None, :] // bs)
        put("mS_" + tag, np.where(same & (i[None, :] < i[:, None]), 0.0, NEG))
        put("tri_" + tag, (same & (i[:, None] <= i[None, :])).astype(np.float32))
        put("blk_" + tag, same.astype(np.float32))
        nb = n // bs
        cm = np.zeros((n, nb, 128), np.float32)
        for b in range(nb):
            cm[b * bs:(b + 1) * bs, b, :] = 1.0
        put("cm_" + tag, cm)
    sm = np.zeros((64, 4), np.float32)
    for s in range(4):
        sm[s * 16:(s + 1) * 16, s] = 1.0
    put("seqm", sm)
    colm = np.zeros((128, 4, 64), np.float32)
    for s in range(4):
        colm[:, s, s * 16:(s + 1) * 16] = 1.0
    put("colm", colm)
    q = np.arange(128)
    dist = np.zeros((128, 2, 128), np.float32)
    am = np.zeros((128, 2, 128), np.float32)
    for kb in range(2):
        j = 128 * kb + p
        dist[:, kb, :] = np.abs(128 + q[None, :] - j[:, None])
        kc = j[:, None] // 64
        qc = q[None, :] // 64
        am[:, kb, :] = np.where((kc >= qc) & (kc <= qc + 2), 0.0, NEG)
    put("dist_P", dist)
    put("am_P", am)
    qs = np.arange(64)
    d = np.zeros((128, 4, 64), np.float32)
    a = np.zeros((128, 4, 64), np.float32)
    for s in range(4):
        i = qs - 16 * s
        d[:, s, :] = np.abs(128 + i[None, :] - p[:, None])
        a[:, s, :] = np.where((qs[None, :] // 16) == s, 0.0, NEG)
    put("dist_SC", d)
    put("am_SC", a)
    k64 = np.arange(64)
    put("dist_S1", np.abs(k64[:, None] % 16 - qs[None, :] % 16).astype(np.float32))
    put("am_S1", np.where((k64[:, None] // 16) == (qs[None, :] // 16), 0.0, NEG))
    t = np.arange(128)
    bc = np.zeros((128, 4, 128), np.float32)
    bp = np.zeros((128, 4, 128), np.float32)
    for gi, w in enumerate(POOLW):
        dd = t[None, :] - t[:, None]
        bc[:, gi, :] = ((dd >= 0) & (dd < w)) - w * np.eye(128)
        dd2 = t[None, :] - (t[:, None] - 128)
        bp[:, gi, :] = (dd2 < w)
    put("bc_P", bc)
    put("bp_P", bp)
    ts = np.arange(64)
    bcs = np.zeros((64, 4, 64), np.float32)
    bhs = np.zeros((60, 4, 64), np.float32)
    r = np.arange(60)
    for gi, w in enumerate(POOLW):
        same = (ts[:, None] // 16) == (ts[None, :] // 16)
        dd = ts[None, :] - ts[:, None]
        bcs[:, gi, :] = (same & (dd >= 0) & (dd < w)) - w * np.eye(64)
        sh = r[:, None] // 15
        rr = r[:, None] % 15
        i = ts[None, :] % 16
        bhs[:, gi, :] = (sh == (ts[None, :] // 16)) & ((i + 15 - rr) < w)
    put("bc_S", bcs)
    put("bh_S", bhs)
    ic = np.zeros((128, 4), np.float32)
    for gi, w in enumerate(POOLW):
        ic[:, gi] = 1.0 / np.minimum(t + 1, w)
    put("icnt0", ic)
    cc0 = np.zeros((128, 4), np.float32)
    for gi, w in enumerate(POOLW):
        cc0[:, gi] = w / np.minimum(t + 1, w) - 1.0
    put("ccor0", cc0)
    return c, c2, c3


PAR_ITEMS = [("gq", 64), ("gk", 64), ("onorm", 128), ("sinks", 8), ("alog", 4), ("dtb", 4),
             ("scale_pc", 8), ("convw", 48), ("norm_ab", 8), ("norm_c", 8)]
PAR_OFF, NPAR = _layout(PAR_ITEMS)


def build_program(NTP=16):
    nc = bass.Bass("TRN2", target_bir_lowering=False)
    SEQ = NTP * 128

    def din(name, shape):
        return nc.dram_tensor(name, list(shape), F32, kind="ExternalInput").ap()

    def dout(name, shape):
        return nc.dram_tensor(name, list(shape), F32, kind="ExternalOutput").ap()
    xp = din("xp", [SEQ, D]); xs = din("xs", [64, D])
    ck = din("ck", [4, 128, 128]); cv = din("cv", [4, 128, 128])
    sb = din("sb", [4, 4, 128, 128]); sconv = din("sconv", [12, 1536]); spool = din("spool", [60, D])
    wab = din("wab", [128, 8 * NAB]); woab = din("woab", [128, 8 * 1024])
    wic = din("wic", [128, 8 * 2048]); wgrp = din("wgrp", [128, 8 * 256]); woc = din("woc", [128, 8 * 1024])
    par_d = din("par", [128, NPAR]); cst_d = din("cst", [128, NCST]); cst2_d = din("cst2", [128, NCST2]); cst3_d = din("cst3", [128, NCST3])
    yp = dout("yp", [SEQ, D]); ys = dout("ys", [64, D])
    pak = dout("pak", [128, 128]); pav = dout("pav", [128, 128])
    pbs = dout("pbs", [4, 128, 128]); pbc = dout("pbc", [3, 1536]); pcp = dout("pcp", [15, D])
    sak = dout("sak", [64, 128]); sav = dout("sav", [64, 128])
    sbs = dout("sbs", [4, 4, 128, 128]); sbc = dout("sbc", [12, 1536]); scp = dout("scp", [60, D])

    with ExitStack() as es:
        S = Sched(nc, es)
        S.expand = {"yc": ["yc"] + ["yc%d" % i for i in range(6)], "tc": ["tc", "tcp", "tcq", "tck"], "junk": ["junkA", "junkB0", "junkB1"], "junkB": ["junkB0", "junkB1"]}

        def T(name, shape, dt=F32):
            return es.enter_context(nc.sbuf_tensor("t_" + name, list(shape), dt))
        banks = [es.enter_context(nc.psum_tensor("psb%d" % i, [128, 512], F32)) for i in range(8)]
        bank_pool = [list(range(8))]
        bank_i = {}

        def bank():
            pl = bank_pool[0]
            k = tuple(pl)
            j = bank_i.get(k, 0)
            bank_i[k] = (j + 1) % len(pl)
            i = pl[j]
            return banks[i], "ps%d" % i

        cst = T("cst", [128, NCST]); par = T("par", [128, NPAR])
        yc = T("yc", [128, 1024]); tc_ = T("tc", [128, 1024]); junk = T("junk", [128, 1024])
        Wab = T("Wab", [128, 8, NAB], BF16); Woab = T("Woab", [128, 8, 1024], BF16)
        Wic = T("Wic", [128, 8, 2048], BF16); Wgrp = T("Wgrp", [128, 4, 2, 256], BF16)
        Woc = T("Woc", [128, 8, 1024], BF16)
        identb = T("identb", [128, 128], BF16)
        onesf = T("onesf", [128, 1])
        biasP = T("biasP", [128, 2, 8, 128], BF16); biasS1 = T("biasS1", [64, 8, 64], BF16)
        biasSC = biasP[:].rearrange("p a h q -> p (a h q)").rearrange("p (s h q) -> p s h q", s=4, h=8)
        esink = T("esink", [128, 8]); nega = T("nega", [128, 4]); mhalf = T("mhalf", [128, 16]); cneg = T("cneg", [128, 2])
        bands = T("bands", [128, 4 * 128 * 2 + 4 * 64 * 2], BF16)

        def C(name, rows=128):
            if name in CSTS_OFF:
                o, w = CSTS_OFF[name]
                o += CST_OFF["mS_P"][0]
            else:
                o, w = CST_OFF[name]
            return cst[0:rows, o:o + w]

        def C2(name, rows=128):
            o, w = CST2_OFF[name]
            t_ = (yc, tc_, junk)[o // 1024]
            assert (o + w - 1) // 1024 == o // 1024
            return t_[0:rows, o % 1024:o % 1024 + w]

        def P(name):
            o, w = PAR_OFF[name]
            return par[:, o:o + w]
        ident = C("ident")

        block = es.enter_context(nc.Block())
        op = S.op

        S.dma("sp", cst[:], cst_d, writes=["cst"])
        S.dma("sp", yc[:, 0:1024], cst2_d[:, 0:1024], writes=["yc"])
        S.dma("sp", tc_[:, 0:1024], cst2_d[:, 1024:2048], writes=["tc"])
        S.dma("sp", junk[:, 0:NCST2 - 2048], cst2_d[:, 2048:NCST2], writes=["junk"])
        S.dma("sp", par[:], par_d, writes=["par"])
        op("pool", lambda e: e.memset(onesf[:], 1.0), writes=["onesf"])
        op("pool", lambda e: e.memset(mhalf[:], -0.5), writes=["mhalf"])
        op("dve", lambda e: e.tensor_copy(out=identb[:], in_=ident), reads=["cst"], writes=["identb"])
        for bi_, nm in enumerate(("bc_P", "bp_P", "bc_S", "bh_S")):
            bo = (0, 512, 1024, 1280)[bi_]
            bw = CST2_OFF[nm][1]
            op("dve", lambda e, nm=nm, bo=bo, bw=bw: e.tensor_copy(out=bands[:, bo:bo + bw], in_=C2(nm)), reads=["yc", "tc", "junk"], writes=["bands"])

        bcP = bands[:, 0:512].rearrange("p (g t) -> p g t", g=4)
        bpP = bands[:, 512:1024].rearrange("p (g t) -> p g t", g=4)
        bcS = bands[:, 1024:1280].rearrange("p (g t) -> p g t", g=4)
        bhS = bands[:, 1280:1536].rearrange("p (g t) -> p g t", g=4)
        distP = C2("dist_P").rearrange("p (k q) -> p k q", k=2); amP = C2("am_P").rearrange("p (k q) -> p k q", k=2)
        distSC = junk[:, 0:256].rearrange("p (s q) -> p s q", s=4); amSC = junk[:, 256:512].rearrange("p (s q) -> p s q", s=4)
        for h in range(8):
            sl = -(2.0 ** (-(h + 1)))
            op("dve", lambda e, h=h, sl=sl: e.scalar_tensor_tensor(out=biasP[:, :, h, :], in0=distP, scalar=sl, in1=amP, op0=ALU.mult, op1=ALU.add), reads=["yc", "tc"], writes=["biasP"])
            op("dve", lambda e, h=h, sl=sl: e.scalar_tensor_tensor(out=biasS1[:, h, :], in0=C2("dist_S1", 64), scalar=sl, in1=C2("am_S1", 64), op0=ALU.mult, op1=ALU.add), reads=["yc", "tc", "junk"], writes=["biasS1"])
        op("dve", lambda e: e.tensor_reduce(out=cneg[:, 0:1], in_=P("gq"), axis=AX.X, op=ALU.max, apply_absolute_value=True), reads=["par"], writes=["cneg"])
        op("dve", lambda e: e.tensor_reduce(out=cneg[:, 1:2], in_=P("gk"), axis=AX.X, op=ALU.max, apply_absolute_value=True), reads=["par"], writes=["cneg"])
        op("dve", lambda e: e.scalar_tensor_tensor(out=cneg[:, 0:1], in0=cneg[:, 0:1], scalar=-8.0, in1=cneg[:, 1:2], op0=ALU.mult, op1=ALU.mult), reads=["cneg"], writes=["cneg"])
        op("act", lambda e: e.activation(out=esink[:], in_=P("sinks"), func=AF.Exp, bias=cneg[:, 0:1]), reads=["par", "cneg"], writes=["esink"])
        op("act", lambda e: e.activation(out=nega[:], in_=P("alog"), func=AF.Exp), reads=["par"], writes=["nega"])
        op("dve", lambda e: e.tensor_scalar(out=nega[:], in0=nega[:], scalar1=-1.0, scalar2=None, op0=ALU.mult), reads=["nega"], writes=["nega"])

        def load_weight(src, dst3, N, wname):
            keys = []
            for c in range(8):
                wk = "%s_%d" % (wname, c)
                keys.append(wk)
                S.dma("pool", dst3[:, c, :], src[:, c * N:(c + 1) * N], writes=[wk])
            return keys
        WK = {}
        WKB = {}
        for (c0_, n_) in ((CQ, 512), (CKV, 264), (CAG, 512), (CBG, 512), (CBQ, 768), (CBQ + 768, 768)):
            ks = []
            for c in range(8):
                wk = "Wab_%d_%d" % (c0_, c)
                ks.append(wk)
                S.dma("pool", Wab[:, c, c0_:c0_ + n_], wab[:, c * NAB + c0_:c * NAB + c0_ + n_], writes=[wk])
            WKB[c0_] = ks
        WK[id(Wab)] = [k for ks in WKB.values() for k in ks]
        WK[id(Woab)] = load_weight(woab, Woab, 1024, "Woab")
        WK[id(Wic)] = load_weight(wic, Wic, 2048, "Wic")
        WK[id(Wgrp)] = load_weight(wgrp, Wgrp[:].rearrange("p a b n -> p (a b) n"), 256, "Wgrp")
        WK[id(Woc)] = load_weight(woc, Woc, 1024, "Woc")
        S.nopool = True

        xtb = [T("xt0", [128, D]), T("xt1", [128, D])]
        xTA = T("xTA", [128, 8, 128], BF16)
        st8 = T("st8", [128, 32])
        b16a = T("b16a", [128, D], BF16)
        xn = cat = pooled = mg = b16a
        xT = T("xT", [128, 8, 128], BF16); catT = xT
        zq = T("zq", [128, 512]); zkv = T("zkv", [128, 264]); zag = T("zag", [128, 512]); zbg = T("zbg", [128, 512])
        ubt = T("ub", [128, 12 * 131]); uhist = T("uhist", [128, 12, 3])
        ubP = ubt[:].rearrange("p (c l) -> p c l", c=12)
        ubS = ubt[:, 0:12 * 4 * 19].rearrange("p (c s l) -> p c s l", c=12, s=4)
        qnb = T("qnb", [128, 512], BF16); kn32 = T("kn32", [128, 128]); knb = T("knb", [128, 128], BF16)
        qTa = T("qTa", [128, 4, 128], BF16)
        kTa = [T("kTa%d" % i, [128, 128], BF16) for i in range(2)]
        vaug = [T("vaug%d" % i, [128, 2, 65], BF16) for i in range(2)]
        kTc = T("kTc", [128, 4, 128], BF16); vcaug = T("vcaug", [128, 4, 2, 65], BF16)
        Es = T("Es", [128, 512]); sc_e = oacc = Es
        Ei = T("Ei", [128, 512]); ob = Ei
        oa = T("oa", [128, 512]); cvv = uu = oa
        cq = tc_[:, 0:512]; ckk = tc_[:, 512:1024]
        gt_ = T("gt", [128, 16]); gb = T("gb", [128, 32]); gts = T("gts", [128, 16]); gtot_t = T("gtot_t", [128, 16])
        Lb = T("Lb", [128, 512], BF16); Mb = [T("Mb%d" % i, [128, 512], BF16) for i in range(2)]
        Pb = [T("Pb%d" % i, [128, 512], BF16) for i in range(2)]; Xb1 = T("Xb", [128, 512], BF16)
        knB = T("knB", [128, 512], BF16); qnB = qnb
        kbg = T("kbg", [128, 512], BF16); kdec = T("kdec", [128, 512], BF16); vbb = T("vbb", [128, 512], BF16)
        kdm = Pb[1][0:64, :]
        kTB = T("kTB", [128, 4, 128], BF16); qTB = qTa
        wT = Mb[0][:].rearrange("p (h n) -> p h n", h=4)
        wTz = Pb[0][:, 0:256].rearrange("p (h n) -> p h n", h=4); qTz = Pb[0][:, 256:512].rearrange("p (h n) -> p h n", h=4)
        qkb = knB; qkT = T("qkT", [128, 512], BF16)
        vnew = Lb
        pT01 = Mb
        Sm = T("Sm", [128, 4, 128]); Sbf = T("Sbf", [128, 4, 128], BF16)
        u32 = yc; gate = tc_
        ubf = [T("ubf%d" % i, [128, 1024], BF16) for i in range(2)]
        hnew = Es[:, 0:144].rearrange("p (c r) -> p c r", c=12)
        gA = T("gA", [128, 512], BF16); gB = T("gB", [128, 512], BF16)
        xnA = T("xnA", [128, 1024], BF16)

        def rms_to_xT(src, srckey, NT, gain, early=False):
            if early:
                c0, tagk = 28, "A"
                xb, xk = xnA, "xnA"
                dst, dstkey = xTA, "xTA"
            else:
                c0, tagk = 0, ""
                xb, xk = xn, "b16a"
                dst, dstkey = xT, "xT"
            op("act", lambda e: e.activation(out=xb[0:NT, :], in_=src[0:NT, :], func=AF.Square, accum_out=st8[0:NT, c0:c0 + 1]),
               reads=[srckey], writes=[xk, "st8a" + tagk])
            if getattr(S, "nopool", False):
                op("act", lambda e: e.activation(out=st8[0:NT, c0 + 1:c0 + 2], in_=st8[0:NT, c0:c0 + 1], func=AF.Sqrt, scale=1.0 / D, bias=EPS),
                   reads=["st8a" + tagk], writes=["st8b" + tagk])
                op("dve", lambda e: e.reciprocal(out=st8[0:NT, c0 + 2:c0 + 3], in_=st8[0:NT, c0 + 1:c0 + 2]),
                   reads=["st8b" + tagk], writes=["st8c" + tagk])
            else:
                op("pool", lambda e: e.tensor_scalar(out=st8[0:NT, c0 + 1:c0 + 2], in0=st8[0:NT, c0:c0 + 1], scalar1=1.0 / D, scalar2=EPS, op0=ALU.mult, op1=ALU.add),
                   reads=["st8a" + tagk], writes=["st8b" + tagk])
                op("pool", lambda e: e.tensor_tensor(out=st8[0:NT, c0 + 2:c0 + 3], in0=st8[0:NT, c0 + 1:c0 + 2], in1=mhalf[0:NT, 0:1], op=ALU.pow),
                   reads=["st8b" + tagk, "mhalf"], writes=["st8c" + tagk])
            op("dve", lambda e: e.tensor_scalar(out=xb[0:NT, :], in0=src[0:NT, :], scalar1=st8[0:NT, c0 + 2:c0 + 3], scalar2=None, op0=ALU.mult),
               reads=[srckey, "st8c" + tagk], writes=[xk])
            transpose8(xb, xk, dst, dstkey, NT, gain)

        def transpose8(src, srckey, dst, dstkey, NT, gain=None, chunk=None):
            pb, pk = bank()
            pbb = pb[:].bitcast(BF16)
            if chunk is None:
                chunk = lambda c: src[0:NT, c * 128:(c + 1) * 128]
            srckeys = list(srckey) if isinstance(srckey, (list, tuple)) else [srckey]
            for c in range(8):
                op("pe", lambda e, c=c: e.transpose(out=pbb[:, c * NT:(c + 1) * NT], in_=chunk(c), identity=identb[0:NT, 0:NT]),
                   reads=srckeys + ["identb"], writes=[pk], acc=(c > 0))
            if gain is None:
                op("act", lambda e: e.activation(out=dst[:, :, 0:NT], in_=pbb[:, 0:8 * NT].rearrange("p (c n) -> p c n", c=8), func=AF.Copy),
                   reads=[pk], writes=[dstkey])
            else:
                op("dve", lambda e: e.tensor_tensor(out=dst[:, :, 0:NT], in0=pbb[:, 0:8 * NT].rearrange("p (c n) -> p c n", c=8),
                                                    in1=gain.unsqueeze(2).to_broadcast([128, 8, NT]), op=ALU.mult),
                   reads=[pk, "par"], writes=[dstkey])

        def proj_tm(W, c0, ncols, NT, lhs, lhskey):
            pb, pk = bank()
            for k in range(8):
                op("pe", lambda e, k=k: e.matmul(pb[0:NT, 0:ncols], lhsT=lhs[:, k, 0:NT], rhs=W[:, k, c0:c0 + ncols], start=(k == 0), stop=(k == 7)),
                   reads=[lhskey] + (WKB[c0] if (W is Wab and c0 in WKB) else WK[id(W)]), writes=[pk], acc=(k > 0))
            return pb, pk

        def silu2(dst, z, zkey, tmp, tmpkey, dstkey, NT, n):
            op("act", lambda e: e.activation(out=tmp[0:NT, 0:n], in_=z[0:NT, 0:n], func=AF.Tanh, scale=0.5), reads=[zkey], writes=[tmpkey])
            op("dve", lambda e: e.scalar_tensor_tensor(out=dst[0:NT, 0:n], in0=tmp[0:NT, 0:n], scalar=1.0, in1=z[0:NT, 0:n], op0=ALU.add, op1=ALU.mult),
               reads=[tmpkey, zkey], writes=[dstkey])

        def rsqrt_small(dst, src, key_src, key_dst, NT, n, mul, eps):
            if getattr(S, "nopool", False):
                op("act", lambda e: e.activation(out=dst, in_=src, func=AF.Sqrt, scale=mul, bias=eps), reads=[key_src], writes=[key_dst])
                op("dve", lambda e: e.reciprocal(out=dst, in_=dst), reads=[key_dst], writes=[key_dst])
                return
            op("pool", lambda e: e.tensor_scalar(out=dst, in0=src, scalar1=mul, scalar2=eps, op0=ALU.mult, op1=ALU.add), reads=[key_src], writes=[key_dst])
            op("pool", lambda e: e.tensor_tensor(out=dst, in0=dst, in1=mhalf[0:NT, 0:n], op=ALU.pow), reads=[key_dst, "mhalf"], writes=[key_dst])

        def sample_prologue():
            ckf = yc[:, 0:512].rearrange("p (s c) -> p s c", s=4)
            cvf = tc_[:, 0:512].rearrange("p (s c) -> p s c", s=4)
            ckb = kbg[:, 0:512].rearrange("p (s c) -> p s c", s=4)
            S.dma("sp", ckf, ck.rearrange("s k c -> k s c"), writes=["yc"])
            S.dma("sp", cvf, cv.rearrange("s k c -> k s c"), writes=["tc"])
            op("dve", lambda e: e.tensor_copy(out=ckb, in_=ckf), reads=["yc"], writes=["kbg"])
            pb, pk = bank()
            pbb = pb[:].bitcast(BF16)
            for s_ in range(4):
                op("pe", lambda e, s_=s_: e.transpose(out=pbb[:, s_ * 128:(s_ + 1) * 128], in_=ckb[:, s_, :], identity=identb[:]),
                   reads=["kbg", "identb"], writes=[pk], acc=(s_ > 0))
            op("act", lambda e: e.activation(out=kTc[:].rearrange("p s k -> p (s k)"), in_=pbb[:, 0:512], func=AF.Copy), reads=[pk], writes=["kTc"])
            op("pool", lambda e: e.memset(vcaug[:], 1.0), writes=["vcaug"])
            op("dve", lambda e: e.tensor_copy(out=vcaug[:, :, :, 0:64], in_=cvf.rearrange("p s (g d) -> p s g d", g=2)), reads=["tc", "vcaug"], writes=["vcaug"])
            S.dma("sp", junk[:, 0:512], cst3_d[:, 0:512], writes=["junk"])
            oM = CST_OFF["mS_P"][0]
            S.dma("sp", cst[:, oM:oM + 704], cst3_d[:, 512:512 + 704], writes=["cst"])
            for h in range(8):
                sl = -(2.0 ** (-(h + 1)))
                op("dve", lambda e, h=h, sl=sl: e.scalar_tensor_tensor(out=biasSC[:, :, h, :], in0=distSC, scalar=sl, in1=amSC, op0=ALU.mult, op1=ALU.add), reads=["junk"], writes=["biasP"])
        for i in range(2):
            op("pool", lambda e, i=i: e.memset(vaug[i][:], 1.0), writes=["vaug%d" % i])
        op("pool", lambda e: e.memset(Sm[:], 0.0), writes=["Sm"])
        op("pool", lambda e: e.memset(Sbf[:], 0.0), writes=["Sbf"])
        op("pool", lambda e: e.memset(uhist[:], 0.0), writes=["uhist"])

        def emit_A(t):
            sample = (t == NTP)
            NT = 64 if sample else 128
            BS = 16 if sample else 64
            NB = NT // BS
            tg = "S" if sample else "P"
            par_i = t % 2
            xin = xs if sample else xp[t * 128:(t + 1) * 128, :]
            yout = ys if sample else yp[t * 128:(t + 1) * 128, :]
            last_prompt = (t == NTP - 1)

            xt = xtb[t % 2]
            XT = "xt%d" % (t % 2)
            if sample:
                sample_prologue()
            S.dma("act", xt[0:NT, :], xin, writes=[XT])
            rms_to_xT(xt, XT, NT, P("norm_ab"), early=True)

            pq, pqk = proj_tm(Wab, CQ, 512, NT, xTA, "xTA")
            op("act", lambda e: e.activation(out=zq[0:NT, :], in_=pq[0:NT, 0:512], func=AF.Copy), reads=[pqk], writes=["zq"])
            pkv, pkvk = proj_tm(Wab, CKV, 264, NT, xTA, "xTA")
            op("dve", lambda e: e.tensor_copy(out=zkv[0:NT, :], in_=pkv[0:NT, 0:264]), reads=[pkvk], writes=["zkv"])
            pag, pagk = proj_tm(Wab, CAG, 512, NT, xTA, "xTA")
            op("act", lambda e: e.activation(out=zag[0:NT, :], in_=pag[0:NT, 0:512], func=AF.Copy), reads=[pagk], writes=["zag"])
            pbg, pbgk = proj_tm(Wab, CBG, 512, NT, xTA, "xTA")
            op("dve", lambda e: e.tensor_copy(out=zbg[0:NT, :], in_=pbg[0:NT, 0:512]), reads=[pbgk], writes=["zbg"])

            ucur, ukey = (ubS if sample else ubP), "ub"
            if not sample:
                op("pool", lambda e: e.tensor_copy(out=ubP[:, :, 0:3], in_=uhist[:]), reads=["uhist"], writes=["ub"])
            for c4 in range(3):
                pb, pk = bank()
                for cc in range(4):
                    ch = c4 * 4 + cc
                    for k in range(8):
                        op("pe", lambda e, k=k, ch=ch, cc=cc, pb=pb: e.matmul(pb[:, cc * NT:(cc + 1) * NT], lhsT=Wab[:, k, CBQ + ch * 128:CBQ + (ch + 1) * 128], rhs=xTA[:, k, 0:NT], start=(k == 0), stop=(k == 7)),
                           reads=["xTA"] + WKB[CBQ if ch < 6 else CBQ + 768], writes=[pk], acc=not (k == 0 and cc == 0))
                if sample:
                    op("act", lambda e, c4=c4, pb=pb: e.activation(out=ubS[:, c4 * 4:(c4 + 1) * 4, :, 3:19], in_=pb[:, 0:256].rearrange("p (c s l) -> p c s l", c=4, s=4), func=AF.Copy),
                       reads=[pk], writes=[ukey])
                else:
                    op("act", lambda e, c4=c4, pb=pb: e.activation(out=ucur[:, c4 * 4:(c4 + 1) * 4, 3:131], in_=pb[:, 0:512].rearrange("p (c l) -> p c l", c=4), func=AF.Copy),
                       reads=[pk], writes=[ukey])
            if sample:
                S.dma("sp", tc_[0:12, 0:1024], sconv[:, 0:1024], writes=["tc"])
                S.dma("sp", junk[0:12, 0:512], sconv[:, 1024:1536], writes=["junk"])
                pb, pk = bank()
                for ch in range(12):
                    op("pe", lambda e, ch=ch, pb=pb: e.transpose(out=pb[:, ch * 12:(ch + 1) * 12], in_=(tc_[0:12, ch * 128:(ch + 1) * 128] if ch < 8 else junk[0:12, (ch - 8) * 128:(ch - 7) * 128]), identity=ident[0:12, 0:12]),
                       reads=["tc", "junk", "cst"], writes=[pk], acc=(ch > 0))
                op("act", lambda e, pb=pb: e.activation(out=ubS[:, :, :, 0:3], in_=pb[:, 0:144].rearrange("p (c s r) -> p c s r", c=12, s=4), func=AF.Copy),
                   reads=[pk], writes=[ukey])
            else:
                op("pool", lambda e: e.tensor_copy(out=uhist[:], in_=ubP[:, :, 128:131]), reads=["ub"], writes=["uhist"])

        def emit_BCD(t, recE=None):
            sample = (t == NTP)
            NT = 64 if sample else 128
            BS = 16 if sample else 64
            NB = NT // BS
            tg = "S" if sample else "P"
            par_i = t % 2
            xin = xs if sample else xp[t * 128:(t + 1) * 128, :]
            yout = ys if sample else yp[t * 128:(t + 1) * 128, :]
            last_prompt = (t == NTP - 1)

            xt = xtb[t % 2]
            XT = "xt%d" % (t % 2)
            ucur, ukey = (ubS if sample else ubP), "ub"
            if sample or last_prompt:
                n = 12 if sample else 3
                if sample:
                    op("pool", lambda e: e.tensor_copy(out=hnew[:].rearrange("p c (s r) -> p c s r", s=4), in_=ubS[:, :, :, 16:19]), reads=[ukey], writes=["Es"])
                else:
                    op("pool", lambda e: e.tensor_copy(out=hnew[:, :, 0:3], in_=ucur[:, :, 128:131]), reads=[ukey], writes=["Es"])
                for c4 in range(3):
                    pb, pk = bank()
                    for cc in range(4):
                        ch = c4 * 4 + cc
                        op("pe", lambda e, ch=ch, cc=cc, pb=pb, n=n: e.transpose(out=pb[0:n, cc * 128:(cc + 1) * 128], in_=hnew[:, ch, 0:n], identity=ident),
                           reads=["Es", "cst"], writes=[pk], acc=(cc > 0))
                    op("act", lambda e, pb=pb, c4=c4, n=n: e.activation(out=junk[0:n, 0:512], in_=pb[0:n, 0:512], func=AF.Copy), reads=[pk], writes=["junk"])
                    S.dma("sp", (sbc if sample else pbc)[:, c4 * 512:(c4 + 1) * 512], junk[0:n, 0:512], reads=["junk"], is_output=True)

            tmaj = ((cq, "tcq"), (ckk, "tck"), (cvv, "oa"))
            tb = []
            first = [True, True, True]

            CHORD = (4, 5, 6, 7, 0, 1, 2, 3, 8, 9, 10, 11)
            done_cnt = [0, 0, 0]

            def conv_half(half, part="all"):
                cw = P("convw")
                if part == "finish":
                    return conv_finish(half)

                def views(c6):
                    ch = CHORD[half * 6 + c6]
                    if sample:
                        return (lambda j, ch=ch: ubS[:, ch, :, j:j + 16]), yc[:, c6 * NT:(c6 + 1) * NT].rearrange("p (s l) -> p s l", s=4)
                    return (lambda j, ch=ch: ucur[:, ch, j:j + 128]), yc[:, c6 * NT:(c6 + 1) * NT]
                for j in range(4):
                    for c6 in range(6):
                        ch = CHORD[half * 6 + c6]
                        uv, yv = views(c6)
                        yk = "yc%d" % c6
                        if j == 0:
                            op("dve", lambda e, uv=uv, yv=yv, ch=ch: e.tensor_scalar(out=yv, in0=uv(0), scalar1=cw[:, ch * 4:ch * 4 + 1], scalar2=None, op0=ALU.mult), reads=[ukey, "par"], writes=[yk])
                        else:
                            op("dve", lambda e, uv=uv, yv=yv, ch=ch, j=j: e.scalar_tensor_tensor(out=yv, in0=uv(j), scalar=cw[:, ch * 4 + j:ch * 4 + j + 1], in1=yv, op0=ALU.mult, op1=ALU.add), reads=[ukey, "par", yk], writes=[yk])
                if part == "taps":
                    return
                conv_finish(half)

            def conv_finish(half):
                if half == 0:
                    silu2(yc, yc, "yc", tc_, "tc", "yc", 128, 6 * NT)
                else:
                    silu2(yc, yc, "yc", junk, "junk", "yc", 128, 6 * NT)
                for c6 in range(6):
                    ch = CHORD[half * 6 + c6]
                    qi, h = ch // 4, ch % 4
                    pb, pk = tb[qi]
                    op("pe", lambda e, pb=pb, h=h, c6=c6: e.transpose(out=pb[0:NT, h * 128:(h + 1) * 128], in_=yc[:, c6 * NT:(c6 + 1) * NT], identity=ident),
                       reads=["yc%d" % c6, "cst"], writes=[pk], acc=not first[qi])
                    first[qi] = False
                    done_cnt[qi] += 1
                    if done_cnt[qi] == 4:
                        dst, dk_ = tmaj[qi]
                        op("act", lambda e, pb=pb, dst=dst: e.activation(out=dst[0:NT, :], in_=pb[0:NT, 0:512], func=AF.Copy), reads=[pk], writes=[dk_])
            silu2(gA, zag, "zag", junk, "junk", "gA", NT, 512)
            silu2(gB, zbg, "zbg", junk, "junk", "gB", NT, 512)
            bank_pool[0] = [2] if recE is not None else [3, 4]
            tb.extend((banks[i], "ps%d" % i) for i in (5, 6, 7))
            recY = S.record()
            op("act", lambda e: e.activation(out=gt_[0:NT, 0:4], in_=zkv[0:NT, 256:260], func=AF.Tanh, scale=0.5), reads=["zkv"], writes=["gtb"])
            op("dve", lambda e: e.tensor_scalar(out=gt_[0:NT, 0:4], in0=gt_[0:NT, 0:4], scalar1=1.0, scalar2=0.5, op0=ALU.add, op1=ALU.mult), reads=["gtb"], writes=["gtb"])
            op("dve", lambda e: e.tensor_tensor(out=gt_[0:NT, 4:8], in0=zkv[0:NT, 260:264], in1=P("dtb")[0:NT, :], op=ALU.add), reads=["zkv", "par"], writes=["gtg"])
            op("act", lambda e: e.activation(out=gt_[0:NT, 4:8], in_=gt_[0:NT, 4:8], func=AF.Exp), reads=["gtg"], writes=["gtg"])
            op("act", lambda e: e.activation(out=gt_[0:NT, 4:8], in_=gt_[0:NT, 4:8], func=AF.Ln, bias=1.0), reads=["gtg"], writes=["gtg"])
            op("dve", lambda e: e.tensor_tensor(out=gt_[0:NT, 4:8], in0=gt_[0:NT, 4:8], in1=nega[0:NT, :], op=ALU.mult), reads=["gtg", "nega"], writes=["gtg"])
            tri = C("tri_" + tg, NT); blk = C("blk_" + tg, NT); cm = C("cm_" + tg, NT).rearrange("p (b q) -> p b q", b=NB); mS = C("mS_" + tg, NT)
            pg, pgk = bank()
            op("pe", lambda e, pg=pg: e.matmul(pg[0:NT, 0:4], lhsT=tri, rhs=gt_[0:NT, 4:8], start=True, stop=True), reads=["cst", "gtg"], writes=[pgk])
            op("pe", lambda e, pg=pg: e.matmul(pg[0:NT, 4:8], lhsT=blk, rhs=gt_[0:NT, 4:8], start=True, stop=True), reads=["cst", "gtg"], writes=[pgk], acc=True)
            for b in range(NB):
                op("pe", lambda e, pg=pg, b=b: e.matmul(pg[:, 8 + 4 * b:12 + 4 * b], lhsT=cm[:, b, :], rhs=gt_[0:NT, 4:8], start=True, stop=True), reads=["cst", "gtg"], writes=[pgk], acc=True)
            op("dve", lambda e, pg=pg: e.tensor_copy(out=gts[0:NT, 0:4], in_=pg[0:NT, 0:4]), reads=[pgk], writes=["gts"])
            op("dve", lambda e, pg=pg: e.tensor_tensor(out=gts[0:NT, 4:8], in0=pg[0:NT, 4:8], in1=gts[0:NT, 0:4], op=ALU.subtract), reads=[pgk, "gts"], writes=["gts"])
            op("act", lambda e, pg=pg: e.activation(out=gtot_t[:, 0:4 * NB], in_=pg[:, 8:8 + 4 * NB], func=AF.Exp), reads=[pgk], writes=["gtot"])
            gtot_ap = gtot_t
            op("act", lambda e: e.activation(out=gts[0:NT, 8:16], in_=gts[0:NT, 0:8], func=AF.Exp), reads=["gts"], writes=["gtse"])
            for h in range(4):
                op("dve", lambda e, h=h: e.tensor_scalar(out=Es[0:NT, h * 128:h * 128 + NT], in0=onesf[0:NT, 0:1].to_broadcast([NT, NT]), scalar1=gt_[0:NT, 4 + h:5 + h], scalar2=None, op0=ALU.mult), reads=["onesf", "gtg"], writes=["Es"])
            op("dve", lambda e: e.tensor_scalar(out=Ei[0:NT, 0:512], in0=Es[0:NT, 0:512], scalar1=-1.0, scalar2=None, op0=ALU.mult), reads=["Es"], writes=["Ei"])
            pd, pdk = bank()
            for h in range(4):
                op("pe", lambda e, pd=pd, h=h: e.matmul(pd[0:NT, h * NT:(h + 1) * NT], lhsT=tri, rhs=Es[0:NT, h * 128:h * 128 + NT], start=True, stop=False), reads=["cst", "Es"], writes=[pdk], acc=(h > 0))
                op("pe", lambda e, pd=pd, h=h: e.matmul(pd[0:NT, h * NT:(h + 1) * NT], lhsT=Ei[0:NT, h * 128:h * 128 + NT], rhs=tri, start=False, stop=True), reads=["cst", "Ei"], writes=[pdk], acc=True)
            op("dve", lambda e, pd=pd: e.tensor_tensor(out=Es[0:NT, 0:4 * NT].rearrange("p (h j) -> p h j", h=4), in0=pd[0:NT, 0:4 * NT].rearrange("p (h j) -> p h j", h=4),
                                                       in1=mS.unsqueeze(1).to_broadcast([NT, 4, NT]), op=ALU.add), reads=[pdk, "cst"], writes=["Es"])
            op("act", lambda e: e.activation(out=Es[0:NT, 0:4 * NT], in_=Es[0:NT, 0:4 * NT], func=AF.Exp), reads=["Es"], writes=["Es"])
            op("dve", lambda e: e.tensor_tensor(out=Ei[0:NT, 0:4 * NT].rearrange("p (h j) -> p h j", h=4), in0=Es[0:NT, 0:4 * NT].rearrange("p (h j) -> p h j", h=4),
                                                in1=ident[0:NT, 0:NT].unsqueeze(1).to_broadcast([NT, 4, NT]), op=ALU.add), reads=["Es", "cst"], writes=["Ei"])

            conv_half(0)
            conv_half(1, "taps")
            S.stop()
            bank_pool[0] = [0, 1] if recE is not None else [0, 1, 2]
            recX = S.record()
            op("dve", lambda e: e.tensor_tensor(out=junk[0:NT, 0:512], in0=zq[0:NT, :], in1=zq[0:NT, :], op=ALU.mult), reads=["zq"], writes=["junkA"])
            op("dve", lambda e: e.tensor_reduce(out=st8[0:NT, 4:12], in_=junk[0:NT, 0:512].rearrange("p (h d) -> p h d", h=8), axis=AX.X, op=ALU.add), reads=["junkA"], writes=["st8q"])
            op("dve", lambda e: e.tensor_tensor(out=kn32[0:NT, :], in0=zkv[0:NT, 0:128], in1=zkv[0:NT, 0:128], op=ALU.mult), reads=["zkv"], writes=["kn32"])
            op("dve", lambda e: e.tensor_reduce(out=st8[0:NT, 12:14], in_=kn32[0:NT, :].rearrange("p (h d) -> p h d", h=2), axis=AX.X, op=ALU.add), reads=["kn32"], writes=["st8q"])
            rsqrt_small(st8[0:NT, 4:14], st8[0:NT, 4:14], "st8q", "st8q", NT, 10, 1.0 / 64, EPS)
            op("dve", lambda e: e.tensor_tensor(out=junk[0:NT, 0:512].rearrange("p (h d) -> p h d", h=8), in0=zq[0:NT, :].rearrange("p (h d) -> p h d", h=8),
                                                in1=st8[0:NT, 4:12].unsqueeze(2).to_broadcast([NT, 8, 64]), op=ALU.mult), reads=["zq", "st8q"], writes=["junkA"])
            op("dve", lambda e: e.tensor_tensor(out=qnb[0:NT, :].rearrange("p (h d) -> p h d", h=8), in0=junk[0:NT, 0:512].rearrange("p (h d) -> p h d", h=8),
                                                in1=P("gq")[0:NT, :].unsqueeze(1).to_broadcast([NT, 8, 64]), op=ALU.mult), reads=["junkA", "par"], writes=["qnb"])
            op("dve", lambda e: e.tensor_tensor(out=kn32[0:NT, :].rearrange("p (h d) -> p h d", h=2), in0=zkv[0:NT, 0:128].rearrange("p (h d) -> p h d", h=2),
                                                in1=st8[0:NT, 12:14].unsqueeze(2).to_broadcast([NT, 2, 64]), op=ALU.mult), reads=["zkv", "st8q", "kn32"], writes=["kn32"])
            op("dve", lambda e: e.tensor_tensor(out=kn32[0:NT, :].rearrange("p (h d) -> p h d", h=2), in0=kn32[0:NT, :].rearrange("p (h d) -> p h d", h=2),
                                                in1=P("gk")[0:NT, :].unsqueeze(1).to_broadcast([NT, 2, 64]), op=ALU.mult), reads=["kn32", "par"], writes=["kn32"])
            op("pool", lambda e: e.tensor_copy(out=knb[0:NT, :], in_=kn32[0:NT, :]), reads=["kn32"], writes=["knb"])
            vcur, vkey = vaug[par_i], "vaug%d" % par_i
            op("pool", lambda e: e.tensor_copy(out=vcur[0:NT, :, 0:64], in_=zkv[0:NT, 128:256].rearrange("p (g d) -> p g d", g=2)), reads=["zkv"], writes=[vkey])
            if sample:
                S.dma("sp", sak, kn32[0:64, :], reads=["kn32"], is_output=True)
                S.dma("sp", sav, zkv[0:64, 128:256], reads=["zkv"], is_output=True)
            elif last_prompt:
                S.dma("sp", pak, kn32[:, :], reads=["kn32"], is_output=True)
                S.dma("sp", pav, zkv[:, 128:256], reads=["zkv"], is_output=True)
            kcur, kkey = kTa[par_i], "kTa%d" % par_i
            pb, pk = bank()
            pbb = pb[:].bitcast(BF16)
            for m in range(4):
                op("pe", lambda e, m=m, pbb=pbb: e.transpose(out=pbb[:, m * NT:(m + 1) * NT], in_=qnb[0:NT, m * 128:(m + 1) * 128], identity=identb[0:NT, 0:NT]),
                   reads=["qnb", "identb"], writes=[pk], acc=(m > 0))
            op("pe", lambda e, pbb=pbb: e.transpose(out=pbb[:, 4 * NT:5 * NT], in_=knb[0:NT, :], identity=identb[0:NT, 0:NT]), reads=["knb", "identb"], writes=[pk], acc=True)
            op("act", lambda e, pbb=pbb: e.activation(out=qTa[:, :, 0:NT], in_=pbb[:, 0:4 * NT].rearrange("p (m n) -> p m n", m=4), func=AF.Copy), reads=[pk], writes=["qTa"])
            op("act", lambda e, pbb=pbb: e.activation(out=kcur[:, 0:NT], in_=pbb[:, 4 * NT:5 * NT], func=AF.Copy), reads=[pk], writes=[kkey])
            kbs = []
            if sample:
                for s in range(4):
                    kbs.append((lambda g, s=s: kTc[g * 64:(g + 1) * 64, s, :], "kTc", lambda g, s=s: vcaug[:, s, g, :], "vcaug", 128,
                                lambda g, s=s: biasSC[:, s, g * 4:(g + 1) * 4, :], "biasP"))
                kbs.append((lambda g: kcur[g * 64:(g + 1) * 64, 0:64], kkey, lambda g: vcur[0:64, g, :], vkey, 64,
                            lambda g: biasS1[:, g * 4:(g + 1) * 4, :], "biasS1"))
            else:
                if t > 0:
                    kprev, vprev = kTa[1 - par_i], vaug[1 - par_i]
                    kbs.append((lambda g: kprev[g * 64:(g + 1) * 64, :], "kTa%d" % (1 - par_i), lambda g: vprev[:, g, :], "vaug%d" % (1 - par_i), 128,
                                lambda g: biasP[:, 0, g * 4:(g + 1) * 4, :], "biasP"))
                kbs.append((lambda g: kcur[g * 64:(g + 1) * 64, :], kkey, lambda g: vcur[:, g, :], vkey, 128,
                            lambda g: biasP[:, 1, g * 4:(g + 1) * 4, :], "biasP"))
            def pTv(bi):
                if bi < 2:
                    return pT01[bi]
                return (Pb[0][:, 0:256], Pb[0][:, 256:512], Pb[1][:, 0:256])[bi - 2]

            def pTk(bi):
                return ("Mb0", "Mb1", "Pb0", "Pb0", "Pb1")[bi]
            for g in range(2):
                for bi, (kf, kk, vf, vk, nk, bf, bk) in enumerate(kbs):
                    pb, pk = bank()
                    op("pe", lambda e, pb=pb, kf=kf, nk=nk, g=g: e.matmul(pb[0:nk, 0:4 * NT], lhsT=kf(g), rhs=qTa[g * 64:(g + 1) * 64, :, 0:NT], start=True, stop=True),
                       reads=[kk, "qTa"], writes=[pk])
                    so_ = 512 + (256 * (bi % 2) if sample else 0)
                    sk_ = ("junkB%d" % (bi % 2)) if sample else "junkB"
                    op("dve", lambda e, pb=pb, nk=nk, bf=bf, g=g, so_=so_: e.scalar_tensor_tensor(out=junk[0:nk, so_:so_ + 4 * NT].rearrange("p (m q) -> p m q", m=4), in0=pb[0:nk, 0:4 * NT].rearrange("p (m q) -> p m q", m=4),
                                                                                       scalar=0.125, in1=bf(g), op0=ALU.mult, op1=ALU.add), reads=[pk, bk], writes=[sk_])
                    op("act", lambda e, nk=nk, bi=bi, so_=so_: e.activation(out=pTv(bi)[0:nk, 0:4 * NT], in_=junk[0:nk, so_:so_ + 4 * NT], func=AF.Exp, bias=cneg[0:nk, 0:1]), reads=[sk_, "cneg"], writes=[pTk(bi)])
                po, pok = bank()
                for m in range(4):
                    for bi, (kf, kk, vf, vk, nk, bf, bk) in enumerate(kbs):
                        op("pe", lambda e, po=po, m=m, bi=bi, nk=nk, vf=vf, g=g: e.matmul(po[0:NT, m * 65:(m + 1) * 65], lhsT=pTv(bi)[0:nk, m * NT:(m + 1) * NT], rhs=vf(g), start=(bi == 0), stop=(bi == len(kbs) - 1)),
                           reads=[pTk(bi), vk], writes=[pok], acc=not (m == 0 and bi == 0))
                pov = po[0:NT, 0:260].rearrange("p (m c) -> p m c", m=4)
                op("dve", lambda e, pov=pov, g=g: e.tensor_tensor(out=st8[0:NT, 16 + g * 4:20 + g * 4], in0=pov[:, :, 64], in1=esink[0:NT, g * 4:(g + 1) * 4], op=ALU.add), reads=[pok, "esink"], writes=["st8d%d" % g])
                op("dve", lambda e, g=g: e.reciprocal(out=st8[0:NT, 16 + g * 4:20 + g * 4], in_=st8[0:NT, 16 + g * 4:20 + g * 4]), reads=["st8d%d" % g], writes=["st8d%d" % g])
                op("dve", lambda e, pov=pov, g=g: e.tensor_tensor(out=oa[0:NT, g * 256:(g + 1) * 256].rearrange("p (m d) -> p m d", m=4), in0=pov[:, :, 0:64],
                                                                  in1=st8[0:NT, 16 + g * 4:20 + g * 4].unsqueeze(2).to_broadcast([NT, 4, 64]), op=ALU.mult), reads=[pok, "st8d%d" % g], writes=["oa"])

            S.stop()
            if recE is not None:
                S.merge(recE, recY, recX)
            else:
                S.merge(recX, recY)
            if sample:
                SS = ((Sm, "Sm"), (zq[:].rearrange("p (h d) -> p h d", h=4), "zq"), (zag[:].rearrange("p (h d) -> p h d", h=4), "zag"), (zbg[:].rearrange("p (h d) -> p h d", h=4), "zbg"))
                for s_ in range(4):
                    S.dma("sp", SS[s_][0][:] if s_ == 0 else SS[s_][0], sb[s_].rearrange("h k v -> k h v"), writes=[SS[s_][1]])
            op("dve", lambda e: e.scalar_tensor_tensor(out=cat[0:NT, 0:512], in0=oa[0:NT, :], scalar=0.5, in1=gA[0:NT, 0:512], op0=ALU.mult, op1=ALU.mult), reads=["oa", "gA"], writes=["b16a"])
            bank_pool[0] = [0, 1, 2]
            recC = S.record()
            conv_half(1, "finish")
            for qi, (src, sk, nk_) in ((1, (ckk, "tck", "gtnk")), (0, (cq, "tcq", "gtnq"))):
                jc = (0, 512) if qi == 1 else (512, 1024)
                jk = "junkA" if qi == 1 else "junkB"
                op("dve", lambda e, src=src, jc=jc: e.tensor_tensor(out=junk[0:NT, jc[0]:jc[1]], in0=src[0:NT, :], in1=src[0:NT, :], op=ALU.mult), reads=[sk], writes=[jk])
                op("dve", lambda e, qi=qi, jc=jc: e.tensor_reduce(out=gt_[0:NT, 8 + qi * 4:12 + qi * 4], in_=junk[0:NT, jc[0]:jc[1]].rearrange("p (h d) -> p h d", h=4), axis=AX.X, op=ALU.add), reads=[jk], writes=[nk_])
                rsqrt_small(gt_[0:NT, 8 + qi * 4:12 + qi * 4], gt_[0:NT, 8 + qi * 4:12 + qi * 4], nk_, nk_, NT, 4, 1.0, 4 * EPS)
            op("dve", lambda e: e.tensor_tensor(out=gb[0:NT, 0:4], in0=gt_[0:NT, 12:16], in1=gt_[0:NT, 0:4], op=ALU.mult), reads=["gtnk", "gtb"], writes=["gbs"])
            op("dve", lambda e: e.tensor_tensor(out=gb[0:NT, 4:8], in0=gb[0:NT, 0:4], in1=gts[0:NT, 8:12], op=ALU.mult), reads=["gbs", "gtse"], writes=["gbs"])
            op("dve", lambda e: e.tensor_tensor(out=gb[0:NT, 8:12], in0=gt_[0:NT, 12:16], in1=gts[0:NT, 12:16], op=ALU.mult), reads=["gtnk", "gtse", "gbs"], writes=["gbs"])
            op("dve", lambda e: e.tensor_scalar(out=gb[0:NT, 12:16], in0=gt_[0:NT, 8:12], scalar1=128.0 ** -0.5, scalar2=None, op0=ALU.mult), reads=["gtnq"], writes=["gbq"])
            op("dve", lambda e: e.tensor_scalar(out=gb[0:NT, 16:20], in0=gt_[0:NT, 0:4], scalar1=0.5, scalar2=None, op0=ALU.mult), reads=["gtb", "gbs"], writes=["gbs"])

            def bc4(a):
                return a.unsqueeze(2).to_broadcast([NT, 4, 128])

            def v3(x):
                return x[0:NT, :].rearrange("p (h d) -> p h d", h=4)
            op("dve", lambda e: e.tensor_tensor(out=v3(knB), in0=v3(ckk), in1=bc4(gt_[0:NT, 12:16]), op=ALU.mult), reads=["tck", "gtnk"], writes=["knB"])
            op("pool", lambda e: e.tensor_tensor(out=v3(kbg), in0=v3(ckk), in1=bc4(gb[0:NT, 4:8]), op=ALU.mult), reads=["tck", "gbs"], writes=["kbg"])
            op("pool", lambda e: e.tensor_tensor(out=v3(kdec), in0=v3(ckk), in1=bc4(gb[0:NT, 8:12]), op=ALU.mult), reads=["tck", "gbs"], writes=["kdec"])
            op("dve", lambda e: e.tensor_tensor(out=v3(qnB), in0=v3(cq), in1=bc4(gb[0:NT, 12:16]), op=ALU.mult), reads=["tcq", "gbq"], writes=["qnb"])
            op("pool", lambda e: e.tensor_tensor(out=v3(vbb), in0=v3(cvv), in1=bc4(gb[0:NT, 16:20]), op=ALU.mult), reads=["oa", "gbs"], writes=["vbb"])
            for src, sk, dst, dk_ in ((knB, "knB", kTB, "kTB"), (qnB, "qnb", qTB, "qTa")):
                pb, pk = bank()
                pbb = pb[:].bitcast(BF16)
                for h in range(4):
                    op("pe", lambda e, pbb=pbb, h=h, src=src: e.transpose(out=pbb[:, h * NT:(h + 1) * NT], in_=src[0:NT, h * 128:(h + 1) * 128], identity=identb[0:NT, 0:NT]),
                       reads=[sk, "identb"], writes=[pk], acc=(h > 0))
                op("act", lambda e, pbb=pbb, dst=dst: e.activation(out=dst[:, :, 0:NT], in_=pbb[:, 0:4 * NT].rearrange("p (h n) -> p h n", h=4), func=AF.Copy), reads=[pk], writes=[dk_])
            pkk, pkkk = bank()
            for h in range(4):
                op("pe", lambda e, pkk=pkk, h=h: e.matmul(pkk[0:NT, h * NT:(h + 1) * NT], lhsT=kTB[:, h, 0:NT], rhs=kTB[:, h, 0:NT], start=True, stop=True), reads=["kTB"], writes=[pkkk], acc=(h > 0))
            for h in range(4):
                op("dve", lambda e, pkk=pkk, h=h: e.scalar_tensor_tensor(out=Lb[0:NT, h * NT:(h + 1) * NT], in0=pkk[0:NT, h * NT:(h + 1) * NT], scalar=gt_[0:NT, h:h + 1], in1=Es[0:NT, h * NT:(h + 1) * NT], op0=ALU.mult, op1=ALU.mult),
                   reads=[pkkk, "gtb", "Es"], writes=["Lb"])
            pqk_, pqkk = bank()
            for h in range(4):
                op("pe", lambda e, h=h: e.matmul(pqk_[0:NT, h * NT:(h + 1) * NT], lhsT=qTB[:, h, 0:NT], rhs=kTB[:, h, 0:NT], start=True, stop=True), reads=["kTB", "qTa"], writes=[pqkk], acc=(h > 0))
            op("dve", lambda e: e.tensor_tensor(out=qkb[0:NT, 0:4 * NT], in0=pqk_[0:NT, 0:4 * NT], in1=Ei[0:NT, 0:4 * NT], op=ALU.mult), reads=[pqkk, "Ei"], writes=["knB"])
            pm, pmk = bank()
            pmb = pm[:].bitcast(BF16)
            for h in range(4):
                op("pe", lambda e, h=h: e.transpose(out=pmb[0:NT, h * NT:(h + 1) * NT], in_=Lb[0:NT, h * NT:(h + 1) * NT], identity=identb[0:NT, 0:NT]), reads=["Lb", "identb"], writes=[pmk], acc=(h > 0))
            op("act", lambda e: e.activation(out=Mb[0][0:NT, 0:4 * NT], in_=pmb[0:NT, 0:4 * NT], func=AF.Copy), reads=[pmk], writes=["Mb0"])
            op("dve", lambda e: e.tensor_tensor(out=Xb1[0:NT, 0:4 * NT].rearrange("p (h j) -> p h j", h=4), in0=identb[0:NT, 0:NT].unsqueeze(1).to_broadcast([NT, 4, NT]),
                                                in1=Mb[0][0:NT, 0:4 * NT].rearrange("p (h j) -> p h j", h=4), op=ALU.subtract), reads=["identb", "Mb0"], writes=["Xb"])
            pt_, ptk = bank()
            ptb = pt_[:].bitcast(BF16)
            for h in range(4):
                op("pe", lambda e, h=h: e.transpose(out=ptb[0:NT, h * NT:(h + 1) * NT], in_=qkb[0:NT, h * NT:(h + 1) * NT], identity=identb[0:NT, 0:NT]), reads=["knB", "identb"], writes=[ptk], acc=(h > 0))
            op("act", lambda e: e.activation(out=qkT[0:NT, 0:4 * NT], in_=ptb[0:NT, 0:4 * NT], func=AF.Copy), reads=[ptk], writes=["qkT"])
            J = 5 if BS == 64 else 3
            Pc, Pk_ = Lb, "Lb"
            Mc, Mk_ = Mb[0], "Mb0"
            xi = 0
            for j in range(1, J + 1):
                Pn, Pnk = Pb[j % 2], "Pb%d" % (j % 2)
                pp, ppk = bank()
                for h in range(4):
                    op("pe", lambda e, pp=pp, h=h, Mc=Mc, Pc=Pc: e.matmul(pp[0:NT, h * NT:(h + 1) * NT], lhsT=Mc[0:NT, h * NT:(h + 1) * NT], rhs=Pc[0:NT, h * NT:(h + 1) * NT], start=True, stop=True),
                       reads=[Mk_, Pk_], writes=[ppk], acc=(h > 0))
                op("act", lambda e, pp=pp, Pn=Pn: e.activation(out=Pn[0:NT, 0:4 * NT], in_=pp[0:NT, 0:4 * NT], func=AF.Copy), reads=[ppk], writes=[Pnk])
                if j < J:
                    Mn, Mnk = Mb[j % 2], "Mb%d" % (j % 2)
                    pm2, pm2k = bank()
                    for h in range(4):
                        op("pe", lambda e, pm2=pm2, h=h, Mc=Mc, Pc=Pc: e.matmul(pm2[0:NT, h * NT:(h + 1) * NT], lhsT=Pc[0:NT, h * NT:(h + 1) * NT], rhs=Mc[0:NT, h * NT:(h + 1) * NT], start=True, stop=True),
                           reads=[Mk_, Pk_], writes=[pm2k], acc=(h > 0))
                    op("dve", lambda e, pm2=pm2, Mn=Mn: e.tensor_copy(out=Mn[0:NT, 0:4 * NT], in_=pm2[0:NT, 0:4 * NT]), reads=[pm2k], writes=[Mnk])
                Xc, Xck = Xb1, "Xb"
                Xn, Xnk = Xb1, "Xb"
                px, pxk = bank()
                for h in range(4):
                    op("pe", lambda e, px=px, h=h, Pn=Pn, Xc=Xc: e.matmul(px[0:NT, h * NT:(h + 1) * NT], lhsT=Pn[0:NT, h * NT:(h + 1) * NT], rhs=Xc[0:NT, h * NT:(h + 1) * NT], start=True, stop=True),
                       reads=[Pnk, Xck], writes=[pxk], acc=(h > 0))
                op("dve", lambda e, px=px, Xc=Xc, Xn=Xn: e.tensor_tensor(out=Xn[0:NT, 0:4 * NT], in0=px[0:NT, 0:4 * NT], in1=Xc[0:NT, 0:4 * NT], op=ALU.add), reads=[pxk, Xck], writes=[Xnk])
                xi = 1 - xi
                Pc, Pk_ = Pn, Pnk
                if j < J:
                    Mc, Mk_ = Mn, Mnk
            X, Xk = Xb1, "Xb"
            pu, puk = bank()
            for h in range(4):
                op("pe", lambda e, h=h: e.matmul(pu[0:NT, h * 128:(h + 1) * 128], lhsT=X[0:NT, h * NT:(h + 1) * NT], rhs=vbb[0:NT, h * 128:(h + 1) * 128], start=True, stop=True), reads=[Xk, "vbb"], writes=[puk], acc=(h > 0))
            op("act", lambda e: e.activation(out=uu[0:NT, :], in_=pu[0:NT, 0:512], func=AF.Copy), reads=[puk], writes=["oa"])
            pw, pwk = bank()
            for h in range(4):
                op("pe", lambda e, h=h: e.matmul(pw[:, h * NT:(h + 1) * NT], lhsT=kbg[0:NT, h * 128:(h + 1) * 128], rhs=X[0:NT, h * NT:(h + 1) * NT], start=True, stop=True), reads=[Xk, "kbg"], writes=[pwk], acc=(h > 0))
            pwv = pw[:, 0:4 * NT].rearrange("p (h n) -> p h n", h=4)
            eG = gts[0:NT, 8:12]
            if not sample:
                op("act", lambda e: e.activation(out=wT[:, :, 0:NT], in_=pwv, func=AF.Copy), reads=[pwk], writes=["Mb0"])
                for b in range(2):
                    r0, r1 = b * 64, (b + 1) * 64
                    op("dve", lambda e, b=b: e.tensor_tensor(out=Sm[:], in0=Sm[:], in1=gtot_ap[:, 4 * b:4 * b + 4].unsqueeze(2).to_broadcast([128, 4, 128]), op=ALU.mult), reads=["Sm", "gtot"], writes=["Sm"])
                    pws, pwsk = bank()
                    for h in range(4):
                        op("pe", lambda e, pws=pws, h=h: e.matmul(pws[0:128, h * 128:(h + 1) * 128], lhsT=wT[:, h, 0:128], rhs=Sbf[:, h, :], start=True, stop=True), reads=["Mb0", "Sbf"], writes=[pwsk], acc=(h > 0))
                    po1, po1k = bank()
                    for h in range(4):
                        op("pe", lambda e, po1=po1, h=h: e.matmul(po1[0:128, h * 128:(h + 1) * 128], lhsT=qTB[:, h, 0:128], rhs=Sbf[:, h, :], start=True, stop=True), reads=["qTa", "Sbf"], writes=[po1k], acc=(h > 0))
                    op("dve", lambda e, pws=pws, r0=r0, r1=r1: e.tensor_tensor(out=vnew[r0:r1, :], in0=uu[r0:r1, :], in1=pws[r0:r1, 0:512], op=ALU.subtract), reads=["oa", pwsk], writes=["Lb"])
                    op("dve", lambda e, po1=po1, r0=r0, r1=r1: e.tensor_tensor(out=oacc[r0:r1, :].rearrange("p (h d) -> p h d", h=4), in0=po1[r0:r1, 0:512].rearrange("p (h d) -> p h d", h=4),
                                                                             in1=eG[r0:r1, :].unsqueeze(2).to_broadcast([64, 4, 128]), op=ALU.mult), reads=[po1k, "gtse"], writes=["Es"])
                    pds, pdsk = bank()
                    for h in range(4):
                        op("pe", lambda e, pds=pds, h=h, r0=r0, r1=r1: e.matmul(pds[:, h * 128:(h + 1) * 128], lhsT=kdec[r0:r1, h * 128:(h + 1) * 128], rhs=vnew[r0:r1, h * 128:(h + 1) * 128], start=True, stop=True), reads=["kdec", "Lb"], writes=[pdsk], acc=(h > 0))
                    op("dve", lambda e, pds=pds: e.tensor_tensor(out=Sm[:].rearrange("p h d -> p (h d)"), in0=Sm[:].rearrange("p h d -> p (h d)"), in1=pds[:, 0:512], op=ALU.add), reads=["Sm", pdsk], writes=["Sm"])
                    op("act", lambda e: e.activation(out=Sbf[:], in_=Sm[:], func=AF.Copy), reads=["Sm"], writes=["Sbf"])
                if last_prompt:
                    S.dma("sp", pbs.rearrange("h k v -> k h v"), Sm[:], reads=["Sm"], is_output=True)
            else:
                oc_ = CST2_OFF["colm"][0]
                S.dma("sp", junk[:, 0:256], cst2_d[:, oc_:oc_ + 256], writes=["junk"])
                colmb = junk[:, 0:256].rearrange("p (a b) -> p a b", a=4)
                pws, pwsk = bank()
                po1, po1k = bank()
                for s_ in range(4):
                    Ss_, Sk_ = SS[s_]
                    Ssv = Ss_[:] if s_ == 0 else Ss_
                    op("act", lambda e, Ssv=Ssv: e.activation(out=Sbf[:], in_=Ssv, func=AF.Copy), reads=[Sk_], writes=["Sbf"])
                    op("dve", lambda e, s_=s_: e.tensor_tensor(out=wTz[:], in0=pwv, in1=colmb[:, s_, :].unsqueeze(1).to_broadcast([128, 4, 64]), op=ALU.mult), reads=[pwk, "junk"], writes=["Pb0"])
                    op("pool", lambda e, s_=s_: e.tensor_tensor(out=qTz[:], in0=qTB[:, :, 0:64], in1=colmb[:, s_, :].unsqueeze(1).to_broadcast([128, 4, 64]), op=ALU.mult), reads=["qTa", "junk"], writes=["Pb0"])
                    for h in range(4):
                        op("pe", lambda e, h=h, s_=s_: e.matmul(pws[0:64, h * 128:(h + 1) * 128], lhsT=wTz[:, h, :], rhs=Sbf[:, h, :], start=(s_ == 0 and h == 0), stop=(s_ == 3 and h == 3), skip_group_check=True), reads=["Pb0", "Sbf"], writes=[pwsk], acc=not (h == 0 and s_ == 0))
                    for h in range(4):
                        op("pe", lambda e, h=h, s_=s_: e.matmul(po1[0:64, h * 128:(h + 1) * 128], lhsT=qTz[:, h, :], rhs=Sbf[:, h, :], start=(s_ == 0 and h == 0), stop=(s_ == 3 and h == 3), skip_group_check=True), reads=["Pb0", "Sbf"], writes=[po1k], acc=not (h == 0 and s_ == 0))
                op("dve", lambda e: e.tensor_tensor(out=vnew[0:64, :], in0=uu[0:64, :], in1=pws[0:64, 0:512], op=ALU.subtract), reads=["oa", pwsk], writes=["Lb"])
                op("dve", lambda e: e.tensor_tensor(out=oacc[0:64, :].rearrange("p (h d) -> p h d", h=4), in0=po1[0:64, 0:512].rearrange("p (h d) -> p h d", h=4),
                                                    in1=eG.unsqueeze(2).to_broadcast([64, 4, 128]), op=ALU.mult), reads=[po1k, "gtse"], writes=["Es"])
                for s_ in range(4):
                    op("pool", lambda e, s_=s_: e.tensor_scalar(out=kdm[:], in0=kdec[0:64, :], scalar1=C("seqm", 64)[:, s_:s_ + 1], scalar2=None, op0=ALU.mult), reads=["kdec", "cst"], writes=["Pb1"])
                    pds, pdsk = bank()
                    for h in range(4):
                        op("pe", lambda e, pds=pds, h=h: e.matmul(pds[:, h * 128:(h + 1) * 128], lhsT=kdm[:, h * 128:(h + 1) * 128], rhs=vnew[0:64, h * 128:(h + 1) * 128], start=True, stop=True), reads=["Pb1", "Lb"], writes=[pdsk], acc=(h > 0))
                    Ss_, Sk_ = SS[s_]
                    Ssv = Ss_[:] if s_ == 0 else Ss_
                    op("dve", lambda e, s_=s_, Ssv=Ssv: e.tensor_tensor(out=Ssv, in0=Ssv, in1=gtot_ap[:, 4 * s_:4 * s_ + 4].unsqueeze(2).to_broadcast([128, 4, 128]), op=ALU.mult), reads=[Sk_, "gtot"], writes=[Sk_])
                    op("dve", lambda e, pds=pds, Ssv=Ssv: e.tensor_tensor(out=Ssv, in0=Ssv, in1=pds[:, 0:512].rearrange("p (h d) -> p h d", h=4), op=ALU.add), reads=[Sk_, pdsk], writes=[Sk_])
                    S.dma("sp", sbs[s_].rearrange("h k v -> k h v"), Ssv, reads=[Sk_], is_output=True)
            po2, po2k = bank()
            for h in range(4):
                op("pe", lambda e, h=h: e.matmul(po2[0:NT, h * 128:(h + 1) * 128], lhsT=qkT[0:NT, h * NT:(h + 1) * NT], rhs=vnew[0:NT, h * 128:(h + 1) * 128], start=True, stop=True), reads=["qkT", "Lb"], writes=[po2k], acc=(h > 0))
            op("dve", lambda e: e.tensor_tensor(out=ob[0:NT, :], in0=po2[0:NT, 0:512], in1=oacc[0:NT, :], op=ALU.add), reads=[po2k, "Es"], writes=["Ei"])
            op("dve", lambda e: e.tensor_tensor(out=junk[0:NT, 0:512], in0=ob[0:NT, :], in1=ob[0:NT, :], op=ALU.mult), reads=["Ei"], writes=["junk"])
            op("dve", lambda e: e.tensor_reduce(out=gts[0:NT, 0:4], in_=junk[0:NT, 0:512].rearrange("p (h d) -> p h d", h=4), axis=AX.X, op=ALU.add), reads=["junk", "gtse"], writes=["gts"])
            rsqrt_small(gts[0:NT, 0:4], gts[0:NT, 0:4], "gts", "gts", NT, 4, 1.0 / 128, EPS)
            op("dve", lambda e: e.tensor_tensor(out=v3(ob), in0=v3(ob), in1=bc4(gts[0:NT, 0:4]), op=ALU.mult), reads=["Ei", "gts"], writes=["Ei"])
            op("dve", lambda e: e.tensor_tensor(out=v3(ob), in0=v3(ob), in1=P("onorm")[0:NT, :].unsqueeze(1).to_broadcast([NT, 4, 128]), op=ALU.mult), reads=["Ei", "par"], writes=["Ei"])
            op("dve", lambda e: e.scalar_tensor_tensor(out=cat[0:NT, 512:1024], in0=ob[0:NT, :], scalar=0.5, in1=gB[0:NT, 0:512], op0=ALU.mult, op1=ALU.mult), reads=["Ei", "gB"], writes=["b16a"])

            transpose8(cat, "b16a", catT, "xT", NT)
            for nb_ in range(2):
                pb, pk = proj_tm(Woab, nb_ * 512, 512, NT, catT, "xT")
                op("dve", lambda e, pb=pb, nb_=nb_: e.tensor_tensor(out=xt[0:NT, nb_ * 512:(nb_ + 1) * 512], in0=pb[0:NT, 0:512], in1=xt[0:NT, nb_ * 512:(nb_ + 1) * 512], op=ALU.add), reads=[pk, XT], writes=[XT])


            S.stop()
            if t + 1 < NTP:
                bank_pool[0] = [3, 4]
                recA = S.record()
                emit_A(t + 1)
                S.stop()
                S.merge(recC, recA)
            else:
                S.merge(recC)
            bank_pool[0] = list(range(8))

        def emit_E(t):
            sample = (t == NTP)
            NT = 64 if sample else 128
            BS = 16 if sample else 64
            NB = NT // BS
            tg = "S" if sample else "P"
            par_i = t % 2
            xin = xs if sample else xp[t * 128:(t + 1) * 128, :]
            yout = ys if sample else yp[t * 128:(t + 1) * 128, :]
            last_prompt = (t == NTP - 1)

            xt = xtb[t % 2]
            XT = "xt%d" % (t % 2)
            ucur, ukey = (ubS if sample else ubP), "ub"
            need_u32 = sample
            Smv = Sm[:].rearrange("p h d -> p (h d)")
            if sample:
                S.dma("sp", yc[0:60, 0:1024], spool, writes=["yc"])
                op("dve", lambda e: e.tensor_copy(out=gA[0:60, :], in_=yc[0:60, 0:512]), reads=["yc"], writes=["gA"])
                op("dve", lambda e: e.tensor_copy(out=gB[0:60, :], in_=yc[0:60, 512:1024]), reads=["yc"], writes=["gB"])
            rms_to_xT(xt, XT, NT, P("norm_c"))
            ucb, uck = ubf[par_i], "ubf%d" % par_i
            for nb_ in range(2):
                pb, pk = proj_tm(Wic, nb_ * 512, 512, NT, xT, "xT")
                op("act", lambda e, pb=pb, nb_=nb_: e.activation(out=ucb[0:NT, nb_ * 512:(nb_ + 1) * 512], in_=pb[0:NT, 0:512], func=AF.Copy), reads=[pk], writes=[uck])
                if need_u32:
                    op("dve", lambda e, pb=pb, nb_=nb_: e.tensor_copy(out=u32[0:NT, nb_ * 512:(nb_ + 1) * 512], in_=pb[0:NT, 0:512]), reads=[pk], writes=["yc"])
                if last_prompt:
                    op("dve", lambda e, pb=pb: e.tensor_copy(out=Smv[96:128, :], in_=pb[96:128, 0:512]), reads=[pk], writes=["Sm"])
                    S.dma("sp", pcp[:, nb_ * 512:(nb_ + 1) * 512], Smv[113:128, :], reads=["Sm"], is_output=True)
            g2 = (xnA[:, 0:512], xnA[:, 512:1024])
            gtmp = ((zbg, "zbg"), (zag, "zag"))
            for nb_ in range(2):
                pb, pk = proj_tm(Wic, 1024 + nb_ * 512, 512, NT, xT, "xT")
                tmp, tk = gtmp[nb_]
                op("act", lambda e, pb=pb, tmp=tmp: e.activation(out=tmp[0:NT, 0:512], in_=pb[0:NT, 0:512], func=AF.Tanh, scale=0.5), reads=[pk], writes=[tk])
                op("dve", lambda e, pb=pb, tmp=tmp, nb_=nb_: e.scalar_tensor_tensor(out=g2[nb_][0:NT, 0:512], in0=tmp[0:NT, 0:512], scalar=1.0, in1=pb[0:NT, 0:512], op0=ALU.add, op1=ALU.mult),
                   reads=[pk, tk], writes=["xnA"])
            if sample:
                for s_ in range(4):
                    S.dma("sp", scp[s_ * 15:(s_ + 1) * 15, :], u32[s_ * 16 + 1:s_ * 16 + 16, :], reads=["yc"], is_output=True)
            for half in range(2):
                pb, pk = bank()
                for gg in range(2):
                    gi = half * 2 + gg
                    cols = slice(gi * 256, (gi + 1) * 256)
                    if sample:
                        op("pe", lambda e, pb=pb, gg=gg, gi=gi, cols=cols: e.matmul(pb[0:64, gg * 256:(gg + 1) * 256], lhsT=bcS[0:64, gi, :], rhs=ucb[0:64, cols], start=True, stop=False), reads=["bands", uck], writes=[pk], acc=(gg > 0))
                        op("pe", lambda e, pb=pb, gg=gg, gi=gi, cols=cols: e.matmul(pb[0:64, gg * 256:(gg + 1) * 256], lhsT=bhS[0:60, gi, :], rhs=(gA if gi < 2 else gB)[0:60, (gi % 2) * 256:(gi % 2 + 1) * 256], start=False, stop=True), reads=["bands", "gA", "gB"], writes=[pk], acc=True)
                    else:
                        op("pe", lambda e, pb=pb, gg=gg, gi=gi, cols=cols: e.matmul(pb[0:128, gg * 256:(gg + 1) * 256], lhsT=bcP[:, gi, :], rhs=ucb[:, cols], start=True, stop=(t == 0)), reads=["bands", uck], writes=[pk], acc=(gg > 0))
                        if t > 0:
                            op("pe", lambda e, pb=pb, gg=gg, gi=gi, cols=cols: e.matmul(pb[0:128, gg * 256:(gg + 1) * 256], lhsT=bpP[:, gi, :], rhs=ubf[1 - par_i][:, cols], start=False, stop=True), reads=["bands", "ubf%d" % (1 - par_i)], writes=[pk], acc=True)
                for gg in range(2):
                    gi = half * 2 + gg
                    cols = slice(gi * 256, (gi + 1) * 256)
                    if t == 0 and not sample:
                        op("dve", lambda e, pb=pb, gg=gg, cols=cols, gi=gi: e.tensor_scalar(out=pooled[0:NT, cols], in0=pb[0:NT, gg * 256:(gg + 1) * 256], scalar1=C("icnt0")[:, gi:gi + 1], scalar2=None, op0=ALU.mult),
                           reads=[pk, "cst"], writes=["b16a"])
                        op("dve", lambda e, cols=cols, gi=gi: e.scalar_tensor_tensor(out=pooled[0:NT, cols], in0=ucb[0:NT, cols], scalar=C("ccor0")[:, gi:gi + 1], in1=pooled[0:NT, cols], op0=ALU.mult, op1=ALU.add),
                           reads=[uck, "cst", "b16a"], writes=["b16a"])
                    else:
                        op("act", lambda e, pb=pb, gg=gg, cols=cols, gi=gi: e.activation(out=pooled[0:NT, cols], in_=pb[0:NT, gg * 256:(gg + 1) * 256], func=AF.Copy, scale=1.0 / POOLW[gi]),
                           reads=[pk], writes=["b16a"])
            transpose8(pooled, "b16a", catT, "xT", NT)
            for half in range(2):
                pb, pk = bank()
                for gg in range(2):
                    gi = half * 2 + gg
                    for kk in range(2):
                        op("pe", lambda e, pb=pb, gg=gg, gi=gi, kk=kk: e.matmul(pb[0:NT, gg * 256:(gg + 1) * 256], lhsT=catT[:, 2 * gi + kk, 0:NT], rhs=Wgrp[:, gi, kk, :], start=(kk == 0), stop=(kk == 1)),
                           reads=["xT"] + WK[id(Wgrp)], writes=[pk], acc=not (gg == 0 and kk == 0))
                cols = slice(half * 512, (half + 1) * 512)
                op("dve", lambda e, pb=pb, cols=cols, half=half: e.scalar_tensor_tensor(out=mg[0:NT, cols], in0=pb[0:NT, 0:512], scalar=0.5, in1=g2[half][0:NT, 0:512], op0=ALU.mult, op1=ALU.mult), reads=[pk, "xnA"], writes=["b16a"])
            transpose8(mg, "b16a", catT, "xT", NT, P("scale_pc"))
            for nb_ in range(2):
                pb, pk = proj_tm(Woc, nb_ * 512, 512, NT, catT, "xT")
                op("dve", lambda e, pb=pb, nb_=nb_: e.tensor_tensor(out=xt[0:NT, nb_ * 512:(nb_ + 1) * 512], in0=pb[0:NT, 0:512], in1=xt[0:NT, nb_ * 512:(nb_ + 1) * 512], op=ALU.add), reads=[pk, XT], writes=[XT])
            S.dma("sp", yout, xt[0:NT, :], reads=[XT], is_output=True)

        emit_A(0)
        emit_BCD(0)
        S.nopool = False
        for t in range(1, NTP):
            bank_pool[0] = [3, 4]
            recE = S.record()
            emit_E(t - 1)
            S.stop()
            emit_BCD(t, recE)
        bank_pool[0] = list(range(8))
        emit_A(NTP)
        bank_pool[0] = [3, 4]
        recE = S.record()
        emit_E(NTP - 1)
        S.stop()
        emit_BCD(NTP, recE)
        bank_pool[0] = list(range(8))
        emit_E(NTP)
        S.emit(block)
    return nc


def _prep_weights(inp):
    w = np.asarray(inp["w_in_ab"][0], np.float32)
    a_q = w[:, 0:512].reshape(1024, 8, 64)
    perm = [g * 4 + m for m in range(4) for g in range(2)]
    a_qp = a_q[:, perm, :].reshape(1024, 512)
    cols = np.concatenate([a_qp, w[:, 512:640], w[:, 640:768], w[:, 3328:3332], w[:, 3332:3336],
                           w[:, 768:1280], w[:, 2816:3328], w[:, 1280:2816]], axis=1)
    assert cols.shape[1] == NAB

    def pcn(a):
        n = a.shape[1]
        return np.ascontiguousarray(a.reshape(8, 128, n).transpose(1, 0, 2).reshape(128, 8 * n))
    wab = pcn(cols)
    woab = pcn(np.asarray(inp["w_out_ab"][0], np.float32))
    wic = pcn(np.asarray(inp["w_in_c"][0], np.float32))
    woc = pcn(np.asarray(inp["w_out_c"][0], np.float32))
    wg = np.asarray(inp["w_grp_c"][0], np.float32)
    wgrp = np.ascontiguousarray(wg.reshape(4, 2, 128, 256).transpose(2, 0, 1, 3).reshape(128, 8 * 256))
    par = np.zeros((128, NPAR), np.float32)

    def put(name, arr):
        o, wd = PAR_OFF[name]
        par[:, o:o + wd] = np.broadcast_to(np.asarray(arr, np.float32).reshape(-1, wd) if np.asarray(arr).ndim > 1 else np.asarray(arr, np.float32)[None, :], (128, wd))
    put("gq", inp["q_norm_a"][0]); put("gk", inp["k_norm_a"][0]); put("onorm", inp["o_norm_b"][0])
    put("sinks", inp["sinks_a"][0]); put("alog", inp["a_log_b"][0]); put("dtb", inp["dt_bias_b"][0])
    o, wd = PAR_OFF["scale_pc"]
    par[:, o:o + wd] = np.asarray(inp["scale_c"][0], np.float32).reshape(8, 128).T
    cw = np.asarray(inp["conv_b"][0], np.float32)
    o, wd = PAR_OFF["convw"]
    par[:, o:o + wd] = cw.reshape(4, 12, 128).transpose(2, 1, 0).reshape(128, 48)
    o, wd = PAR_OFF["norm_ab"]
    par[:, o:o + wd] = np.asarray(inp["norm_ab"][0], np.float32).reshape(8, 128).T
    o, wd = PAR_OFF["norm_c"]
    par[:, o:o + wd] = np.asarray(inp["norm_c"][0], np.float32).reshape(8, 128).T
    return dict(wab=wab, woab=woab, wic=wic, wgrp=wgrp, woc=woc, par=par)


_CACHE = {}


def make_in_maps(inp):
    xp = np.asarray(inp["x_prompt"], np.float32)
    shared = _prep_weights(inp)
    shared["cst"], shared["cst2"], shared["cst3"] = build_consts()
    xs = np.asarray(inp["x_sample"], np.float32)
    in_maps = []
    for c in range(8):
        m = dict(shared)
        m["xp"] = np.ascontiguousarray(xp[c])
        m["xs"] = np.ascontiguousarray(xs[4 * c:4 * c + 4].reshape(64, D))
        m["ck"] = np.ascontiguousarray(np.asarray(inp["cache_a_k"], np.float32)[0, 4 * c:4 * c + 4].reshape(4, 128, 128))
        m["cv"] = np.ascontiguousarray(np.asarray(inp["cache_a_v"], np.float32)[0, 4 * c:4 * c + 4].reshape(4, 128, 128))
        m["sb"] = np.ascontiguousarray(np.asarray(inp["state_b_s"], np.float32)[0, 4 * c:4 * c + 4])
        m["sconv"] = np.ascontiguousarray(np.asarray(inp["state_b_conv"], np.float32)[0, 4 * c:4 * c + 4].reshape(12, 1536))
        m["spool"] = np.ascontiguousarray(np.asarray(inp["state_c_pool"], np.float32)[0, 4 * c:4 * c + 4].reshape(60, D))
        in_maps.append(m)
    return in_maps


def assemble(res, SEQ):
    nb = len(res)

    def cat(name, shape_per):
        return np.stack([np.asarray(r[name]).reshape(shape_per) for r in res])
    y_p = cat("yp", (SEQ, D))
    y_s = cat("ys", (4, 16, D)).reshape(4 * nb, 16, D)
    pa_k = cat("pak", (128, 2, 64))[None]
    pa_v = cat("pav", (128, 2, 64))[None]
    pb_s = cat("pbs", (4, 128, 128))[None]
    pb_c = cat("pbc", (3, 1536))[None]
    pc = cat("pcp", (15, D))[None]
    sa_k = cat("sak", (4, 16, 2, 64)).reshape(4 * nb, 16, 2, 64)[None]
    sa_v = cat("sav", (4, 16, 2, 64)).reshape(4 * nb, 16, 2, 64)[None]
    sb_s = cat("sbs", (4, 4, 128, 128)).reshape(4 * nb, 4, 128, 128)[None]
    sb_c = cat("sbc", (4, 3, 1536)).reshape(4 * nb, 3, 1536)[None]
    sc = cat("scp", (4, 15, D)).reshape(4 * nb, 15, D)[None]
    return (y_p, y_s, pa_k, pa_v, pb_s, pb_c, pc, sa_k, sa_v, sb_s, sb_c, sc)


def kernel(**inp):
    SEQ = np.asarray(inp["x_prompt"]).shape[1]
    NTP = SEQ // 128
    if NTP not in _CACHE:
        _CACHE[NTP] = build_program(NTP)
    nc = _CACHE[NTP]
    in_maps = make_in_maps(inp)
    res = run_bass_kernel_spmd(nc, in_maps, core_ids=list(range(8))).results
    return assemble(res, SEQ)
```
